# Optimizing a Trainium2 kernel written in Bass

```python
import jax
import jax.numpy as jnp
from jax import lax
import numpy as np

D_MODEL = 1024
BATCH = 4
SEQ = 8192
DEPTH = 1
DEC_BATCH = 32
DEC_SEQ = 1
PAST_LEN = 16384
PAGE_SIZE = 128

CONV_CH = D_MODEL // 2
CONV_W = 3
N_HEADS = 8
HEAD_DIM = (D_MODEL - CONV_CH) // N_HEADS
ATTN_W = N_HEADS * HEAD_DIM
N_KV_HEADS = 2
GROUP = N_HEADS // N_KV_HEADS
N_BRANCH = 3
CMP_BLK = 32
SEL_BLK = 64
SEL_PER_CMP = SEL_BLK // CMP_BLK
N_SEL = 16
WINDOW = 512
Q_BLK = 128
ROPE_THETA = 10000.0
NORM_EPS = 1e-6
FORCE_BONUS = 1e4
KV_W = N_BRANCH * 2 * N_KV_HEADS * HEAD_DIM
IN_SPLITS = (CONV_CH, CONV_CH, CONV_CH, CONV_CH, ATTN_W, KV_W, N_HEADS * N_BRANCH, ATTN_W)
IN_W = sum(IN_SPLITS)
MIX_W = CONV_CH + ATTN_W

kernel_name = "hymba_shortconv_nsa_decode_step"


def rms_norm(x, g):
    xf = x.astype(jnp.float32)
    xf = xf * lax.rsqrt(jnp.mean(xf * xf, axis=-1, keepdims=True) + NORM_EPS)
    return (xf * g.astype(jnp.float32)).astype(x.dtype)


def rope(x, pos):
    half = HEAD_DIM // 2
    inv = ROPE_THETA ** (-jnp.arange(half, dtype=jnp.float32) / half)
    ang = pos.astype(jnp.float32)[:, None] * inv[None, :]
    shape = (1, pos.shape[0]) + (1,) * (x.ndim - 3) + (half,)
    cos = jnp.cos(ang).reshape(shape)
    sin = jnp.sin(ang).reshape(shape)
    xf = x.astype(jnp.float32)
    x1, x2 = xf[..., :half], xf[..., half:]
    return jnp.concatenate([x1 * cos - x2 * sin, x2 * cos + x1 * sin], axis=-1).astype(x.dtype)


def masked_softmax(s, mask):
    s = jnp.where(mask, s.astype(jnp.float32), -jnp.inf)
    m = jnp.max(s, axis=-1, keepdims=True)
    m = jnp.where(jnp.isfinite(m), m, 0.0)
    p = jnp.exp(s - m)
    return p / jnp.maximum(jnp.sum(p, axis=-1, keepdims=True), 1e-30)


def short_conv(u_ext, w, bias):
    t = u_ext.shape[1] - (CONV_W - 1)
    out = w[0] * u_ext[:, 0:t]
    for k in range(1, CONV_W):
        out = out + w[k] * u_ext[:, k:k + t]
    return out + bias


def compress_blocks(kv, cmp_pe, cmp_w):
    b, l = kv.shape[:2]
    nc = l // CMP_BLK
    blocks = kv[:, :nc * CMP_BLK].reshape(b, nc, CMP_BLK, 2, N_KV_HEADS, HEAD_DIM)
    blocks = blocks + cmp_pe[None, None, :, :, None, :]
    return jnp.einsum('bcrjgd,rjde->bcjge', blocks, cmp_w)


def to_sel_blocks(kv):
    b, l = kv.shape[:2]
    ns = -(-l // SEL_BLK)
    kv = jnp.pad(kv, ((0, 0), (0, ns * SEL_BLK - l), (0, 0), (0, 0), (0, 0)))
    return kv.reshape(b, ns, SEL_BLK, 2, N_KV_HEADS, HEAD_DIM).transpose(0, 4, 1, 2, 3, 5)


def nsa_block(q, q_pos, gate, kv_cmp, sel_blocks, kv_win, win_pos):
    b, tq = q.shape[:2]
    nc = kv_cmp.shape[1]
    ns = sel_blocks.shape[2]
    qp = q_pos[:, None]
    s_cmp = jnp.einsum('bqgnd,bcgd->bqgnc', q, kv_cmp[:, :, 0])
    cmp_end = (jnp.arange(nc, dtype=jnp.int32) + 1) * CMP_BLK - 1
    p_cmp = masked_softmax(s_cmp, (cmp_end[None, :] <= qp)[None, :, None, None, :])
    o_cmp = jnp.einsum('bqgnc,bcgd->bqgnd', p_cmp.astype(q.dtype), kv_cmp[:, :, 1])
    imp = jnp.sum(p_cmp, axis=3)
    imp = jnp.pad(imp, ((0, 0), (0, 0), (0, 0), (0, ns * SEL_PER_CMP - nc)))
    imp = imp.reshape(b, tq, N_KV_HEADS, ns, SEL_PER_CMP).sum(-1)
    blk = jnp.arange(ns, dtype=jnp.int32)[None, :]
    q_blk = (q_pos // SEL_BLK)[:, None]
    valid = blk * SEL_BLK <= qp
    forced = (blk == 0) | (blk == q_blk) | (blk == q_blk - 1)
    bonus = jnp.where(forced, FORCE_BONUS, 0.0).astype(jnp.float32)
    score = jnp.where(valid[None, :, None, :], imp + bonus[None, :, None, :], -jnp.inf)
    _, idx = lax.top_k(score, min(N_SEL, ns))
    bi = jnp.arange(b)[:, None, None, None]
    gi = jnp.arange(N_KV_HEADS)[None, None, :, None]
    sel = sel_blocks[bi, gi, idx]
    n_keys = idx.shape[-1] * SEL_BLK
    sel = sel.reshape(b, tq, N_KV_HEADS, n_keys, 2, HEAD_DIM)
    sel_pos = (idx[..., None] * SEL_BLK + jnp.arange(SEL_BLK, dtype=jnp.int32)).reshape(b, tq, N_KV_HEADS, n_keys)
    s_slc = jnp.einsum('bqgnd,bqgkd->bqgnk', q, sel[..., 0, :])
    m_slc = (sel_pos <= q_pos[None, :, None, None])[:, :, :, None, :]
    p_slc = masked_softmax(s_slc, m_slc)
    o_slc = jnp.einsum('bqgnk,bqgkd->bqgnd', p_slc.astype(q.dtype), sel[..., 1, :])
    s_win = jnp.einsum('bqgnd,bkgd->bqgnk', q, kv_win[:, :, 0])
    wp = win_pos[None, :]
    m_win = (wp <= qp) & (wp > qp - WINDOW) & (wp >= 0)
    p_win = masked_softmax(s_win, m_win[None, :, None, None, :])
    o_win = jnp.einsum('bqgnk,bkgd->bqgnd', p_win.astype(q.dtype), kv_win[:, :, 1])
    o = gate[..., 0:1] * o_cmp + gate[..., 1:2] * o_slc + gate[..., 2:3] * o_win
    return o.reshape(b, tq, N_HEADS, HEAD_DIM)


def mixer_inputs(x, pos, norm_g, w_in, q_gain, k_gain):
    bsz, t = x.shape[:2]
    h = rms_norm(x, norm_g)
    proj = h @ w_in
    offs = np.cumsum(IN_SPLITS)[:-1].tolist()
    b_gate, c_gate, conv_in, z_conv, q, kv, gate_logits, z_attn = jnp.split(proj, offs, axis=-1)
    q = q.reshape(bsz, t, N_HEADS, HEAD_DIM)
    q = rope(rms_norm(q, q_gain), pos) * (HEAD_DIM ** -0.5)
    q = q.reshape(bsz, t, N_KV_HEADS, GROUP, HEAD_DIM)
    kv = kv.reshape(bsz, t, N_BRANCH, 2, N_KV_HEADS, HEAD_DIM)
    k = rope(rms_norm(kv[:, :, :, 0], k_gain[:, None, :]), pos)
    kv = jnp.stack([k, kv[:, :, :, 1]], axis=3)
    gate = jax.nn.sigmoid(gate_logits.reshape(bsz, t, N_KV_HEADS, GROUP, N_BRANCH))
    u = c_gate * conv_in
    return u, b_gate, z_conv, q, kv, gate, z_attn


def mixer_output(x, conv_y, b_gate, z_conv, attn_o, z_attn, w_out):
    bsz, t = x.shape[:2]
    conv_branch = b_gate * conv_y * jax.nn.silu(z_conv)
    attn_branch = attn_o.reshape(bsz, t, ATTN_W) * jax.nn.silu(z_attn)
    return x + jnp.concatenate([conv_branch, attn_branch], axis=-1) @ w_out


def prompt_layer(x, norm_g, w_in, conv_w, conv_b, q_gain, k_gain, cmp_pe, cmp_w, w_out):
    bsz, t = x.shape[:2]
    pos = jnp.arange(t, dtype=jnp.int32)
    u, b_gate, z_conv, q, kv, gate, z_attn = mixer_inputs(x, pos, norm_g, w_in, q_gain, k_gain)
    conv_y = short_conv(jnp.pad(u, ((0, 0), (CONV_W - 1, 0), (0, 0))), conv_w, conv_b)
    kv_cmp_rows, kv_slc_rows, kv_win_rows = kv[:, :, 0], kv[:, :, 1], kv[:, :, 2]
    kv_cmp = compress_blocks(kv_cmp_rows, cmp_pe, cmp_w)
    sel_blocks = to_sel_blocks(kv_slc_rows)
    win_pad = jnp.pad(kv_win_rows, ((0, 0), (WINDOW, 0), (0, 0), (0, 0), (0, 0)))

    def q_block(i):
        s = i * Q_BLK
        q_b = lax.dynamic_slice_in_dim(q, s, Q_BLK, axis=1)
        g_b = lax.dynamic_slice_in_dim(gate, s, Q_BLK, axis=1)
        kv_w = lax.dynamic_slice_in_dim(win_pad, s, WINDOW + Q_BLK, axis=1)
        q_pos = s + jnp.arange(Q_BLK, dtype=jnp.int32)
        w_pos = s - WINDOW + jnp.arange(WINDOW + Q_BLK, dtype=jnp.int32)
        return nsa_block(q_b, q_pos, g_b, kv_cmp, sel_blocks, kv_w, w_pos)

    o = lax.map(q_block, jnp.arange(t // Q_BLK, dtype=jnp.int32))
    o = jnp.moveaxis(o, 0, 1).reshape(bsz, t, N_HEADS, HEAD_DIM)
    y = mixer_output(x, conv_y, b_gate, z_conv, o, z_attn, w_out)
    w_keep = min(WINDOW, t)
    return y, kv_cmp_rows, kv_slc_rows, kv_win_rows[:, t - w_keep:], u[:, t - (CONV_W - 1):]


def sample_layer(x, cache_cmp, cache_slc, win_buf, conv_buf, page_table,
                 norm_g, w_in, conv_w, conv_b, q_gain, k_gain, cmp_pe, cmp_w, w_out):
    bsz, t = x.shape[:2]
    past = page_table.shape[1] * PAGE_SIZE
    pos = past + jnp.arange(t, dtype=jnp.int32)
    u, b_gate, z_conv, q, kv, gate, z_attn = mixer_inputs(x, pos, norm_g, w_in, q_gain, k_gain)
    u_ext = jnp.concatenate([conv_buf.astype(u.dtype), u], axis=1)
    conv_y = short_conv(u_ext, conv_w, conv_b)

    def gather_past(pool):
        return pool[page_table].reshape(bsz, past, 2, N_KV_HEADS, HEAD_DIM)

    full_cmp = jnp.concatenate([gather_past(cache_cmp).astype(kv.dtype), kv[:, :, 0]], axis=1)
    full_slc = jnp.concatenate([gather_past(cache_slc).astype(kv.dtype), kv[:, :, 1]], axis=1)
    kv_cmp = compress_blocks(full_cmp, cmp_pe, cmp_w)
    sel_blocks = to_sel_blocks(full_slc)
    w_buf = win_buf.shape[1]
    kv_w = jnp.concatenate([win_buf.astype(kv.dtype), kv[:, :, 2]], axis=1)
    w_pos = past - w_buf + jnp.arange(w_buf + t, dtype=jnp.int32)
    o = nsa_block(q, pos, gate, kv_cmp, sel_blocks, kv_w, w_pos)
    y = mixer_output(x, conv_y, b_gate, z_conv, o, z_attn, w_out)
    return y, kv[:, :, 0], kv[:, :, 1], kv_w[:, t:], u_ext[:, t:]


def setup_inputs(seed: int = 0) -> dict:
    key = jax.random.key(seed)
    ks = jax.random.split(key, 16)
    n_pages = PAST_LEN // PAGE_SIZE
    n_used = DEC_BATCH * n_pages
    n_pool = n_used + max(1, n_used // 4)
    w_buf = min(WINDOW, PAST_LEN)
    f32 = jnp.float32
    nrm = lambda k, s: jax.random.normal(k, s, f32)
    page_table = jax.random.permutation(ks[6], n_pool)[:n_used].reshape(DEC_BATCH, n_pages).astype(jnp.int32)
    return {
        "x_prompt": nrm(ks[0], (BATCH, SEQ, D_MODEL)),
        "x_sample": nrm(ks[1], (DEC_BATCH, DEC_SEQ, D_MODEL)),
        "cache_cmp_kv": nrm(ks[2], (DEPTH, n_pool, PAGE_SIZE, 2, N_KV_HEADS, HEAD_DIM)),
        "cache_slc_kv": nrm(ks[3], (DEPTH, n_pool, PAGE_SIZE, 2, N_KV_HEADS, HEAD_DIM)),
        "state_win_kv": nrm(ks[4], (DEPTH, DEC_BATCH, w_buf, 2, N_KV_HEADS, HEAD_DIM)),
        "state_conv": nrm(ks[5], (DEPTH, DEC_BATCH, CONV_W - 1, CONV_CH)),
        "page_table": page_table,
        "norm_g": 1.0 + 0.02 * nrm(ks[7], (DEPTH, D_MODEL)),
        "w_in": nrm(ks[8], (DEPTH, D_MODEL, IN_W)) * D_MODEL ** -0.5,
        "conv_w": nrm(ks[9], (DEPTH, CONV_W, CONV_CH)) * CONV_W ** -0.5,
        "conv_b": 0.02 * nrm(ks[10], (DEPTH, CONV_CH)),
        "q_gain": 1.0 + 0.02 * nrm(ks[11], (DEPTH, HEAD_DIM)),
        "k_gain": 1.0 + 0.02 * nrm(ks[12], (DEPTH, N_BRANCH, HEAD_DIM)),
        "cmp_pe": 0.1 * nrm(ks[13], (DEPTH, CMP_BLK, 2, HEAD_DIM)),
        "cmp_w": nrm(ks[14], (DEPTH, CMP_BLK, 2, HEAD_DIM, HEAD_DIM)) * (CMP_BLK * HEAD_DIM) ** -0.5,
        "w_out": nrm(ks[15], (DEPTH, MIX_W, D_MODEL)) * MIX_W ** -0.5,
    }


def reference(x_prompt, x_sample, cache_cmp_kv, cache_slc_kv, state_win_kv, state_conv, page_table,
              norm_g, w_in, conv_w, conv_b, q_gain, k_gain, cmp_pe, cmp_w, w_out):
    yp, ys = x_prompt, x_sample
    p_cmp, p_slc, p_win, p_conv = [], [], [], []
    s_cmp, s_slc, s_win, s_conv = [], [], [], []
    for layer in range(DEPTH):
        params = (norm_g[layer], w_in[layer], conv_w[layer], conv_b[layer], q_gain[layer],
                  k_gain[layer], cmp_pe[layer], cmp_w[layer], w_out[layer])
        yp, a, b, c, d = prompt_layer(yp, *params)
        p_cmp.append(a); p_slc.append(b); p_win.append(c); p_conv.append(d)
        ys, a, b, c, d = sample_layer(ys, cache_cmp_kv[layer], cache_slc_kv[layer], state_win_kv[layer],
                                      state_conv[layer], page_table, *params)
        s_cmp.append(a); s_slc.append(b); s_win.append(c); s_conv.append(d)
    return (yp, ys, jnp.stack(p_cmp), jnp.stack(p_slc), jnp.stack(p_win), jnp.stack(p_conv),
            jnp.stack(s_cmp), jnp.stack(s_slc), jnp.stack(s_win), jnp.stack(s_conv))
```

```python
import numpy as np
from contextlib import ExitStack
import concourse.bass as bass
import concourse.mybir as mybir
from concourse.bass_utils import run_bass_kernel_spmd

F32 = mybir.dt.float32
BF16 = mybir.dt.bfloat16
I32 = mybir.dt.int32
U32 = mybir.dt.uint32
ALU = mybir.AluOpType
AF = mybir.ActivationFunctionType
AX = mybir.AxisListType

NEG = -30000.0
BIGNEG = -1.0e30
EPS = 1e-6
NTILE = 64
NSLOT = 32
IN_W = 3864
TMW = 1304
FM0 = 1304
PAST = 16384
NPOOL = 5120

C_WP, C_B0, C_VIS, C_AB, C_END = 0, 384, 512, 520, 528


class StopBuild(Exception):
    pass


class Buf:
    __slots__ = ("w", "r", "dsem", "dkey", "dcnt", "name")

    def __init__(self, name=""):
        self.w = None
        self.r = {}
        self.dsem = None
        self.dkey = None
        self.dcnt = 0
        self.name = name


class TT(Buf):
    __slots__ = ("t",)

    def __init__(self, t, name=""):
        Buf.__init__(self, name)
        self.t = t


class Fw:
    def __init__(self, nc, es):
        self.nc, self.es = nc, es
        self.eng = dict(pe=nc.tensor, act=nc.scalar, dve=nc.vector, pool=nc.gpsimd, sp=nc.sync)
        self.semh = {}
        for e in self.eng:
            self.semh[e] = es.enter_context(nc.semaphore("s_" + e))
        self.cnt = {e: 0 for e in self.eng}
        self.vc = {e: {} for e in self.eng}
        self.snaps = {}
        self.nd = 0
        self.out_events = []
        self.ninst = 0

    def sb(self, name, shape, dt):
        return TT(self.es.enter_context(self.nc.sbuf_tensor(name, shape, dt)), name)

    def ps(self, name, shape, dt):
        return TT(self.es.enter_context(self.nc.psum_tensor(name, shape, dt)), name)

    def _wait(self, e, ev):
        key, val = ev
        vc = self.vc[e]
        if vc.get(key, 0) >= val:
            return
        if key == e and e in ("pe", "sp"):
            return
        self.eng[e].wait_ge(self.semh[key], val)
        snap = self.snaps.get(ev)
        if snap:
            for k2, v2 in snap.items():
                if vc.get(k2, 0) < v2:
                    vc[k2] = v2
        if vc.get(key, 0) < val:
            vc[key] = val

    def _deps(self, e, reads, writes):
        deps = []
        for b in reads:
            if b.w is not None:
                deps.append(b.w)
        for b in writes:
            if b.w is not None:
                deps.append(b.w)
            deps.extend(b.r.items())
        deps.sort(key=lambda ev: -ev[1])
        for ev in deps:
            self._wait(e, ev)

    def _mark(self, ev, reads, writes):
        for b in reads:
            if b.r.get(ev[0], 0) < ev[1]:
                b.r[ev[0]] = ev[1]
        for b in writes:
            b.w = ev
            b.r = {}

    def op(self, e, fn, reads=(), writes=()):
        self._deps(e, reads, writes)
        inst = fn(self.eng[e])
        self.cnt[e] += 1
        n = self.cnt[e]
        inst.then_inc(self.semh[e], 1)
        ev = (e, n)
        s = dict(self.vc[e])
        s[e] = n
        self.snaps[ev] = s
        self._mark(ev, reads, writes)
        self.ninst += 1
        return ev

    def dma(self, q, out, in_, reads=(), writes=(), is_output=False, indirect=None, **kw):
        self._deps(q, reads, writes)
        owner = writes[0] if writes else reads[0]
        if owner.dsem is None:
            owner.dkey = "d%d" % self.nd
            self.nd += 1
            owner.dsem = self.es.enter_context(self.nc.semaphore(owner.dkey))
            self.semh[owner.dkey] = owner.dsem
        if indirect is not None:
            inst = self.eng[q].indirect_dma_start(out=out, out_offset=None, in_=in_, in_offset=indirect, **kw)
        else:
            inst = self.eng[q].dma_start(out=out, in_=in_, **kw)
        inst.then_inc(owner.dsem, 16)
        owner.dcnt += 16
        ev = (owner.dkey, owner.dcnt)
        self.snaps[ev] = dict(self.vc[q])
        self._mark(ev, reads, writes)
        if is_output:
            self.out_events.append(ev)
        self.ninst += 1
        return ev

    def finish(self):
        last = {}
        for k, v in self.out_events:
            last[k] = max(last.get(k, 0), v)
        for k, v in last.items():
            self._wait("sp", (k, v))
        for e in ("pe", "act", "dve", "pool"):
            if self.cnt[e]:
                self._wait("sp", (e, self.cnt[e]))

    def mm(self, out, lhsT, rhs, start, stop, reads, writes, **kw):
        return self.op("pe", lambda e: e.matmul(out, lhsT=lhsT, rhs=rhs, start=start, stop=stop, **kw),
                       reads, writes)

    def tr(self, out, in_, ident, reads, writes):
        return self.op("pe", lambda e: e.transpose(out=out, in_=in_, identity=ident), reads, writes)

    def act(self, out, in_, func, reads, writes, **kw):
        return self.op("act", lambda e: e.activation(out=out, in_=in_, func=func, **kw), reads, writes)

    def tt(self, eng, out, in0, in1, op, reads, writes):
        return self.op(eng, lambda e: e.tensor_tensor(out=out, in0=in0, in1=in1, op=op), reads, writes)

    def ts(self, eng, out, in0, s1, s2, op0, op1, reads, writes):
        if op1 is None:
            return self.op(eng, lambda e: e.tensor_scalar(out=out, in0=in0, scalar1=s1, scalar2=None, op0=op0),
                           reads, writes)
        return self.op(eng, lambda e: e.tensor_scalar(out=out, in0=in0, scalar1=s1, scalar2=s2, op0=op0, op1=op1),
                       reads, writes)

    def stt(self, out, in0, scalar, in1, op0, op1, reads, writes):
        return self.op("dve", lambda e: e.scalar_tensor_tensor(out=out, in0=in0, scalar=scalar, in1=in1,
                                                                op0=op0, op1=op1), reads, writes)

    def cp(self, eng, out, in_, reads, writes):
        if eng == "act":
            return self.act(out, in_, AF.Copy, reads, writes)
        return self.op(eng, lambda e: e.tensor_copy(out=out, in_=in_), reads, writes)

    def memset(self, eng, ap, val, writes):
        return self.op(eng, lambda e: e.memset(ap, val), (), writes)


def build_program(nslot=NSLOT, do_sample=True):
    nc = bass.Bass("TRN2", target_bir_lowering=False)
    es = ExitStack()
    fw = Fw(nc, es)

    def din(name, shape, dt=F32):
        return nc.dram_tensor(name, shape, dt, kind="ExternalInput").ap()

    def dout(name, shape, dt=F32):
        return nc.dram_tensor(name, shape, dt, kind="ExternalOutput").ap()

    xp = din("xp", [NTILE * 128, 1024])
    rope = din("rope", [NTILE * 128, 64])
    w_in = din("w_in_d", [1024, IN_W])
    w_out = din("w_out_d", [1024, 1024])
    norm_g = din("norm_g_d", [128, 8])
    gains_d = din("gains_in", [1, 896])
    convp = din("convp", [128, 16])
    cmpw = din("cmpw", [64, 32 * 2 * 64])
    cmppe = din("cmppe", [64, 64])
    consts_d = din("consts", [128, C_END])
    consts2_d = din("consts2", [128, 640])

    y_o = dout("y_o", [NSLOT * 128, 1024])
    cmp_o = dout("cmp_o", [NSLOT * 128, 256])
    slc_o = dout("slc_o", [NSLOT * 128, 256])
    win_o = dout("win_o", [256, 256])
    conv_o = dout("conv_o", [2, 512])

    xs_d = din("xs", [4, 1024])
    ropes_d = din("ropes", [4, 64])
    ptab_d = din("ptab", [128, 4], I32)
    iota_d = din("iota16", [128, 16], F32)
    ccmp_d = din("ccmp", [NPOOL * 16, 2048]) if do_sample else None
    cslc_d = din("cslc", [NPOOL * 16, 2048]) if do_sample else None
    swin_d = din("swin", [4, 512, 256])
    sconv_d = din("sconv", [4, 2, 512])
    convw_d = din("convw_row", [1, 3 * 512])
    convb_d = din("convb_row", [1, 512])
    cs_d = din("consts_s", [128, 272])
    ys_o = dout("ys_o", [4, 1024])
    scmp_o = dout("scmp_o", [4, 256])
    sslc_o = dout("sslc_o", [4, 256])
    swin_o = dout("swin_o", [4, 512, 256])
    sconv_o = dout("sconv_o", [4, 2, 512])

    winb = fw.sb("winb", [128, 8, IN_W], BF16)
    woutb = fw.sb("woutb", [128, 8, 1024], BF16)
    wbd = fw.sb("wbd", [128, 32, 2, 128], BF16)
    cst = fw.sb("cst", [128, C_END], F32)
    cbf = fw.sb("cbf", [128, 4, 128], BF16)
    identb = fw.sb("identb", [128, 128], BF16)
    zerob = fw.sb("zerob", [128, 512], BF16)
    gains = fw.sb("gains", [128, 896], F32)
    cvp = fw.sb("cvp", [128, 16], F32)
    ngt = fw.sb("ngt", [128, 8], F32)
    biasK = fw.sb("biasK", [128, 1], F32)
    biasVb = fw.sb("biasVb", [128, 128], F32)

    P = [fw.ps("psP%d" % i, [128, 512], F32) for i in range(2)]
    Sps = es.enter_context(nc.psum_tensor("psS", [128, 1024], F32))
    S = [TT(Sps, "S0"), TT(Sps, "S1")]
    OA = fw.ps("psOA", [128, 512], F32)
    OB = fw.ps("psOB", [128, 512], F32)
    Tb = [fw.ps("psT%d" % i, [128, 1024], BF16) for i in range(2)]
    rr = {"P": 0, "T": 0, "S": 0}

    def nxt(kind):
        lst = {"P": P, "T": Tb, "S": S}[kind]
        b = lst[rr[kind] % len(lst)]
        rr[kind] += 1
        return b

    def Sap(b):
        return Sps[:, 0:512] if b is S[0] else Sps[:, 512:1024]

    def barrier(extra=()):
        for e in ("dve", "pool", "act", "pe", "sp"):
            for e2 in ("dve", "pool", "act", "pe"):
                if fw.cnt[e2]:
                    fw._wait(e, (e2, fw.cnt[e2]))
            for b in extra:
                if b.w is not None:
                    fw._wait(e, b.w)
                for ev in b.r.items():
                    fw._wait(e, ev)

    fw.dma("sp", cst.t[:], consts_d[:, :], writes=[cst])
    fw.dma("sp", ngt.t[:], norm_g[:, :], writes=[ngt])
    fw.dma("sp", cvp.t[:], convp[:, :], writes=[cvp])
    fw.dma("sp", gains.t[:], gains_d[0:1, :].to_broadcast([128, 896]), writes=[gains])
    fw.ts("dve", gains.t[:, 0:512], gains.t[:, 0:512], 0.125, None, ALU.mult, None, [gains], [gains])
    fw.memset("pool", zerob.t[:], 0.0, [zerob])
    fw.memset("pool", wbd.t[:], 0.0, [wbd])

    with ExitStack() as es0:
        def sb0(name, shape, dt):
            return TT(es0.enter_context(nc.sbuf_tensor(name, shape, dt)), name)
        stg = [sb0("stg%d" % i, [128, 1932], F32) for i in range(2)]
        cwst = sb0("cwst", [128, 32 * 2 * 64], F32)
        pest = sb0("pest", [128, 64], F32)
        peb = sb0("peb", [128, 64], BF16)
        perep = [sb0("perep%d" % i, [128, 128], BF16) for i in range(2)]
        cst2 = sb0("cst2", [128, 640], F32)
        fw.dma("sp", cst2.t[:], consts2_d[:, :], writes=[cst2])
        fw.cp("dve", cbf.t[:].rearrange("p a b -> p (a b)"), cst2.t[:, 0:512], [cst2], [cbf])
        fw.cp("dve", identb.t[:], cst2.t[:, 512:640], [cst2], [identb])
        it = 0
        for kc in range(8):
            for hf in range(2):
                sg = stg[it % 2]
                it += 1
                fw.dma("sp", sg.t[:], w_in[kc * 128:(kc + 1) * 128, hf * 1932:(hf + 1) * 1932], writes=[sg])
                fw.ts("dve" if hf == 0 else "pool", winb.t[:, kc, hf * 1932:(hf + 1) * 1932], sg.t[:],
                      ngt.t[:, kc:kc + 1], None, ALU.mult, None, [sg, ngt], [winb])
        for kc in range(8):
            sg = stg[it % 2]
            it += 1
            fw.dma("sp", sg.t[:, 0:1024], w_out[kc * 128:(kc + 1) * 128, :], writes=[sg])
            fw.cp("dve" if kc % 2 == 0 else "pool", woutb.t[:, kc, :], sg.t[:, 0:1024], [sg], [woutb])
        fw.dma("sp", cwst.t[0:64, :], cmpw[:, :], writes=[cwst])
        fw.dma("sp", cwst.t[64:128, :], cmpw[:, :], writes=[cwst])
        fw.dma("sp", pest.t[0:64, :], cmppe[:, :], writes=[pest])
        fw.dma("sp", pest.t[64:128, :], cmppe[:, :], writes=[pest])
        cw4 = cwst.t[:].rearrange("p (r j e) -> p r j e", r=32, j=2)
        fw.cp("dve", wbd.t[0:64, :, :, 0:64], cw4[0:64], [cwst], [wbd])
        fw.cp("dve", wbd.t[64:128, :, :, 64:128], cw4[64:128], [cwst], [wbd])
        fw.cp("dve", peb.t[:], pest.t[:], [pest], [peb])
        pk = nxt("P")
        for r in range(32):
            fw.mm(pk.t[:, 0:2], wbd.t[:, r, 0, :], peb.t[:, 2 * r:2 * r + 2], r == 0, r == 31, [wbd, peb], [pk])
        fw.cp("act", biasK.t[:], pk.t[:, 0:1], [pk], [biasK])
        pv = nxt("P")
        for r in range(32):
            pr = perep[r % 2]
            fw.cp("dve", pr.t[:], peb.t[:, 2 * r + 1:2 * r + 2].to_broadcast([128, 128]), [peb], [pr])
            fw.mm(pv.t[:, 0:128], pr.t[:], wbd.t[:, r, 1, :], r == 0, r == 31, [pr, wbd], [pv])
        fw.cp("act", biasVb.t[:], pv.t[:, 0:128], [pv], [biasVb])
        barrier(stg + [cwst, pest, cst2])

    ident = identb.t[:]
    ident4 = identb.t[:].unsqueeze(1).to_broadcast([128, 4, 128])

    def rope_apply(eng2, src3, dst3, cosb, sinb, tcs, tsn, H, reads, wbufs, np_=128):
        def h2(ap):
            return ap.rearrange("p h (t d) -> p (h t) d", d=32)
        t3 = tcs.t[0:np_, 0:H * 64].rearrange("p (h d) -> p h d", d=64)
        s3 = tsn.t[0:np_, 0:H * 64].rearrange("p (h d) -> p h d", d=64)
        fw.tt("dve", h2(t3), h2(src3), cosb, ALU.mult, reads, [tcs])
        fw.tt(eng2, h2(s3), h2(src3), sinb, ALU.mult, reads, [tsn])
        fw.tt("dve", dst3[:, :, 0:32], t3[:, :, 0:32], s3[:, :, 32:64], ALU.subtract, [tcs, tsn], wbufs)
        fw.tt(eng2, dst3[:, :, 32:64], t3[:, :, 32:64], s3[:, :, 0:32], ALU.add, [tcs, tsn], wbufs)

    def qk_norm(eng2, tmb, r0, r1, ssh, lnh, rsh, tsn, np_, epsap):
        h0, h1 = r0 // 64, r1 // 64
        H = h1 - h0
        v = tmb.t[0:np_, r0:r1].rearrange("p (h d) -> p h d", d=64)
        sqv = tsn.t[0:np_, 0:r1 - r0]
        fw.tt(eng2, sqv, tmb.t[0:np_, r0:r1], tmb.t[0:np_, r0:r1], ALU.mult, [tmb], [tsn])
        fw.op("dve", lambda e: e.tensor_reduce(out=ssh.t[0:np_, h0:h1], in_=sqv.rearrange("p (h d) -> p h d", d=64),
                                               axis=AX.X, op=ALU.add), [tsn], [ssh])
        fw.act(lnh.t[0:np_, h0:h1], ssh.t[0:np_, h0:h1], AF.Ln, [ssh], [lnh], scale=1.0 / 64, bias=epsap)
        fw.act(rsh.t[0:np_, h0:h1], lnh.t[0:np_, h0:h1], AF.Exp, [lnh], [rsh], scale=-0.5)
        fw.tt("dve", v, v, rsh.t[0:np_, h0:h1].unsqueeze(2).to_broadcast([np_, H, 64]), ALU.mult, [tmb, rsh], [tmb])
        fw.tt(eng2, tmb.t[0:np_, r0:r1], tmb.t[0:np_, r0:r1], gains.t[0:np_, r0:r1], ALU.mult, [tmb, gains], [tmb])

    def sample_path():
        ess = ExitStack()

        def sbs(name, shape, dt):
            return TT(ess.enter_context(nc.sbuf_tensor(name, shape, dt)), name)
        cs = sbs("cs", [128, 272], F32)
        xs = sbs("xs_t", [4, 1024], F32)
        rps = sbs("rps", [4, 64], F32)
        ptab = sbs("ptab_t", [128, 4], I32)
        iot = sbs("iot", [128, 16], F32)
        idxf = sbs("idxf", [128, 4, 16], F32)
        idxa = sbs("idxa", [128, 4, 16], I32)
        hbs = sbs("hbs", [4, 1024], BF16)
        s1 = sbs("s1", [4, 1], F32)
        s2 = sbs("s2", [4, 1], F32)
        s3 = sbs("s3", [4, 1], F32)
        hTs = sbs("hTs", [128, 8, 4], BF16)
        tms = sbs("tms", [4, IN_W], F32)
        ssh = sbs("ssh_s", [4, 14], F32)
        lnh = sbs("lnh_s", [4, 14], F32)
        rsh = sbs("rsh_s", [4, 14], F32)
        tcs = sbs("tcs_s", [4, 896], F32)
        tsn = sbs("tsn_s", [4, 896], F32)
        qsf = sbs("qsf", [4, 512], F32)
        qsb = sbs("qsb", [4, 512], BF16)
        kvs = sbs("kvs", [4, 2, 384], F32)
        kvsb = sbs("kvsb", [4, 2, 384], BF16)
        gs = sbs("gs", [4, 24], F32)
        cwr = sbs("cwr", [4, 3, 512], F32)
        cbr = sbs("cbr", [4, 512], F32)
        scv = sbs("scv", [4, 2, 512], F32)
        us = sbs("us", [4, 512], F32)
        cy = sbs("cy", [4, 512], F32)
        ezs = sbs("ezs", [4, 512], F32)
        mixs = sbs("mixs", [4, 1024], F32)
        mixb = sbs("mixb", [4, 1024], BF16)
        mixTs = sbs("mixTs", [128, 8, 4], BF16)
        QTzs = sbs("QTzs", [128, 4, 2, 4], BF16)
        prod = sbs("prod", [4, 4, 128], F32)
        snew = sbs("snew", [4, 2, 8], F32)
        pnm = sbs("pnm", [4, 2, 2, 4], F32)
        pnmb = sbs("pnmb", [4, 2, 2, 4], BF16)
        stg = [sbs("sstg%d" % i, [128, 8, 256], BF16) for i in range(2)]
        kcTs = sbs("kcTs", [128, 8, 128], BF16)
        vcTs = sbs("vcTs", [128, 8, 128], BF16)
        KcmpTs = sbs("KcmpTs", [128, 512], BF16)
        Vcmps = sbs("Vcmps", [128, 4, 128], BF16)
        pns = sbs("pns", [4, 2, 512], F32)
        pnsb = sbs("pnsb", [4, 2, 512], BF16)
        mxs = sbs("mxs", [4, 2], F32)
        sms = sbs("sms", [4, 2], F32)
        impf = sbs("impf", [1, 512], F32)
        scs = sbs("scs", [1, 256], F32)
        wks = sbs("wks", [1, 256], F32)
        m8s = sbs("m8s", [1, 16], F32)
        nms = sbs("nms", [1, 256], BF16)
        nmcol = sbs("nmcol", [128, 2, 2], F32)
        pnT = sbs("pnT", [128, 2, 4, 4], BF16)
        PTs = sbs("PTs", [128, 8, 2, 4], BF16)
        accs = sbs("accs", [128, 8], F32)
        acct = sbs("acct", [128, 8], F32)
        swt = sbs("swt", [128, 4, 256], F32)
        swb = sbs("swb", [128, 4, 256], BF16)
        stw = sbs("stw", [128, 4, 8], F32)
        Oall = sbs("Oall", [4, 3, 2, 65], F32)
        Ocat = sbs("Ocat", [4, 4, 3, 2, 65], F32)
        fcs = sbs("fcs", [4, 3, 4], F32)
        o1 = sbs("o1", [4, 4, 64], F32)
        o2 = sbs("o2", [4, 4, 64], F32)
        osf = sbs("osf", [4, 512], F32)
        ysb = xs
        dd = Buf("dramcopy")
        epsap = cst.t[0:4, C_AB + 2:C_AB + 3]
        onesf = cs.t[:, 268:269]

        fw.dma("sp", cs.t[:], cs_d[:, :], writes=[cs])
        fw.dma("sp", xs.t[:], xs_d[:, :], writes=[xs])
        fw.dma("sp", rps.t[:], ropes_d[:, :], writes=[rps])
        fw.dma("sp", ptab.t[:], ptab_d[:, :], writes=[ptab])
        fw.dma("sp", iot.t[:], iota_d[:, :], writes=[iot])
        fw.dma("sp", cwr.t[:].rearrange("p a b -> p (a b)"), convw_d[0:1, :].to_broadcast([4, 1536]), writes=[cwr])
        fw.dma("sp", cbr.t[:], convb_d[0:1, :].to_broadcast([4, 512]), writes=[cbr])
        fw.dma("sp", scv.t[:], sconv_d[:, :, :], writes=[scv])
        fw.dma("sp", swin_o[:, 0:511, :], swin_d[:, 1:512, :], reads=[dd], is_output=True)
        for s in range(4):
            fw.ts("dve", idxf.t[:, s, :], ptab.t[:, s:s + 1].to_broadcast([128, 16]), 16.0, None, ALU.mult, None, [ptab], [idxf])
        fw.tt("dve", idxf.t[:], idxf.t[:], iot.t[:].unsqueeze(1).to_broadcast([128, 4, 16]), ALU.add, [idxf, iot], [idxf])
        fw.cp("dve", idxa.t[:], idxf.t[:], [idxf], [idxa])
        fw.act(hbs.t[:], xs.t[:], AF.Square, [xs], [hbs, s1], accum_out=s1.t[:])
        fw.act(s2.t[:], s1.t[:], AF.Ln, [s1], [s2], scale=1.0 / 1024, bias=epsap)
        fw.act(s3.t[:], s2.t[:], AF.Exp, [s2], [s3], scale=-0.5)
        fw.ts("dve", hbs.t[:], xs.t[:], s3.t[:, 0:1], None, ALU.mult, None, [xs, s3], [hbs])
        tb = nxt("T")
        for kc in range(8):
            fw.tr(tb.t[:, kc * 4:kc * 4 + 4], hbs.t[:, kc * 128:(kc + 1) * 128], identb.t[0:4, 0:4], [hbs, identb], [tb])
        fw.cp("act", hTs.t[:].rearrange("p a b -> p (a b)"), tb.t[:, 0:32], [tb], [hTs])
        c0 = 0
        while c0 < IN_W:
            c1 = min(c0 + 512, IN_W)
            pb = nxt("P")
            for kc in range(8):
                fw.mm(pb.t[0:4, 0:c1 - c0], hTs.t[:, kc, :], winb.t[:, kc, c0:c1], kc == 0, kc == 7, [hTs, winb], [pb])
            fw.cp("act", tms.t[:, c0:c1], pb.t[0:4, 0:c1 - c0], [pb], [tms])
            c0 = c1
        qk_norm("dve", tms, 0, 896, ssh, lnh, rsh, tsn, 4, epsap)
        cosb = rps.t[:, 0:32].unsqueeze(1).to_broadcast([4, 28, 32])
        sinb = rps.t[:, 32:64].unsqueeze(1).to_broadcast([4, 28, 32])
        src = tms.t[:, 0:896].rearrange("p (h d) -> p h d", d=64)
        t3 = tcs.t[:, :].rearrange("p (h d) -> p h d", d=64)
        s3v = tsn.t[:, :].rearrange("p (h d) -> p h d", d=64)

        def h2(ap):
            return ap.rearrange("p h (t d) -> p (h t) d", d=32)
        fw.tt("dve", h2(t3), h2(src), cosb, ALU.mult, [tms, rps], [tcs])
        fw.tt("dve", h2(s3v), h2(src), sinb, ALU.mult, [tms, rps], [tsn])
        qv = qsf.t[:].rearrange("p (h d) -> p h d", d=64)
        kv_ = kvs.t[:, 0, :].rearrange("p (h d) -> p h d", d=64)
        fw.tt("dve", qv[:, :, 0:32], t3[:, 0:8, 0:32], s3v[:, 0:8, 32:64], ALU.subtract, [tcs, tsn], [qsf])
        fw.tt("dve", qv[:, :, 32:64], t3[:, 0:8, 32:64], s3v[:, 0:8, 0:32], ALU.add, [tcs, tsn], [qsf])
        fw.tt("dve", kv_[:, :, 0:32], t3[:, 8:14, 0:32], s3v[:, 8:14, 32:64], ALU.subtract, [tcs, tsn], [kvs])
        fw.tt("dve", kv_[:, :, 32:64], t3[:, 8:14, 32:64], s3v[:, 8:14, 0:32], ALU.add, [tcs, tsn], [kvs])
        fw.cp("dve", kvs.t[:, 1, :], tms.t[:, 896:1280], [tms], [kvs])
        fw.cp("dve", qsb.t[:], qsf.t[:], [qsf], [qsb])
        fw.cp("dve", kvsb.t[:], kvs.t[:], [kvs], [kvsb])
        kvo = kvs.t[:].rearrange("p j (b c) -> p j b c", c=128)
        fw.dma("sp", scmp_o.rearrange("s (j c) -> s j c", j=2), kvo[:, :, 0, :], reads=[kvs], is_output=True)
        fw.dma("sp", sslc_o.rearrange("s (j c) -> s j c", j=2), kvo[:, :, 1, :], reads=[kvs], is_output=True)
        fw.dma("sp", swin_o[:, 511, :].rearrange("s (j c) -> s j c", j=2), kvo[:, :, 2, :], reads=[kvs], is_output=True)
        fw.act(gs.t[:], tms.t[:, 1280:1304], AF.Exp, [tms], [gs], scale=-1.0)
        fw.ts("dve", gs.t[:], gs.t[:], 1.0, None, ALU.add, None, [gs], [gs])
        fw.op("dve", lambda e: e.reciprocal(out=gs.t[:], in_=gs.t[:]), [gs], [gs])
        f0 = FM0
        fw.tt("dve", us.t[:], tms.t[:, f0 + 512:f0 + 1024], tms.t[:, f0 + 1024:f0 + 1536], ALU.mult, [tms], [us])
        fw.dma("sp", sconv_o[:, 0, :], scv.t[:, 1, :], reads=[scv], is_output=True)
        fw.dma("sp", sconv_o[:, 1, :], us.t[:], reads=[us], is_output=True)
        fw.tt("dve", cy.t[:], scv.t[:, 0, :], cwr.t[:, 0, :], ALU.mult, [scv, cwr], [cy])
        fw.tt("dve", ezs.t[:], scv.t[:, 1, :], cwr.t[:, 1, :], ALU.mult, [scv, cwr], [ezs])
        fw.tt("dve", cy.t[:], cy.t[:], ezs.t[:], ALU.add, [cy, ezs], [cy])
        fw.tt("dve", ezs.t[:], us.t[:], cwr.t[:, 2, :], ALU.mult, [us, cwr], [ezs])
        fw.tt("dve", cy.t[:], cy.t[:], ezs.t[:], ALU.add, [cy, ezs], [cy])
        fw.tt("dve", cy.t[:], cy.t[:], cbr.t[:], ALU.add, [cy, cbr], [cy])
        fw.act(ezs.t[:], tms.t[:, f0 + 1536:f0 + 2048], AF.Exp, [tms], [ezs], scale=-1.0)
        fw.ts("dve", ezs.t[:], ezs.t[:], 1.0, None, ALU.add, None, [ezs], [ezs])
        fw.op("dve", lambda e: e.reciprocal(out=ezs.t[:], in_=ezs.t[:]), [ezs], [ezs])
        fw.tt("dve", ezs.t[:], ezs.t[:], tms.t[:, f0 + 1536:f0 + 2048], ALU.mult, [ezs, tms], [ezs])
        fw.tt("dve", ezs.t[:], ezs.t[:], tms.t[:, f0:f0 + 512], ALU.mult, [ezs, tms], [ezs])
        fw.tt("dve", mixs.t[:, 0:512], cy.t[:], ezs.t[:], ALU.mult, [cy, ezs], [mixs])
        fw.memset("pool", QTzs.t[:], 0.0, [QTzs])
        tb = nxt("T")
        for n in range(4):
            fw.tr(tb.t[:, n * 4:n * 4 + 4], qsb.t[:, n * 128:(n + 1) * 128], identb.t[0:4, 0:4], [qsb, identb], [tb])
        fw.cp("act", QTzs.t[0:64, :, 0, :], tb.t[0:64, 0:16].rearrange("p (n s) -> p s n", s=4), [tb], [QTzs])
        fw.cp("act", QTzs.t[64:128, :, 1, :], tb.t[64:128, 0:16].rearrange("p (n s) -> p s n", s=4), [tb], [QTzs])
        for bi, br in enumerate((1, 2)):
            fw.tt("dve", prod.t[:], qsf.t[:].rearrange("p (n c) -> p n c", c=128),
                  kvs.t[:, 0, br * 128:(br + 1) * 128].unsqueeze(1).to_broadcast([4, 4, 128]), ALU.mult, [qsf, kvs], [prod])
            fw.op("dve", lambda e: e.tensor_reduce(out=snew.t[:, bi, :], in_=prod.t[:].rearrange("p n (g d) -> p (n g) d", d=64),
                                                   axis=AX.X, op=ALU.add), [prod], [snew])
        fw.act(snew.t[:], snew.t[:], AF.Exp, [snew], [snew])

        for s in range(4):
            qrhs = QTzs.t[:, s, :, :].rearrange("p g n -> p (g n)")
            fw.ts("dve", pnm.t[:].rearrange("p b g n -> p b n g"), snew.t[:].rearrange("p b (n g) -> p b n g", g=2),
                  cs.t[0:4, 256 + s:257 + s], None, ALU.mult, None, [snew, cs], [pnm])
            fw.cp("dve", pnmb.t[:], pnm.t[:], [pnm], [pnmb])
            for ck in range(16):
                sg = stg[ck % 2]
                fw.dma("pool", sg.t[:].rearrange("p a b -> p (a b)"), ccmp_d[:, :], reads=[idxa], writes=[sg],
                       indirect=bass.IndirectOffsetOnAxis(ap=idxa.t[:, s, ck:ck + 1], axis=0))
                tk, tv = nxt("T"), nxt("T")
                for r8 in range(8):
                    fw.tr(tk.t[:, r8 * 128:(r8 + 1) * 128], sg.t[:, r8, 0:128], ident, [sg, identb], [tk])
                    fw.tr(tv.t[:, r8 * 128:(r8 + 1) * 128], sg.t[:, r8, 128:256], ident, [sg, identb], [tv])
                fw.cp("act", kcTs.t[:].rearrange("p a b -> p (a b)"), tk.t[:, :], [tk], [kcTs])
                fw.cp("act", vcTs.t[:].rearrange("p a b -> p (a b)"), tv.t[:, :], [tv], [vcTs])
                for r8 in range(8):
                    r = ck * 8 + r8
                    q4, r32 = r // 32, r % 32
                    fw.mm(OA.t[:, q4 * 128:(q4 + 1) * 128], wbd.t[:, r32, 0, :], kcTs.t[:, r8, :], r32 == 0, r32 == 31,
                          [wbd, kcTs], [OA])
                    fw.mm(OB.t[:, q4 * 128:(q4 + 1) * 128], vcTs.t[:, r8, :], wbd.t[:, r32, 1, :], r32 == 0, r32 == 31,
                          [wbd, vcTs], [OB])
            fw.act(KcmpTs.t[:], OA.t[:, :], AF.Identity, [OA, biasK], [KcmpTs], bias=biasK.t[:, 0:1])
            fw.tt("dve", Vcmps.t[:], OB.t[:, :].rearrange("p (q c) -> p q c", c=128),
                  biasVb.t[:].unsqueeze(1).to_broadcast([128, 4, 128]), ALU.add, [OB, biasVb], [Vcmps])
            for g in range(2):
                pb = nxt("P")
                fw.mm(pb.t[0:4, :], QTzs.t[:, s, g, :], KcmpTs.t[:, :], True, True, [QTzs, KcmpTs], [pb])
                fw.op("dve", lambda e: e.tensor_reduce(out=mxs.t[:, g:g + 1], in_=pb.t[0:4, :], axis=AX.X, op=ALU.max), [pb], [mxs])
                fw.ts("dve", mxs.t[:, g:g + 1], mxs.t[:, g:g + 1], -1.0, None, ALU.mult, None, [mxs], [mxs])
                fw.act(pns.t[:, g, :], pb.t[0:4, :], AF.Exp, [pb, mxs], [pns, sms], bias=mxs.t[:, g:g + 1],
                       accum_out=sms.t[:, g:g + 1])
                fw.op("dve", lambda e: e.reciprocal(out=sms.t[:, g:g + 1], in_=sms.t[:, g:g + 1]), [sms], [sms])
                fw.ts("dve", pns.t[:, g, :], pns.t[:, g, :], sms.t[:, g:g + 1], None, ALU.mult, None, [pns, sms], [pns])
                fw.cp("dve", pnsb.t[:, g, :], pns.t[:, g, :], [pns], [pnsb])
                pi = nxt("P")
                fw.mm(pi.t[0:1, :], cs.t[0:4, 268:269], pns.t[:, g, :], True, True, [cs, pns], [pi])
                fw.cp("act", impf.t[:], pi.t[0:1, :], [pi], [impf])
                iv = impf.t[:].rearrange("p (h two j) -> p h two j", two=2, j=128)
                fw.tt("dve", scs.t[:].rearrange("p (h j) -> p h j", j=128), iv[:, :, 0, :], iv[:, :, 1, :], ALU.add, [impf], [scs])
                fw.tt("dve", scs.t[:], scs.t[:], cs.t[0:1, 0:256], ALU.add, [scs, cs], [scs])
                fw.op("dve", lambda e: e.max(out=m8s.t[:, 0:8], in_=scs.t[:]), [scs], [m8s])
                fw.op("dve", lambda e: e.match_replace(out=wks.t[:], in_to_replace=m8s.t[:, 0:8], in_values=scs.t[:],
                                                       imm_value=-3.0e38), [scs, m8s], [wks])
                fw.op("dve", lambda e: e.max(out=m8s.t[:, 8:16], in_=wks.t[:]), [wks], [m8s])
                fw.ts("dve", nms.t[:], scs.t[:], m8s.t[:, 14:15], NEG, ALU.is_lt, ALU.mult, [scs, m8s], [nms])
                pc = nxt("P")
                for h in range(2):
                    fw.mm(pc.t[:, h:h + 1], nms.t[0:1, h * 128:(h + 1) * 128], identb.t[0:1, 0:1], True, True,
                          [nms, identb], [pc])
                fw.cp("act", nmcol.t[:, g, :], pc.t[:, 0:2], [pc], [nmcol])
                tb = nxt("T")
                for q4 in range(4):
                    fw.tr(tb.t[:, q4 * 4:q4 * 4 + 4], pnsb.t[:, g, q4 * 128:(q4 + 1) * 128], identb.t[0:4, 0:4],
                          [pnsb, identb], [tb])
                fw.cp("act", pnT.t[:, g, :, :].rearrange("p a b -> p (a b)"), tb.t[:, 0:16], [tb], [pnT])
                po = nxt("P")
                for q4 in range(4):
                    fw.mm(po.t[0:4, 0:64], pnT.t[:, g, q4, :], Vcmps.t[:, q4, g * 64:(g + 1) * 64], q4 == 0, q4 == 3,
                          [pnT, Vcmps], [po])
                fw.cp("act", Oall.t[:, 0, g, 0:64], po.t[0:4, 0:64], [po], [Oall])
            fw.mm(OA.t[0:4, 0:128], zerob.t[:, 0:4], zerob.t[:, 0:128], True, True, [zerob], [OA])
            fw.memset("pool", accs.t[:], 0.0, [accs])
            for ck in range(16):
                sg = stg[ck % 2]
                fw.dma("pool", sg.t[:].rearrange("p a b -> p (a b)"), cslc_d[:, :], reads=[idxa], writes=[sg],
                       indirect=bass.IndirectOffsetOnAxis(ap=idxa.t[:, s, ck:ck + 1], axis=0))
                tk = nxt("T")
                for r8 in range(8):
                    fw.tr(tk.t[:, r8 * 128:(r8 + 1) * 128], sg.t[:, r8, 0:128], ident, [sg, identb], [tk])
                fw.cp("act", kcTs.t[:].rearrange("p a b -> p (a b)"), tk.t[:, :], [tk], [kcTs])
                pb = nxt("P")
                for r8 in range(8):
                    fw.mm(pb.t[:, r8 * 8:r8 * 8 + 8], kcTs.t[:, r8, :], qrhs, True, True, [kcTs, QTzs], [pb])
                h = (ck * 8) // 64
                p4 = pb.t[:, 0:64].rearrange("p (r g n) -> p r g n", g=2, n=4)
                for g in range(2):
                    fw.act(PTs.t[:, :, g, :], p4[:, :, g, :], AF.Exp, [pb, nmcol], [PTs], bias=nmcol.t[:, g, h:h + 1])
                fw.op("dve", lambda e: e.tensor_reduce(out=acct.t[:], in_=PTs.t[:].rearrange("p r g n -> p (g n) r"),
                                                       axis=AX.X, op=ALU.add), [PTs], [acct])
                fw.tt("dve", accs.t[:], accs.t[:], acct.t[:], ALU.add, [accs, acct], [accs])
                for r8 in range(8):
                    for g in range(2):
                        fw.mm(OA.t[0:4, g * 64:(g + 1) * 64], PTs.t[:, r8, g, :], sg.t[:, r8, 128 + g * 64:192 + g * 64],
                              False, False, [PTs, sg], [OA], skip_group_check=True)
            ps_ = nxt("P")
            for g in range(2):
                fw.mm(OA.t[0:4, g * 64:(g + 1) * 64], pnmb.t[:, 0, g, :], kvsb.t[:, 1, 128 + g * 64:192 + g * 64],
                      False, False, [pnmb, kvsb], [OA], skip_group_check=True)
                fw.mm(ps_.t[0:4, g:g + 1], accs.t[:, g * 4:(g + 1) * 4], onesf, True, False, [accs, cs], [ps_])
                fw.mm(ps_.t[0:4, g:g + 1], pnm.t[:, 0, g, :], cs.t[0:4, 268:269], False, True, [pnm, cs], [ps_])
            fw.cp("act", Oall.t[:, 1, :, 0:64], OA.t[0:4, 0:128].rearrange("p (g d) -> p g d", d=64), [OA], [Oall])
            fw.cp("act", Oall.t[:, 1, :, 64], ps_.t[0:4, 0:2], [ps_], [Oall])
            fw.dma("sp", swt.t[:], swin_d[s].rearrange("(c p) f -> p c f", p=128), writes=[swt])
            fw.cp("dve", swb.t[:].rearrange("p a b -> p (a b)"), swt.t[:].rearrange("p a b -> p (a b)"), [swt], [swb])
            tk = nxt("T")
            for c in range(4):
                fw.tr(tk.t[:, c * 128:(c + 1) * 128], swb.t[:, c, 0:128], ident, [swb, identb], [tk])
            fw.cp("act", kcTs.t[:, 0:4, :].rearrange("p a b -> p (a b)"), tk.t[:, 0:512], [tk], [kcTs])
            pb = nxt("P")
            for c in range(4):
                fw.mm(pb.t[:, c * 8:c * 8 + 8], kcTs.t[:, c, :], qrhs, True, True, [kcTs, QTzs], [pb])
            fw.tt("dve", stw.t[:], pb.t[:, 0:32].rearrange("p (c e) -> p c e", e=8),
                  cs.t[:, 260:264].unsqueeze(2).to_broadcast([128, 4, 8]), ALU.add, [pb, cs], [stw])
            fw.act(PTs.t[:, 0:4, :, :].rearrange("p c g n -> p c (g n)"), stw.t[:], AF.Exp, [stw], [PTs])
            fw.op("dve", lambda e: e.tensor_reduce(out=acct.t[:], in_=PTs.t[:, 0:4, :, :].rearrange("p r g n -> p (g n) r"),
                                                   axis=AX.X, op=ALU.add), [PTs], [acct])
            fw.mm(OB.t[0:4, 0:128], zerob.t[:, 0:4], zerob.t[:, 0:128], True, True, [zerob], [OB])
            ps_ = nxt("P")
            for c in range(4):
                for g in range(2):
                    fw.mm(OB.t[0:4, g * 64:(g + 1) * 64], PTs.t[:, c, g, :], swb.t[:, c, 128 + g * 64:192 + g * 64],
                          False, False, [PTs, swb], [OB], skip_group_check=True)
            for g in range(2):
                fw.mm(OB.t[0:4, g * 64:(g + 1) * 64], pnmb.t[:, 1, g, :], kvsb.t[:, 1, 256 + g * 64:320 + g * 64],
                      False, False, [pnmb, kvsb], [OB], skip_group_check=True)
                fw.mm(ps_.t[0:4, g:g + 1], acct.t[:, g * 4:(g + 1) * 4], onesf, True, False, [acct, cs], [ps_])
                fw.mm(ps_.t[0:4, g:g + 1], pnm.t[:, 1, g, :], cs.t[0:4, 268:269], False, True, [pnm, cs], [ps_])
            fw.cp("act", Oall.t[:, 2, :, 0:64], OB.t[0:4, 0:128].rearrange("p (g d) -> p g d", d=64), [OB], [Oall])
            fw.cp("act", Oall.t[:, 2, :, 64], ps_.t[0:4, 0:2], [ps_], [Oall])
            for n in range(4):
                fw.dma("sp", Ocat.t[s:s + 1, n, :, :, :].rearrange("p a b c -> p (a b c)"),
                       Oall.t[n:n + 1, :, :, :].rearrange("p a b c -> p (a b c)"), reads=[Oall], writes=[Ocat])
        for g in range(2):
            gv = gs.t[:, g * 12:(g + 1) * 12].rearrange("p (n b) -> p n b", b=3)
            fw.ts("dve", fcs.t[:, 1, :], Ocat.t[:, :, 1, g, 64], 1e-30, None, ALU.max, None, [Ocat], [fcs])
            fw.ts("dve", fcs.t[:, 2, :], Ocat.t[:, :, 2, g, 64], 1e-30, None, ALU.max, None, [Ocat], [fcs])
            fw.op("dve", lambda e: e.reciprocal(out=fcs.t[:, 1:3, :], in_=fcs.t[:, 1:3, :]), [fcs], [fcs])
            fw.tt("dve", fcs.t[:, 1, :], fcs.t[:, 1, :], gv[:, :, 1], ALU.mult, [fcs, gs], [fcs])
            fw.tt("dve", fcs.t[:, 2, :], fcs.t[:, 2, :], gv[:, :, 2], ALU.mult, [fcs, gs], [fcs])
            fw.tt("dve", o1.t[:], Ocat.t[:, :, 0, g, 0:64], gv[:, :, 0].unsqueeze(2).to_broadcast([4, 4, 64]), ALU.mult,
                  [Ocat, gs], [o1])
            fw.tt("dve", o2.t[:], Ocat.t[:, :, 1, g, 0:64], fcs.t[:, 1, :].unsqueeze(2).to_broadcast([4, 4, 64]), ALU.mult,
                  [Ocat, fcs], [o2])
            fw.tt("dve", o1.t[:], o1.t[:], o2.t[:], ALU.add, [o1, o2], [o1])
            fw.tt("dve", o2.t[:], Ocat.t[:, :, 2, g, 0:64], fcs.t[:, 2, :].unsqueeze(2).to_broadcast([4, 4, 64]), ALU.mult,
                  [Ocat, fcs], [o2])
            fw.tt("dve", osf.t[:, g * 256:(g + 1) * 256].rearrange("p (n d) -> p n d", d=64), o1.t[:], o2.t[:], ALU.add,
                  [o1, o2], [osf])
        f0 = FM0 + 2048
        fw.act(ezs.t[:], tms.t[:, f0:f0 + 512], AF.Exp, [tms], [ezs], scale=-1.0)
        fw.ts("dve", ezs.t[:], ezs.t[:], 1.0, None, ALU.add, None, [ezs], [ezs])
        fw.op("dve", lambda e: e.reciprocal(out=ezs.t[:], in_=ezs.t[:]), [ezs], [ezs])
        fw.tt("dve", ezs.t[:], ezs.t[:], tms.t[:, f0:f0 + 512], ALU.mult, [ezs, tms], [ezs])
        fw.tt("dve", mixs.t[:, 512:1024], osf.t[:], ezs.t[:], ALU.mult, [osf, ezs], [mixs])
        fw.cp("dve", mixb.t[:], mixs.t[:], [mixs], [mixb])
        tb = nxt("T")
        for kc in range(8):
            fw.tr(tb.t[:, kc * 4:kc * 4 + 4], mixb.t[:, kc * 128:(kc + 1) * 128], identb.t[0:4, 0:4], [mixb, identb], [tb])
        fw.cp("act", mixTs.t[:].rearrange("p a b -> p (a b)"), tb.t[:, 0:32], [tb], [mixTs])
        for hf in range(2):
            pb = nxt("P")
            for kc in range(8):
                fw.mm(pb.t[0:4, :], mixTs.t[:, kc, :], woutb.t[:, kc, hf * 512:(hf + 1) * 512], kc == 0, kc == 7,
                      [mixTs, woutb], [pb])
            fw.tt("dve", xs.t[:, hf * 512:(hf + 1) * 512], pb.t[0:4, :], xs.t[:, hf * 512:(hf + 1) * 512], ALU.add,
                  [pb, xs], [xs])
        fw.dma("sp", ys_o[:, :], ysb.t[:], reads=[ysb], is_output=True)
        barrier([ysb, kvs, scv, us, swt, Oall, Ocat] + stg)
        ess.close()

    import os
    STAGE = int(os.environ.get("KSTAGE", "99"))
    if STAGE == 0:
        fw.finish()
        return nc, es

    def chk(n):
        if STAGE == n:
            raise StopBuild()
    if do_sample:
        sample_path()

    KTs = fw.sb("KTs", [128, NTILE, 128], BF16)
    KTs_b = [Buf("KTs%d" % i) for i in range(NTILE)]
    Vs = fw.sb("Vs", [128, NTILE, 2, 65], BF16)
    Vs_b = [Buf("Vs%d" % i) for i in range(NTILE)]
    KTw = fw.sb("KTw", [128, 8, 128], BF16)
    KTw_b = [Buf("KTw%d" % i) for i in range(8)]
    Vw = fw.sb("Vw", [128, 8, 2, 65], BF16)
    Vw_b = [Buf("Vw%d" % i) for i in range(8)]
    KcmpT = fw.sb("KcmpT", [128, 256], BF16)
    VcmpT = fw.sb("VcmpT", [128, 256], BF16)
    Vcmp_tm = fw.sb("Vcmp_tm", [128, 2, 128], BF16)
    score = [fw.sb("score%d" % g, [128, 128], F32) for g in range(2)]
    pn = fw.sb("pn", [128, 4, 256], F32)
    pnb = fw.sb("pnb", [128, 4, 256], BF16)
    xt = [fw.sb("xt%d" % i, [128, 1024], F32) for i in range(3)]
    rp = [fw.sb("rp%d" % i, [128, 64], F32) for i in range(3)]
    ss = fw.sb("ss", [128, 1], F32)
    lnv = fw.sb("lnv", [128, 1], F32)
    rstd = fw.sb("rstd", [128, 1], F32)
    hb = fw.sb("hb", [128, 1024], BF16)
    hT = [fw.sb("hT%d" % i, [128, 8, 128], BF16) for i in range(2)]
    tm = fw.sb("tm", [128, 920], F32)
    ssh = fw.sb("ssh", [128, 14], F32)
    lnh = fw.sb("lnh", [128, 14], F32)
    rsh = fw.sb("rsh", [128, 14], F32)
    tcs = fw.sb("tcs", [128, 896], F32)
    tsn = fw.sb("tsn", [128, 896], F32)
    kvout = fw.sb("kvout", [128, 2, 384], F32)
    kb = fw.sb("kb", [128, 4, 128], BF16)
    qb = fw.sb("qb", [128, 512], BF16)
    kcT = fw.sb("kcT", [128, 2, 2, 128], BF16)
    kcT_b = [Buf("kcT0"), Buf("kcT1")]
    QTz = fw.sb("QTz", [128, 2, 4, 128], BF16)
    gate = fw.sb("gate", [128, 24], F32)
    csb = fw.sb("csb", [128, 128], F32)
    uext = fw.sb("uext", [128, 4, 130], F32)
    halo = [fw.sb("halo%d" % i, [128, 4, 2], F32) for i in range(2)]
    halot = fw.sb("halot", [128, 4, 2], F32)
    hcs = fw.sb("hcs", [128, 16], F32)
    ez = fw.sb("ez", [128, 512], F32)
    g1 = fw.sb("g1", [128, 128], F32)
    acc = fw.sb("acc", [128, 128], F32)
    mixT = fw.sb("mixT", [128, 8, 128], BF16)
    szT = fw.sb("szT", [128, 4, 128], F32)
    mx = fw.sb("mx", [128, 4], F32)
    sm = fw.sb("sm", [128, 4], F32)
    a1 = fw.sb("a1", [128, 256], F32)
    a2 = fw.sb("a2", [128, 256], F32)
    m8 = fw.sb("m8", [128, 16], F32)
    wk = fw.sb("wk", [128, 128], F32)
    nmask = [fw.sb("nmask%d" % g, [128, 128], BF16) for g in range(2)]
    expb = [fw.sb("expb%d" % i, [128, 4, 128], BF16) for i in range(2)]
    PT = [fw.sb("PT%d" % i, [128, 4, 128], BF16) for i in range(2)]
    PTc = fw.sb("PTc", [128, 2, 4, 128], BF16)
    ocmp = fw.sb("ocmp", [128, 4, 64], F32)
    fac = fw.sb("fac", [128, 3, 4], F32)
    rs2 = fw.sb("rs2", [128, 2, 4], F32)
    ot1 = fw.sb("ot1", [128, 4, 64], F32)
    ot2 = fw.sb("ot2", [128, 4, 64], F32)
    oall = fw.sb("oall", [128, 8, 64], BF16)
    fw.memset("pool", Vs.t[:], 1.0, [Vs] + Vs_b)
    fw.memset("pool", Vw.t[:], 1.0, [Vw] + Vw_b)
    fw.memset("pool", KcmpT.t[:], 0.0, [KcmpT])
    fw.memset("pool", VcmpT.t[:], 0.0, [VcmpT])
    for g in range(2):
        fw.memset("pool", score[g].t[:], BIGNEG, [score[g]])
    fw.memset("pool", pn.t[:], 0.0, [pn])
    fw.memset("pool", pnb.t[:], 0.0, [pnb])
    fw.memset("pool", QTz.t[:], 0.0, [QTz])
    fw.memset("pool", halo[1].t[:], 0.0, [halo[1]])
    epsap = cst.t[:, C_AB + 2:C_AB + 3]

    def xbuf(m):
        k, own = m // 2, (m % 2 == 0)
        sq_ = 2 * k + (1 if own else 0)
        return xt[sq_ % 3], rp[sq_ % 3]

    def load_x(m):
        xb, rb = xbuf(m)
        fw.dma("sp", xb.t[:], xp[m * 128:(m + 1) * 128, :], writes=[xb])
        fw.dma("sp", rb.t[:], rope[m * 128:(m + 1) * 128, :], writes=[rb])

    def tile_front(m, own, k):
        xb, rb = xbuf(m)
        hTb = hT[m % 2]
        fw.act(hb.t[:], xb.t[:], AF.Square, [xb], [hb, ss], accum_out=ss.t[:])
        fw.act(lnv.t[:], ss.t[:], AF.Ln, [ss], [lnv], scale=1.0 / 1024, bias=epsap)
        fw.act(rstd.t[:], lnv.t[:], AF.Exp, [lnv], [rstd], scale=-0.5)
        fw.ts("dve", hb.t[:], xb.t[:], rstd.t[:, 0:1], None, ALU.mult, None, [xb, rstd], [hb])
        tb = nxt("T")
        for kc in range(8):
            fw.tr(tb.t[:, kc * 128:(kc + 1) * 128], hb.t[:, kc * 128:(kc + 1) * 128], ident, [hb, identb], [tb])
        fw.cp("act", hTb.t[:].rearrange("p a b -> p (a b)"), tb.t[:, :], [tb], [hTb])
        chk(10)
        pieces = [(0, 512), (512, 1024), (1024, 1304)] if own else [(512, 1024), (1024, 1280)]
        for (c0, c1) in pieces:
            pb = nxt("P")
            for kc in range(8):
                fw.mm(pb.t[:, 0:c1 - c0], hTb.t[:, kc, :], winb.t[:, kc, c0:c1], kc == 0, kc == 7, [hTb, winb], [pb])
            if c0 == 0:
                fw.cp("act", tm.t[:, 0:512], pb.t[:, 0:512], [pb], [tm])
            elif c0 == 512:
                fw.cp("act", tm.t[:, 512:896], pb.t[:, 0:384], [pb], [tm])
                fw.cp("act", kvout.t[:, 1, 0:128], pb.t[:, 384:512], [pb], [kvout])
            else:
                fw.cp("act", kvout.t[:, 1, 128:384], pb.t[:, 0:256], [pb], [kvout])
                if own:
                    fw.cp("act", tm.t[:, 896:920], pb.t[:, 256:280], [pb], [tm])
        r0, r1 = (0, 896) if own else (512, 896)
        chk(11)
        qk_norm("pool", tm, r0, r1, ssh, lnh, rsh, tsn, 128, epsap)
        chk(12)
        if own:
            cosb = rb.t[:, 0:32].unsqueeze(1).to_broadcast([128, 16, 32])
            sinb = rb.t[:, 32:64].unsqueeze(1).to_broadcast([128, 16, 32])
            rope_apply("pool", tm.t[:, 0:512].rearrange("p (h d) -> p h d", d=64),
                       qb.t[:].rearrange("p (h d) -> p h d", d=64), cosb, sinb, tcs, tsn, 8, [tm, rb], [qb])
        cosb = rb.t[:, 0:32].unsqueeze(1).to_broadcast([128, 12, 32])
        sinb = rb.t[:, 32:64].unsqueeze(1).to_broadcast([128, 12, 32])
        rope_apply("pool", tm.t[:, 512:896].rearrange("p (h d) -> p h d", d=64),
                   kvout.t[:, 0, :].rearrange("p (h d) -> p h d", d=64), cosb, sinb, tcs, tsn, 6, [tm, rb], [kvout])
        kvo = kvout.t[:].rearrange("p j (b c) -> p j b c", c=128)
        chk(13)
        if own:
            fw.dma("sp", cmp_o[k * 128:(k + 1) * 128, :].rearrange("t (j c) -> t j c", j=2), kvo[:, :, 0, :],
                   reads=[kvout], is_output=True)
            fw.dma("sp", slc_o[k * 128:(k + 1) * 128, :].rearrange("t (j c) -> t j c", j=2), kvo[:, :, 1, :],
                   reads=[kvout], is_output=True)
            if k >= 30:
                fw.dma("sp", win_o[(k - 30) * 128:(k - 29) * 128, :].rearrange("t (j c) -> t j c", j=2), kvo[:, :, 2, :],
                       reads=[kvout], is_output=True)
        chk(14)
        fw.cp("dve", kb.t[:, 0:2, :], kvo[:, :, 0, :], [kvout], [kb])
        fw.cp("dve", kb.t[:, 2:4, :], kvo[:, 0, 1:3, :], [kvout], [kb])
        fw.cp("pool", Vs.t[:, m, :, 0:64], kvout.t[:, 1, 128:256].rearrange("p (g d) -> p g d", d=64), [kvout], [Vs_b[m]])
        fw.cp("pool", Vw.t[:, m % 8, :, 0:64], kvout.t[:, 1, 256:384].rearrange("p (g d) -> p g d", d=64), [kvout],
              [Vw_b[m % 8]])
        chk(15)
        tb = nxt("T")
        for j in range(4):
            fw.tr(tb.t[:, j * 128:(j + 1) * 128], kb.t[:, j, :], ident, [kb, identb], [tb])
        if own:
            for n in range(4):
                fw.tr(tb.t[:, 512 + n * 128:512 + (n + 1) * 128], qb.t[:, n * 128:(n + 1) * 128], ident, [qb, identb], [tb])
        chk(16)
        ti = m % 2
        fw.cp("act", kcT.t[:, ti, :, :].rearrange("p a b -> p (a b)"), tb.t[:, 0:256], [tb], [kcT_b[ti]])
        fw.cp("act", KTs.t[:, m, :], tb.t[:, 256:384], [tb], [KTs_b[m]])
        fw.cp("act", KTw.t[:, m % 8, :], tb.t[:, 384:512], [tb], [KTw_b[m % 8]])
        if own:
            fw.cp("act", QTz.t[0:64, 0, :, :], tb.t[0:64, 512:1024].rearrange("p (n t) -> p n t", t=128), [tb], [QTz])
            fw.cp("act", QTz.t[64:128, 1, :, :], tb.t[64:128, 512:1024].rearrange("p (n t) -> p n t", t=128), [tb], [QTz])
            fw.act(gate.t[:], tm.t[:, 896:920], AF.Exp, [tm], [gate], scale=-1.0)
            fw.ts("dve", gate.t[:], gate.t[:], 1.0, None, ALU.add, None, [gate], [gate])
            fw.op("dve", lambda e: e.reciprocal(out=gate.t[:], in_=gate.t[:]), [gate], [gate])
        else:
            chk(17)
            pb = nxt("P")
            for ct in range(4):
                for j in range(2):
                    col = FM0 + (1 + j) * 512 + ct * 128
                    o0 = (ct * 2 + j) * 2
                    for kc in range(8):
                        fw.mm(pb.t[:, o0:o0 + 2], winb.t[:, kc, col:col + 128], hTb.t[:, kc, 126:128], kc == 0, kc == 7,
                              [winb, hTb], [pb], skip_group_check=True)
            hl = halo[k % 2]
            fw.cp("act", hcs.t[:], pb.t[:, 0:16], [pb], [hcs])
            hc4 = hcs.t[:].rearrange("p (c j t) -> p c j t", j=2, t=2)
            fw.tt("dve", hl.t[:], hc4[:, :, 0, :], hc4[:, :, 1, :], ALU.mult, [hcs], [hl])

    def compress(k):
        rk = kcT.t[:, :, 0, :].rearrange("p a (c r) -> p a c r", r=32)
        rv = kcT.t[:, :, 1, :].rearrange("p a (c r) -> p a c r", r=32)
        pb = nxt("P")
        for r in range(32):
            fw.mm(pb.t[:, 0:8], wbd.t[:, r, 0, :], rk[:, :, :, r], r == 0, r == 31, [wbd] + kcT_b, [pb])
        for r in range(32):
            fw.mm(pb.t[:, 8:16], wbd.t[:, r, 1, :], rv[:, :, :, r], r == 0, r == 31, [wbd] + kcT_b, [pb],
                  skip_group_check=True)
        fw.act(KcmpT.t[:, 8 * k:8 * k + 8], pb.t[:, 0:8], AF.Identity, [pb, biasK], [KcmpT], bias=biasK.t[:, 0:1])
        fw.cp("dve", VcmpT.t[:, 8 * k:8 * k + 8], pb.t[:, 8:16], [pb], [VcmpT])
        ch = (8 * k) // 128
        tb = nxt("T")
        fw.tr(tb.t[:, 0:128], VcmpT.t[:, ch * 128:(ch + 1) * 128], ident, [VcmpT, identb], [tb])
        fw.cp("act", csb.t[:], tb.t[:, 0:128], [tb], [csb])
        fw.tt("dve", Vcmp_tm.t[:, ch, :], csb.t[:], biasVb.t[:], ALU.add, [csb, biasVb], [Vcmp_tm])

    def chunk_attn(g, kt_ap, kt_buf, v_ap, v_buf, bias_list, Oacc):
        sb_ = nxt("S")
        sap = Sap(sb_)
        nb = len(bias_list)
        fw.mm(sap, kt_ap, QTz.t[:, g, :, :].rearrange("p n t -> p (n t)"), True, nb == 0, [kt_buf, QTz], [sb_])
        for bi, (l_ap, r_ap, bufs) in enumerate(bias_list):
            fw.mm(sap, l_ap, r_ap, False, bi == nb - 1, bufs, [sb_])
        pt = PT[rr["S"] % 2]
        fw.act(pt.t[:].rearrange("p n t -> p (n t)"), sap, AF.Exp, [sb_], [pt])
        for n in range(4):
            fw.mm(Oacc.t[:, n * 65:n * 65 + 65], pt.t[:, n, :], v_ap, False, False, [pt, v_buf], [Oacc],
                  skip_group_check=True)

    def attention(k):
        C = 8 * (k + 1)
        nch = (C + 127) // 128
        NS = C // 2
        wp0 = C_WP + 128 - 4 * k
        for g in range(2):
            sa, sb2 = S[0], S[1]
            for n in range(4):
                bank = sa if n < 2 else sb2
                fw.mm(Sps[:, n * 256:n * 256 + C], QTz.t[:, g, n, :], KcmpT.t[:, 0:C], True, True, [QTz, KcmpT], [bank])
            s4 = Sps[:, :].rearrange("p (n c) -> p n c", c=256)[:, :, 0:C]
            fw.op("dve", lambda e: e.tensor_reduce(out=mx.t[:], in_=s4, axis=AX.X, op=ALU.max), [sa, sb2], [mx])
            fw.ts("dve", mx.t[:], mx.t[:], -1.0, None, ALU.mult, None, [mx], [mx])
            for n in range(4):
                bank = sa if n < 2 else sb2
                fw.act(pn.t[:, n, 0:C], Sps[:, n * 256:n * 256 + C], AF.Exp, [bank, mx], [pn], bias=mx.t[:, n:n + 1])
            fw.tt("dve", pn.t[:, :, C - 8:C], pn.t[:, :, C - 8:C],
                  cst.t[:, C_VIS:C_VIS + 8].unsqueeze(1).to_broadcast([128, 4, 8]), ALU.mult, [pn, cst], [pn])
            fw.op("dve", lambda e: e.tensor_reduce(out=sm.t[:], in_=pn.t[:, :, 0:C], axis=AX.X, op=ALU.add), [pn], [sm])
            fw.ts("dve", sm.t[:], sm.t[:], 1e-30, None, ALU.max, None, [sm], [sm])
            fw.op("dve", lambda e: e.reciprocal(out=sm.t[:], in_=sm.t[:]), [sm], [sm])
            fw.tt("dve", pn.t[:, :, 0:C], pn.t[:, :, 0:C], sm.t[:].unsqueeze(2).to_broadcast([128, 4, C]), ALU.mult,
                  [pn, sm], [pn])
            fw.cp("pool", pnb.t[:, :, 0:C], pn.t[:, :, 0:C], [pn], [pnb])
            fw.tt("pool", a1.t[:, 0:C], pn.t[:, 0, 0:C], pn.t[:, 1, 0:C], ALU.add, [pn], [a1])
            fw.tt("dve", a2.t[:, 0:C], pn.t[:, 2, 0:C], pn.t[:, 3, 0:C], ALU.add, [pn], [a2])
            fw.tt("dve", a1.t[:, 0:C], a1.t[:, 0:C], a2.t[:, 0:C], ALU.add, [a1, a2], [a1])
            a1v = a1.t[:, 0:C].rearrange("p (s two) -> p s two", two=2)
            sc = score[g]
            fw.tt("dve", sc.t[:, 0:NS], a1v[:, :, 0], a1v[:, :, 1], ALU.add, [a1], [sc])
            fw.tt("dve", sc.t[:, 0:NS], sc.t[:, 0:NS], cst.t[:, wp0:wp0 + NS], ALU.add, [sc, cst], [sc])
            fw.tt("dve", sc.t[:, 0:NS], sc.t[:, 0:NS], cst.t[:, C_B0:C_B0 + NS], ALU.add, [sc, cst], [sc])
            fw.op("dve", lambda e: e.max(out=m8.t[:, 0:8], in_=sc.t[:]), [sc], [m8])
            fw.op("dve", lambda e: e.match_replace(out=wk.t[:], in_to_replace=m8.t[:, 0:8], in_values=sc.t[:],
                                                   imm_value=-3.0e38), [sc, m8], [wk])
            fw.op("dve", lambda e: e.max(out=m8.t[:, 8:16], in_=wk.t[:]), [wk], [m8])
            nm = nmask[g]
            fw.ts("dve", nm.t[:], sc.t[:], m8.t[:, 15:16], NEG, ALU.is_lt, ALU.mult, [sc, m8], [nm])
            tb = nxt("T")
            for ch in range(nch):
                for n in range(4):
                    fw.tr(tb.t[:, (ch * 4 + n) * 128:(ch * 4 + n + 1) * 128], pnb.t[:, n, ch * 128:(ch + 1) * 128], ident,
                          [pnb, identb], [tb])
            fw.cp("act", PTc.t[:, 0:nch, :, :].rearrange("p c n t -> p (c n t)"), tb.t[:, 0:nch * 512], [tb], [PTc])
            pb = nxt("P")
            for n in range(4):
                for ch in range(nch):
                    fw.mm(pb.t[:, n * 64:(n + 1) * 64], PTc.t[:, ch, n, :], Vcmp_tm.t[:, ch, g * 64:(g + 1) * 64],
                          ch == 0, ch == nch - 1, [PTc, Vcmp_tm], [pb], skip_group_check=True)
            fw.cp("act", ocmp.t[:].rearrange("p n d -> p (n d)"), pb.t[:, 0:256], [pb], [ocmp])
            fw.mm(OA.t[:, :], zerob.t[:, 0:128], zerob.t[:, :], True, True, [zerob], [OA])
            fw.mm(OB.t[:, :], zerob.t[:, 0:128], zerob.t[:, :], True, True, [zerob], [OB])
            nchunk = 2 * k + 2
            for mc in range(nchunk):
                eb = expb[(mc // 4) % 2]
                if mc % 4 == 0 or mc == 2 * k + 1:
                    if mc == 2 * k + 1:
                        lo, nb_ = mc, 1
                    else:
                        lo, nb_ = mc, min(4, 2 * k - mc)
                    if nb_ > 0:
                        fw.cp("pool", eb.t[:, lo % 4:lo % 4 + nb_, :].rearrange("p c (h r) -> p (c h) r", r=64),
                              nm.t[:, 2 * lo:2 * lo + 2 * nb_].unsqueeze(2).to_broadcast([128, 2 * nb_, 64]), [nm], [eb])
                if mc < 2 * k:
                    bl = [(eb.t[:, mc % 4, :], ident4, [eb, identb])]
                elif mc == 2 * k:
                    bl = [(ident, cbf.t[:, 0, :].unsqueeze(1).to_broadcast([128, 4, 128]), [identb, cbf])]
                else:
                    bl = [(eb.t[:, mc % 4, :], ident4, [eb, identb]),
                          (ident, cbf.t[:, 2, :].unsqueeze(1).to_broadcast([128, 4, 128]), [identb, cbf])]
                chunk_attn(g, KTs.t[:, mc, :], KTs_b[mc], Vs.t[:, mc, g, :], Vs_b[mc], bl, OA)
            for mc, bidx in ((2 * k - 4, 1), (2 * k - 3, 3), (2 * k - 2, None), (2 * k - 1, None), (2 * k, 0), (2 * k + 1, 2)):
                if mc < 0:
                    continue
                bl = []
                if bidx is not None:
                    bl = [(ident, cbf.t[:, bidx, :].unsqueeze(1).to_broadcast([128, 4, 128]), [identb, cbf])]
                chunk_attn(g, KTw.t[:, mc % 8, :], KTw_b[mc % 8], Vw.t[:, mc % 8, g, :], Vw_b[mc % 8], bl, OB)
            oa4 = OA.t[:, 0:260].rearrange("p (n d) -> p n d", d=65)
            ob4 = OB.t[:, 0:260].rearrange("p (n d) -> p n d", d=65)
            gv = gate.t[:, g * 12:(g + 1) * 12].rearrange("p (n b) -> p n b", b=3)
            fw.ts("dve", rs2.t[:, 0, :], oa4[:, :, 64], 1e-30, None, ALU.max, None, [OA], [rs2])
            fw.ts("dve", rs2.t[:, 1, :], ob4[:, :, 64], 1e-30, None, ALU.max, None, [OB], [rs2])
            fw.op("dve", lambda e: e.reciprocal(out=rs2.t[:], in_=rs2.t[:]), [rs2], [rs2])
            fw.tt("dve", fac.t[:, 1, :], rs2.t[:, 0, :], gv[:, :, 1], ALU.mult, [rs2, gate], [fac])
            fw.tt("dve", fac.t[:, 2, :], rs2.t[:, 1, :], gv[:, :, 2], ALU.mult, [rs2, gate], [fac])
            fw.tt("dve", ot1.t[:], ocmp.t[:], gv[:, :, 0].unsqueeze(2).to_broadcast([128, 4, 64]), ALU.mult, [ocmp, gate], [ot1])
            fw.tt("dve", ot2.t[:], oa4[:, :, 0:64], fac.t[:, 1, :].unsqueeze(2).to_broadcast([128, 4, 64]), ALU.mult,
                  [OA, fac], [ot2])
            fw.tt("pool", ot1.t[:], ot1.t[:], ot2.t[:], ALU.add, [ot1, ot2], [ot1])
            fw.tt("dve", ot2.t[:], ob4[:, :, 0:64], fac.t[:, 2, :].unsqueeze(2).to_broadcast([128, 4, 64]), ALU.mult,
                  [OB, fac], [ot2])
            fw.tt("dve", oall.t[:, g * 4:(g + 1) * 4, :], ot1.t[:], ot2.t[:], ALU.add, [ot1, ot2], [oall])

    def tile_back(m, k):
        hTb = hT[m % 2]
        ue, mt, sz = uext, mixT, szT
        fw.ts("dve", halot.t[:], halo[(k + 1) % 2].t[:], cst.t[:, C_AB:C_AB + 1], None, ALU.mult, None,
              [halo[(k + 1) % 2], cst], [halot])
        fw.stt(ue.t[:, :, 0:2], halo[k % 2].t[:], cst.t[:, C_AB + 1:C_AB + 2], halot.t[:], ALU.mult, ALU.add,
               [halo[k % 2], cst, halot], [ue])
        for ct in range(4):
            pb = nxt("P")
            for j in range(4):
                col = FM0 + j * 512 + ct * 128
                for kc in range(8):
                    fw.mm(pb.t[:, j * 128:(j + 1) * 128], winb.t[:, kc, col:col + 128], hTb.t[:, kc, :], kc == 0, kc == 7,
                          [winb, hTb], [pb], skip_group_check=True)
            fw.cp("act", csb.t[:], pb.t[:, 128:256], [pb], [csb])
            fw.tt("dve", ue.t[:, ct, 2:130], pb.t[:, 256:384], csb.t[:], ALU.mult, [pb, csb], [ue])
            fw.act(ez.t[:, 0:128], pb.t[:, 384:512], AF.Exp, [pb], [ez], scale=-1.0)
            fw.ts("dve", ez.t[:, 0:128], ez.t[:, 0:128], 1.0, None, ALU.add, None, [ez], [ez])
            fw.op("dve", lambda e: e.reciprocal(out=ez.t[:, 0:128], in_=ez.t[:, 0:128]), [ez], [ez])
            fw.tt("dve", g1.t[:], pb.t[:, 0:128], ez.t[:, 0:128], ALU.mult, [pb, ez], [g1])
            fw.tt("dve", g1.t[:], pb.t[:, 384:512], g1.t[:], ALU.mult, [pb, g1], [g1])
            fw.ts("dve", acc.t[:], ue.t[:, ct, 0:128], cvp.t[:, ct * 3:ct * 3 + 1], None, ALU.mult, None, [ue, cvp], [acc])
            fw.stt(acc.t[:], ue.t[:, ct, 1:129], cvp.t[:, ct * 3 + 1:ct * 3 + 2], acc.t[:], ALU.mult, ALU.add,
                   [ue, cvp, acc], [acc])
            fw.stt(acc.t[:], ue.t[:, ct, 2:130], cvp.t[:, ct * 3 + 2:ct * 3 + 3], acc.t[:], ALU.mult, ALU.add,
                   [ue, cvp, acc], [acc])
            fw.stt(mt.t[:, ct, :], acc.t[:], cvp.t[:, 12 + ct:13 + ct], g1.t[:], ALU.add, ALU.mult, [acc, cvp, g1], [mt])
        if k == NSLOT - 1:
            for ct in range(4):
                fw.dma("sp", conv_o[:, ct * 128:(ct + 1) * 128].rearrange("t p -> p t"), ue.t[:, ct, 128:130], reads=[ue],
                       is_output=True, allow_slow_non_contiguous=True)
        pb = nxt("P")
        for ct in range(4):
            col = FM0 + 2048 + ct * 128
            for kc in range(8):
                fw.mm(pb.t[:, ct * 128:(ct + 1) * 128], winb.t[:, kc, col:col + 128], hTb.t[:, kc, :], kc == 0, kc == 7,
                      [winb, hTb], [pb], skip_group_check=True)
        fw.act(ez.t[:], pb.t[:, :], AF.Exp, [pb], [ez], scale=-1.0)
        fw.ts("dve", ez.t[:], ez.t[:], 1.0, None, ALU.add, None, [ez], [ez])
        fw.op("dve", lambda e: e.reciprocal(out=ez.t[:], in_=ez.t[:]), [ez], [ez])
        fw.tt("dve", sz.t[:].rearrange("p a b -> p (a b)"), pb.t[:, :], ez.t[:], ALU.mult, [pb, ez], [sz])

    def tile_out(m, k):
        xb, _ = xbuf(m)
        mt, sz = mixT, szT
        tb = nxt("T")
        for ct in range(4):
            fw.tr(tb.t[:, ct * 128:(ct + 1) * 128], oall.t[:, 2 * ct:2 * ct + 2, :].rearrange("p a b -> p (a b)"), ident,
                  [oall, identb], [tb])
        fw.cp("act", ez.t[:], tb.t[:, 0:512], [tb], [ez])
        fw.tt("dve", mt.t[:, 4:8, :].rearrange("p a b -> p (a b)"), ez.t[:], sz.t[:].rearrange("p a b -> p (a b)"),
              ALU.mult, [ez, sz], [mt])
        for hf in range(2):
            pb = nxt("P")
            for kc in range(8):
                fw.mm(pb.t[:, :], mt.t[:, kc, :], woutb.t[:, kc, hf * 512:(hf + 1) * 512], kc == 0, kc == 7, [mt, woutb], [pb])
            fw.tt("dve", xb.t[:, hf * 512:(hf + 1) * 512], pb.t[:, :], xb.t[:, hf * 512:(hf + 1) * 512], ALU.add, [pb, xb], [xb])
        fw.dma("sp", y_o[k * 128:(k + 1) * 128, :], xb.t[:], reads=[xb], is_output=True)

    load_x(1)
    load_x(0)
    try:
      for k in range(nslot):
        mo, mw = 2 * k + 1, 2 * k
        if k + 1 < nslot:
            load_x(2 * k + 3)
        tile_front(mo, False, k)
        chk(1)
        if k + 1 < nslot:
            load_x(2 * k + 2)
        tile_front(mw, True, k)
        chk(2)
        compress(k)
        chk(3)
        tile_back(mw, k)
        chk(4)
        attention(k)
        chk(5)
        tile_out(mw, k)
    except StopBuild:
        pass

    fw.finish()
    print("program: %d tracked instructions, %d dma sems, counts %s" % (fw.ninst, fw.nd, fw.cnt))
    return nc, es


def _tile_of_slot(mm, p):
    k, o = mm // 2, mm % 2
    return 2 * k + (p if o == 0 else 1 - p)


def _consts(p):
    c = np.zeros((128, C_END), np.float32)
    c2 = np.zeros((128, 640), np.float32)
    kk = np.arange(128)[:, None]
    qq = np.arange(128)[None, :]
    c2[:, 0:128] = np.where(kk <= qq, 0.0, NEG)
    c2[:, 128:256] = np.where(kk > qq, 0.0, NEG)
    c2[:, 256:384] = NEG if p == 0 else 0.0
    c2[:, 384:512] = 0.0 if p == 0 else NEG
    c2[:, 512:640] = np.eye(128, dtype=np.float32)
    ql = np.arange(128)
    W = np.zeros((128, 384), np.float32)
    for r in range(-128, 256):
        col = 128 + r
        if r <= -2:
            v = np.zeros(128)
        elif r == -1:
            v = np.where(ql < 64, 1e4, 0.0) if p == 0 else np.zeros(128)
        elif r == 0:
            v = np.full(128, 1e4)
        elif r == 1:
            v = np.where(ql >= 64, 1e4, BIGNEG)
        elif r == 2:
            v = np.full(128, BIGNEG) if p == 0 else np.zeros(128)
        elif r == 3:
            v = np.full(128, BIGNEG) if p == 0 else np.where(ql < 64, 1e4, 0.0)
        else:
            v = np.full(128, BIGNEG)
        W[:, col] = v
    c[:, C_WP:C_WP + 384] = W
    c[:, C_B0 + (0 if p == 0 else 2)] = 1e4
    vis = np.zeros((128, 8), np.float32)
    for cl in range(4):
        vis[:, cl] = (32 * cl + 31 <= ql)
        vis[:, 4 + cl] = 0.0 if p == 0 else 1.0
    c[:, C_VIS:C_VIS + 8] = vis
    c[:, C_AB] = 1.0 if p == 0 else 0.0
    c[:, C_AB + 1] = 0.0 if p == 0 else 1.0
    c[:, C_AB + 2] = EPS
    return c, c2


def _consts_s():
    c = np.zeros((128, 272), np.float32)
    c[:, 0] = 1e4
    c[:, 255] = 1e4
    for s in range(4):
        c[s, 256 + s] = 1.0
    c[0, 260] = NEG
    c[:, 268] = 1.0
    return c


def _perm_w_in():
    q0, kv0, gl0, za0 = 2048, 2560, 3328, 3352
    cols = []
    for n in range(4):
        for g in range(2):
            cols += list(range(q0 + (g * 4 + n) * 64, q0 + (g * 4 + n + 1) * 64))
    for br in range(3):
        for g in range(2):
            cols += list(range(kv0 + br * 256 + g * 64, kv0 + br * 256 + g * 64 + 64))
    for br in range(3):
        for g in range(2):
            cols += list(range(kv0 + br * 256 + 128 + g * 64, kv0 + br * 256 + 128 + g * 64 + 64))
    cols += list(range(gl0, gl0 + 24))
    cols += list(range(0, 2048))
    cols += list(range(za0, za0 + 512))
    return np.array(cols)


_CACHE = {}


def kernel(x_prompt, x_sample, cache_cmp_kv, cache_slc_kv, state_win_kv, state_conv, page_table,
           norm_g, w_in, conv_w, conv_b, q_gain, k_gain, cmp_pe, cmp_w, w_out, _nslot=NSLOT, _do_sample=True):
    f = np.float32
    x_prompt = np.asarray(x_prompt, f)
    if "nc" not in _CACHE or _CACHE.get("key") != (_nslot, _do_sample):
        nc, es = build_program(_nslot, _do_sample)
        _CACHE["nc"], _CACHE["es"], _CACHE["key"] = nc, es, (_nslot, _do_sample)
    nc = _CACHE["nc"]
    wperm = np.ascontiguousarray(np.asarray(w_in, f)[0][:, _perm_w_in()])
    wo = np.ascontiguousarray(np.asarray(w_out, f)[0])
    ng = np.ascontiguousarray(np.asarray(norm_g, f)[0].reshape(8, 128).T)
    qg = np.asarray(q_gain, f)[0]
    kg = np.asarray(k_gain, f)[0]
    gains = np.concatenate([np.tile(qg, 8)] + [np.tile(kg[br], 2) for br in range(3)])[None, :].astype(f)
    cw = np.asarray(conv_w, f)[0]
    cb = np.asarray(conv_b, f)[0]
    convp = np.zeros((128, 16), f)
    for ct in range(4):
        for kk in range(3):
            convp[:, ct * 3 + kk] = cw[kk, ct * 128:(ct + 1) * 128]
        convp[:, 12 + ct] = cb[ct * 128:(ct + 1) * 128]
    cmpw = np.ascontiguousarray(np.asarray(cmp_w, f)[0].transpose(2, 0, 1, 3).reshape(64, 32 * 2 * 64))
    cmppe = np.ascontiguousarray(np.asarray(cmp_pe, f)[0].transpose(2, 0, 1).reshape(64, 64))
    inv = (10000.0 ** (-np.arange(32, dtype=np.float32) / 32)).astype(f)
    ccmp = np.asarray(cache_cmp_kv, f).reshape(NPOOL * 16, 2048)
    cslc = np.asarray(cache_slc_kv, f).reshape(NPOOL * 16, 2048)
    xs_all = np.asarray(x_sample, f).reshape(32, 1024)
    swin_all = np.asarray(state_win_kv, f).reshape(32, 512, 256)
    sconv_all = np.asarray(state_conv, f).reshape(32, 2, 512)
    pt_all = np.asarray(page_table).astype(np.int32)
    angs = (np.float32(PAST) * inv)[None, :]
    ropes = np.repeat(np.concatenate([np.cos(angs), np.sin(angs)], axis=1).astype(f), 4, axis=0)
    iota16 = np.tile(np.arange(16, dtype=np.float32)[None, :], (128, 1))
    cs_s = _consts_s()
    in_maps = []
    for c in range(8):
        b, p = c // 2, c % 2
        order = np.array([_tile_of_slot(mm, p) for mm in range(NTILE)])
        xpb = np.ascontiguousarray(x_prompt[b].reshape(NTILE, 128, 1024)[order].reshape(NTILE * 128, 1024))
        pos = (order[:, None] * 128 + np.arange(128)[None, :]).reshape(-1).astype(f)
        ang = pos[:, None] * inv[None, :]
        ropet = np.concatenate([np.cos(ang), np.sin(ang)], axis=1).astype(f)
        c1, c2 = _consts(p)
        sl = slice(4 * c, 4 * c + 4)
        in_maps.append(dict(xp=xpb, rope=ropet, w_in_d=wperm, w_out_d=wo, norm_g_d=ng, gains_in=gains, convp=convp,
                            cmpw=cmpw, cmppe=cmppe, consts=c1, consts2=c2,
                            xs=np.ascontiguousarray(xs_all[sl]), ropes=ropes,
                            ptab=np.ascontiguousarray(pt_all[sl].T), iota16=iota16, ccmp=ccmp, cslc=cslc,
                            swin=np.ascontiguousarray(swin_all[sl]), sconv=np.ascontiguousarray(sconv_all[sl]),
                            convw_row=np.ascontiguousarray(cw.reshape(1, 1536)), convb_row=np.ascontiguousarray(cb.reshape(1, 512)),
                            consts_s=cs_s))
    if not _do_sample:
        for mp in in_maps:
            del mp["ccmp"], mp["cslc"]
    res = run_bass_kernel_spmd(nc, in_maps, core_ids=list(range(8)))
    R = res.results
    B, T = 4, 8192
    y = np.zeros((B, T, 1024), f)
    pc = np.zeros((1, B, T, 2, 2, 64), f)
    psl = np.zeros((1, B, T, 2, 2, 64), f)
    pw = np.zeros((1, B, 512, 2, 2, 64), f)
    pcv = np.zeros((1, B, 2, 512), f)
    ys = np.zeros((32, 1, 1024), f)
    sc_ = np.zeros((1, 32, 1, 2, 2, 64), f)
    ss_ = np.zeros((1, 32, 1, 2, 2, 64), f)
    sw_ = np.zeros((1, 32, 512, 2, 2, 64), f)
    scv = np.zeros((1, 32, 2, 512), f)
    for c in range(8):
        b, p = c // 2, c % 2
        r = R[c]
        for k in range(NSLOT):
            i = 2 * k + p
            y[b, i * 128:(i + 1) * 128] = r["y_o"][k * 128:(k + 1) * 128]
            pc[0, b, i * 128:(i + 1) * 128] = r["cmp_o"][k * 128:(k + 1) * 128].reshape(128, 2, 2, 64)
            psl[0, b, i * 128:(i + 1) * 128] = r["slc_o"][k * 128:(k + 1) * 128].reshape(128, 2, 2, 64)
            if k >= 30:
                pw[0, b, (i - 60) * 128:(i - 59) * 128] = r["win_o"][(k - 30) * 128:(k - 29) * 128].reshape(128, 2, 2, 64)
        if p == 1:
            pcv[0, b] = r["conv_o"]
        sl = slice(4 * c, 4 * c + 4)
        ys[sl, 0] = r["ys_o"]
        sc_[0, sl, 0] = r["scmp_o"].reshape(4, 2, 2, 64)
        ss_[0, sl, 0] = r["sslc_o"].reshape(4, 2, 2, 64)
        sw_[0, sl] = r["swin_o"].reshape(4, 512, 2, 2, 64)
        scv[0, sl] = r["sconv_o"]
    return y, ys, pc, psl, pw, pcv, sc_, ss_, sw_, scv
```

```python
import numpy as np
from contextlib import ExitStack
import concourse.bass as bass
import concourse.mybir as mybir
from concourse.bass_utils import run_bass_kernel_spmd

F32 = mybir.dt.float32
BF16 = mybir.dt.bfloat16
I32 = mybir.dt.int32
U32 = mybir.dt.uint32
ALU = mybir.AluOpType
AF = mybir.ActivationFunctionType
AX = mybir.AxisListType

NEG = -30000.0
BIGNEG = -1.0e30
EPS = 1e-6
NTILE = 64
NSLOT = 32
IN_W = 3864
TMW = 1304
FM0 = 1304
PAST = 16384
NPOOL = 5120

C_WP, C_B0, C_VIS, C_AB, C_END = 0, 384, 512, 520, 528


class StopBuild(Exception):
    pass


class Buf:
    __slots__ = ("w", "r", "dsem", "dkey", "dcnt", "name")

    def __init__(self, name=""):
        self.w = None
        self.r = {}
        self.dsem = None
        self.dkey = None
        self.dcnt = 0
        self.name = name


class TT(Buf):
    __slots__ = ("t",)

    def __init__(self, t, name=""):
        Buf.__init__(self, name)
        self.t = t


class Fw:
    def __init__(self, nc, es):
        self.nc, self.es = nc, es
        self.eng = dict(pe=nc.tensor, act=nc.scalar, dve=nc.vector, pool=nc.gpsimd, sp=nc.sync)
        self.semh = {}
        for e in self.eng:
            self.semh[e] = es.enter_context(nc.semaphore("s_" + e))
        self.cnt = {e: 0 for e in self.eng}
        self.vc = {e: {} for e in self.eng}
        self.snaps = {}
        self.nd = 0
        self.out_events = []
        self.ninst = 0

    def sb(self, name, shape, dt):
        return TT(self.es.enter_context(self.nc.sbuf_tensor(name, shape, dt)), name)

    def ps(self, name, shape, dt):
        return TT(self.es.enter_context(self.nc.psum_tensor(name, shape, dt)), name)

    def _wait(self, e, ev):
        key, val = ev
        vc = self.vc[e]
        if vc.get(key, 0) >= val:
            return
        if key == e and e in ("pe", "sp"):
            return
        self.eng[e].wait_ge(self.semh[key], val)
        if vc.get(key, 0) < val:
            vc[key] = val

    def _deps(self, e, reads, writes):
        deps = []
        for b in reads:
            if b.w is not None:
                deps.append(b.w)
        for b in writes:
            if b.w is not None:
                deps.append(b.w)
            deps.extend(b.r.items())
        deps.sort(key=lambda ev: -ev[1])
        for ev in deps:
            self._wait(e, ev)

    def _mark(self, ev, reads, writes):
        for b in reads:
            if b.r.get(ev[0], 0) < ev[1]:
                b.r[ev[0]] = ev[1]
        for b in writes:
            b.w = ev
            b.r = {}

    def op(self, e, fn, reads=(), writes=()):
        self._deps(e, reads, writes)
        inst = fn(self.eng[e])
        self.cnt[e] += 1
        n = self.cnt[e]
        inst.then_inc(self.semh[e], 1)
        ev = (e, n)
        s = dict(self.vc[e])
        s[e] = n
        self.snaps[ev] = s
        self._mark(ev, reads, writes)
        self.ninst += 1
        return ev

    def dma(self, q, out, in_, reads=(), writes=(), is_output=False, indirect=None, **kw):
        self._deps(q, reads, writes)
        owner = writes[0] if writes else reads[0]
        if owner.dsem is None:
            owner.dkey = "d%d" % self.nd
            self.nd += 1
            owner.dsem = self.es.enter_context(self.nc.semaphore(owner.dkey))
            self.semh[owner.dkey] = owner.dsem
        if indirect is not None:
            inst = self.eng[q].indirect_dma_start(out=out, out_offset=None, in_=in_, in_offset=indirect, **kw)
        else:
            inst = self.eng[q].dma_start(out=out, in_=in_, **kw)
        inst.then_inc(owner.dsem, 16)
        owner.dcnt += 16
        ev = (owner.dkey, owner.dcnt)
        self.snaps[ev] = dict(self.vc[q])
        self._mark(ev, reads, writes)
        if is_output:
            self.out_events.append(ev)
        self.ninst += 1
        return ev

    def finish(self):
        last = {}
        for k, v in self.out_events:
            last[k] = max(last.get(k, 0), v)
        for k, v in last.items():
            self._wait("sp", (k, v))
        for e in ("pe", "act", "dve", "pool"):
            if self.cnt[e]:
                self._wait("sp", (e, self.cnt[e]))

    def mm(self, out, lhsT, rhs, start, stop, reads, writes, **kw):
        return self.op("pe", lambda e: e.matmul(out, lhsT=lhsT, rhs=rhs, start=start, stop=stop, **kw),
                       reads, writes)

    def tr(self, out, in_, ident, reads, writes):
        return self.op("pe", lambda e: e.transpose(out=out, in_=in_, identity=ident), reads, writes)

    def act(self, out, in_, func, reads, writes, **kw):
        return self.op("act", lambda e: e.activation(out=out, in_=in_, func=func, **kw), reads, writes)

    def tt(self, eng, out, in0, in1, op, reads, writes):
        return self.op(eng, lambda e: e.tensor_tensor(out=out, in0=in0, in1=in1, op=op), reads, writes)

    def ts(self, eng, out, in0, s1, s2, op0, op1, reads, writes):
        if op1 is None:
            return self.op(eng, lambda e: e.tensor_scalar(out=out, in0=in0, scalar1=s1, scalar2=None, op0=op0),
                           reads, writes)
        return self.op(eng, lambda e: e.tensor_scalar(out=out, in0=in0, scalar1=s1, scalar2=s2, op0=op0, op1=op1),
                       reads, writes)

    def stt(self, out, in0, scalar, in1, op0, op1, reads, writes):
        return self.op("dve", lambda e: e.scalar_tensor_tensor(out=out, in0=in0, scalar=scalar, in1=in1,
                                                                op0=op0, op1=op1), reads, writes)

    def cp(self, eng, out, in_, reads, writes):
        if eng == "act":
            return self.act(out, in_, AF.Copy, reads, writes)
        return self.op(eng, lambda e: e.tensor_copy(out=out, in_=in_), reads, writes)

    def memset(self, eng, ap, val, writes):
        return self.op(eng, lambda e: e.memset(ap, val), (), writes)


def build_program(nslot=NSLOT, do_sample=True):
    nc = bass.Bass("TRN2", target_bir_lowering=False)
    es = ExitStack()
    fw = Fw(nc, es)

    def din(name, shape, dt=F32):
        return nc.dram_tensor(name, shape, dt, kind="ExternalInput").ap()

    def dout(name, shape, dt=F32):
        return nc.dram_tensor(name, shape, dt, kind="ExternalOutput").ap()

    xp = din("xp", [NTILE * 128, 1024])
    rope = din("rope", [NTILE * 128, 64])
    w_in = din("w_in_d", [1024, IN_W])
    w_out = din("w_out_d", [1024, 1024])
    norm_g = din("norm_g_d", [128, 8])
    gains_d = din("gains_in", [1, 896])
    convp = din("convp", [128, 16])
    cmpw = din("cmpw", [64, 32 * 2 * 64])
    cmppe = din("cmppe", [64, 64])
    consts_d = din("consts", [128, C_END])
    consts2_d = din("consts2", [128, 640])

    y_o = dout("y_o", [NSLOT * 128, 1024])
    cmp_o = dout("cmp_o", [NSLOT * 128, 256])
    slc_o = dout("slc_o", [NSLOT * 128, 256])
    win_o = dout("win_o", [256, 256])
    conv_o = dout("conv_o", [2, 512])

    xs_d = din("xs", [4, 1024])
    ropes_d = din("ropes", [4, 64])
    ptab_d = din("ptab", [128, 4], I32)
    iota_d = din("iota16", [128, 16], F32)
    ccmp_d = din("ccmp", [NPOOL * 16, 2048]) if do_sample else None
    cslc_d = din("cslc", [NPOOL * 16, 2048]) if do_sample else None
    swin_d = din("swin", [4, 512, 256])
    sconv_d = din("sconv", [4, 2, 512])
    convw_d = din("convw_row", [1, 3 * 512])
    convb_d = din("convb_row", [1, 512])
    cs_d = din("consts_s", [128, 272])
    ys_o = dout("ys_o", [4, 1024])
    scmp_o = dout("scmp_o", [4, 256])
    sslc_o = dout("sslc_o", [4, 256])
    swin_o = dout("swin_o", [4, 512, 256])
    sconv_o = dout("sconv_o", [4, 2, 512])

    winb = fw.sb("winb", [128, 8, IN_W], BF16)
    woutb = fw.sb("woutb", [128, 8, 1024], BF16)
    wbd = fw.sb("wbd", [128, 32, 2, 128], BF16)
    cst = fw.sb("cst", [128, C_END], F32)
    cbf = fw.sb("cbf", [128, 4, 128], BF16)
    identb = fw.sb("identb", [128, 128], BF16)
    zerob = fw.sb("zerob", [128, 512], BF16)
    gains = fw.sb("gains", [128, 896], F32)
    cvp = fw.sb("cvp", [128, 16], F32)
    ngt = fw.sb("ngt", [128, 8], F32)
    biasK = fw.sb("biasK", [128, 1], F32)
    biasVb = fw.sb("biasVb", [128, 128], F32)

    P = [fw.ps("psP%d" % i, [128, 512], F32) for i in range(2)]
    Sps = es.enter_context(nc.psum_tensor("psS", [128, 1024], F32))
    S = [TT(Sps, "S0"), TT(Sps, "S1")]
    OA = fw.ps("psOA", [128, 512], F32)
    OB = fw.ps("psOB", [128, 512], F32)
    Tb = [fw.ps("psT%d" % i, [128, 1024], BF16) for i in range(2)]
    rr = {"P": 0, "T": 0, "S": 0}

    def nxt(kind):
        lst = {"P": P, "T": Tb, "S": S}[kind]
        b = lst[rr[kind] % len(lst)]
        rr[kind] += 1
        return b

    def Sap(b):
        return Sps[:, 0:512] if b is S[0] else Sps[:, 512:1024]

    def barrier(extra=()):
        for e in ("dve", "pool", "act", "pe", "sp"):
            for e2 in ("dve", "pool", "act", "pe"):
                if fw.cnt[e2]:
                    fw._wait(e, (e2, fw.cnt[e2]))
            for b in extra:
                if b.w is not None:
                    fw._wait(e, b.w)
                for ev in b.r.items():
                    fw._wait(e, ev)

    fw.dma("sp", cst.t[:], consts_d[:, :], writes=[cst])
    fw.dma("sp", ngt.t[:], norm_g[:, :], writes=[ngt])
    fw.dma("sp", cvp.t[:], convp[:, :], writes=[cvp])
    fw.dma("sp", gains.t[:], gains_d[0:1, :].to_broadcast([128, 896]), writes=[gains])
    fw.ts("dve", gains.t[:, 0:512], gains.t[:, 0:512], 0.125, None, ALU.mult, None, [gains], [gains])
    fw.memset("pool", zerob.t[:], 0.0, [zerob])
    fw.memset("pool", wbd.t[:], 0.0, [wbd])

    with ExitStack() as es0:
        def sb0(name, shape, dt):
            return TT(es0.enter_context(nc.sbuf_tensor(name, shape, dt)), name)
        stg = [sb0("stg%d" % i, [128, 1932], F32) for i in range(2)]
        cwst = sb0("cwst", [128, 32 * 2 * 64], F32)
        pest = sb0("pest", [128, 64], F32)
        peb = sb0("peb", [128, 64], BF16)
        perep = [sb0("perep%d" % i, [128, 128], BF16) for i in range(2)]
        cst2 = sb0("cst2", [128, 640], F32)
        fw.dma("sp", cst2.t[:], consts2_d[:, :], writes=[cst2])
        fw.cp("dve", cbf.t[:].rearrange("p a b -> p (a b)"), cst2.t[:, 0:512], [cst2], [cbf])
        fw.cp("dve", identb.t[:], cst2.t[:, 512:640], [cst2], [identb])
        it = 0
        for kc in range(8):
            for hf in range(2):
                sg = stg[it % 2]
                it += 1
                fw.dma("sp", sg.t[:], w_in[kc * 128:(kc + 1) * 128, hf * 1932:(hf + 1) * 1932], writes=[sg])
                fw.ts("dve" if hf == 0 else "pool", winb.t[:, kc, hf * 1932:(hf + 1) * 1932], sg.t[:],
                      ngt.t[:, kc:kc + 1], None, ALU.mult, None, [sg, ngt], [winb])
        for kc in range(8):
            sg = stg[it % 2]
            it += 1
            fw.dma("sp", sg.t[:, 0:1024], w_out[kc * 128:(kc + 1) * 128, :], writes=[sg])
            fw.cp("dve" if kc % 2 == 0 else "pool", woutb.t[:, kc, :], sg.t[:, 0:1024], [sg], [woutb])
        fw.dma("sp", cwst.t[0:64, :], cmpw[:, :], writes=[cwst])
        fw.dma("sp", cwst.t[64:128, :], cmpw[:, :], writes=[cwst])
        fw.dma("sp", pest.t[0:64, :], cmppe[:, :], writes=[pest])
        fw.dma("sp", pest.t[64:128, :], cmppe[:, :], writes=[pest])
        cw4 = cwst.t[:].rearrange("p (r j e) -> p r j e", r=32, j=2)
        fw.cp("dve", wbd.t[0:64, :, :, 0:64], cw4[0:64], [cwst], [wbd])
        fw.cp("dve", wbd.t[64:128, :, :, 64:128], cw4[64:128], [cwst], [wbd])
        fw.cp("dve", peb.t[:], pest.t[:], [pest], [peb])
        pk = nxt("P")
        for r in range(32):
            fw.mm(pk.t[:, 0:2], wbd.t[:, r, 0, :], peb.t[:, 2 * r:2 * r + 2], r == 0, r == 31, [wbd, peb], [pk])
        fw.cp("act", biasK.t[:], pk.t[:, 0:1], [pk], [biasK])
        pv = nxt("P")
        for r in range(32):
            pr = perep[r % 2]
            fw.cp("dve", pr.t[:], peb.t[:, 2 * r + 1:2 * r + 2].to_broadcast([128, 128]), [peb], [pr])
            fw.mm(pv.t[:, 0:128], pr.t[:], wbd.t[:, r, 1, :], r == 0, r == 31, [pr, wbd], [pv])
        fw.cp("act", biasVb.t[:], pv.t[:, 0:128], [pv], [biasVb])
        barrier(stg + [cwst, pest, cst2])

    ident = identb.t[:]
    ident4 = identb.t[:].unsqueeze(1).to_broadcast([128, 4, 128])

    def rope_apply(eng2, src3, dst3, cosb, sinb, tcs, tsn, H, reads, wbufs, np_=128):
        def h2(ap):
            return ap.rearrange("p h (t d) -> p (h t) d", d=32)
        t3 = tcs.t[0:np_, 0:H * 64].rearrange("p (h d) -> p h d", d=64)
        s3 = tsn.t[0:np_, 0:H * 64].rearrange("p (h d) -> p h d", d=64)
        fw.tt("dve", h2(t3), h2(src3), cosb, ALU.mult, reads, [tcs])
        fw.tt(eng2, h2(s3), h2(src3), sinb, ALU.mult, reads, [tsn])
        fw.tt("dve", dst3[:, :, 0:32], t3[:, :, 0:32], s3[:, :, 32:64], ALU.subtract, [tcs, tsn], wbufs)
        fw.tt(eng2, dst3[:, :, 32:64], t3[:, :, 32:64], s3[:, :, 0:32], ALU.add, [tcs, tsn], wbufs)

    def qk_norm(eng2, tmb, r0, r1, ssh, lnh, rsh, tsn, np_, epsap):
        h0, h1 = r0 // 64, r1 // 64
        H = h1 - h0
        v = tmb.t[0:np_, r0:r1].rearrange("p (h d) -> p h d", d=64)
        sqv = tsn.t[0:np_, 0:r1 - r0]
        fw.tt(eng2, sqv, tmb.t[0:np_, r0:r1], tmb.t[0:np_, r0:r1], ALU.mult, [tmb], [tsn])
        fw.op("dve", lambda e: e.tensor_reduce(out=ssh.t[0:np_, h0:h1], in_=sqv.rearrange("p (h d) -> p h d", d=64),
                                               axis=AX.X, op=ALU.add), [tsn], [ssh])
        fw.act(lnh.t[0:np_, h0:h1], ssh.t[0:np_, h0:h1], AF.Ln, [ssh], [lnh], scale=1.0 / 64, bias=epsap)
        fw.act(rsh.t[0:np_, h0:h1], lnh.t[0:np_, h0:h1], AF.Exp, [lnh], [rsh], scale=-0.5)
        fw.tt("dve", v, v, rsh.t[0:np_, h0:h1].unsqueeze(2).to_broadcast([np_, H, 64]), ALU.mult, [tmb, rsh], [tmb])
        fw.tt(eng2, tmb.t[0:np_, r0:r1], tmb.t[0:np_, r0:r1], gains.t[0:np_, r0:r1], ALU.mult, [tmb, gains], [tmb])

    def sample_path():
        ess = ExitStack()

        def sbs(name, shape, dt):
            return TT(ess.enter_context(nc.sbuf_tensor(name, shape, dt)), name)
        cs = sbs("cs", [128, 272], F32)
        xs = sbs("xs_t", [4, 1024], F32)
        rps = sbs("rps", [4, 64], F32)
        ptab = sbs("ptab_t", [128, 4], I32)
        iot = sbs("iot", [128, 16], F32)
        idxf = sbs("idxf", [128, 4, 16], F32)
        idxa = sbs("idxa", [128, 4, 16], I32)
        hbs = sbs("hbs", [4, 1024], BF16)
        s1 = sbs("s1", [4, 1], F32)
        s2 = sbs("s2", [4, 1], F32)
        s3 = sbs("s3", [4, 1], F32)
        hTs = sbs("hTs", [128, 8, 4], BF16)
        tms = sbs("tms", [4, IN_W], F32)
        ssh = sbs("ssh_s", [4, 14], F32)
        lnh = sbs("lnh_s", [4, 14], F32)
        rsh = sbs("rsh_s", [4, 14], F32)
        tcs = sbs("tcs_s", [4, 896], F32)
        tsn = sbs("tsn_s", [4, 896], F32)
        qsf = sbs("qsf", [4, 512], F32)
        qsb = sbs("qsb", [4, 512], BF16)
        kvs = sbs("kvs", [4, 2, 384], F32)
        kvsb = sbs("kvsb", [4, 2, 384], BF16)
        gs = sbs("gs", [4, 24], F32)
        cwr = sbs("cwr", [4, 3, 512], F32)
        cbr = sbs("cbr", [4, 512], F32)
        scv = sbs("scv", [4, 2, 512], F32)
        us = sbs("us", [4, 512], F32)
        cy = sbs("cy", [4, 512], F32)
        ezs = sbs("ezs", [4, 512], F32)
        mixs = sbs("mixs", [4, 1024], F32)
        mixb = sbs("mixb", [4, 1024], BF16)
        mixTs = sbs("mixTs", [128, 8, 4], BF16)
        QTzs = sbs("QTzs", [128, 4, 2, 4], BF16)
        prod = sbs("prod", [4, 4, 128], F32)
        snew = sbs("snew", [4, 2, 8], F32)
        pnm = sbs("pnm", [4, 2, 2, 4], F32)
        pnmb = sbs("pnmb", [4, 2, 2, 4], BF16)
        stg = [sbs("sstg%d" % i, [128, 8, 256], BF16) for i in range(2)]
        kcTs = sbs("kcTs", [128, 8, 128], BF16)
        vcTs = sbs("vcTs", [128, 8, 128], BF16)
        KcmpTs = sbs("KcmpTs", [128, 512], BF16)
        Vcmps = sbs("Vcmps", [128, 4, 128], BF16)
        pns = sbs("pns", [4, 2, 512], F32)
        pnsb = sbs("pnsb", [4, 2, 512], BF16)
        mxs = sbs("mxs", [4, 2], F32)
        sms = sbs("sms", [4, 2], F32)
        impf = sbs("impf", [1, 512], F32)
        scs = sbs("scs", [1, 256], F32)
        wks = sbs("wks", [1, 256], F32)
        m8s = sbs("m8s", [1, 16], F32)
        nms = sbs("nms", [1, 256], BF16)
        nmcol = sbs("nmcol", [128, 2, 2], F32)
        pnT = sbs("pnT", [128, 2, 4, 4], BF16)
        PTs = sbs("PTs", [128, 8, 2, 4], BF16)
        accs = sbs("accs", [128, 8], F32)
        acct = sbs("acct", [128, 8], F32)
        swt = sbs("swt", [128, 4, 256], F32)
        swb = sbs("swb", [128, 4, 256], BF16)
        stw = sbs("stw", [128, 4, 8], F32)
        Oall = sbs("Oall", [4, 3, 2, 65], F32)
        Ocat = sbs("Ocat", [4, 4, 3, 2, 65], F32)
        fcs = sbs("fcs", [4, 3, 4], F32)
        o1 = sbs("o1", [4, 4, 64], F32)
        o2 = sbs("o2", [4, 4, 64], F32)
        osf = sbs("osf", [4, 512], F32)
        ysb = xs
        dd = Buf("dramcopy")
        epsap = cst.t[0:4, C_AB + 2:C_AB + 3]
        onesf = cs.t[:, 268:269]

        fw.dma("sp", cs.t[:], cs_d[:, :], writes=[cs])
        fw.dma("sp", xs.t[:], xs_d[:, :], writes=[xs])
        fw.dma("sp", rps.t[:], ropes_d[:, :], writes=[rps])
        fw.dma("sp", ptab.t[:], ptab_d[:, :], writes=[ptab])
        fw.dma("sp", iot.t[:], iota_d[:, :], writes=[iot])
        fw.dma("sp", cwr.t[:].rearrange("p a b -> p (a b)"), convw_d[0:1, :].to_broadcast([4, 1536]), writes=[cwr])
        fw.dma("sp", cbr.t[:], convb_d[0:1, :].to_broadcast([4, 512]), writes=[cbr])
        fw.dma("sp", scv.t[:], sconv_d[:, :, :], writes=[scv])
        fw.dma("sp", swin_o[:, 0:511, :], swin_d[:, 1:512, :], reads=[dd], is_output=True)
        for s in range(4):
            fw.ts("dve", idxf.t[:, s, :], ptab.t[:, s:s + 1].to_broadcast([128, 16]), 16.0, None, ALU.mult, None, [ptab], [idxf])
        fw.tt("dve", idxf.t[:], idxf.t[:], iot.t[:].unsqueeze(1).to_broadcast([128, 4, 16]), ALU.add, [idxf, iot], [idxf])
        fw.cp("dve", idxa.t[:], idxf.t[:], [idxf], [idxa])
        fw.act(hbs.t[:], xs.t[:], AF.Square, [xs], [hbs, s1], accum_out=s1.t[:])
        fw.act(s2.t[:], s1.t[:], AF.Ln, [s1], [s2], scale=1.0 / 1024, bias=epsap)
        fw.act(s3.t[:], s2.t[:], AF.Exp, [s2], [s3], scale=-0.5)
        fw.ts("dve", hbs.t[:], xs.t[:], s3.t[:, 0:1], None, ALU.mult, None, [xs, s3], [hbs])
        tb = nxt("T")
        for kc in range(8):
            fw.tr(tb.t[:, kc * 4:kc * 4 + 4], hbs.t[:, kc * 128:(kc + 1) * 128], identb.t[0:4, 0:4], [hbs, identb], [tb])
        fw.cp("act", hTs.t[:].rearrange("p a b -> p (a b)"), tb.t[:, 0:32], [tb], [hTs])
        c0 = 0
        while c0 < IN_W:
            c1 = min(c0 + 512, IN_W)
            pb = nxt("P")
            for kc in range(8):
                fw.mm(pb.t[0:4, 0:c1 - c0], hTs.t[:, kc, :], winb.t[:, kc, c0:c1], kc == 0, kc == 7, [hTs, winb], [pb])
            fw.cp("act", tms.t[:, c0:c1], pb.t[0:4, 0:c1 - c0], [pb], [tms])
            c0 = c1
        qk_norm("dve", tms, 0, 896, ssh, lnh, rsh, tsn, 4, epsap)
        cosb = rps.t[:, 0:32].unsqueeze(1).to_broadcast([4, 28, 32])
        sinb = rps.t[:, 32:64].unsqueeze(1).to_broadcast([4, 28, 32])
        src = tms.t[:, 0:896].rearrange("p (h d) -> p h d", d=64)
        t3 = tcs.t[:, :].rearrange("p (h d) -> p h d", d=64)
        s3v = tsn.t[:, :].rearrange("p (h d) -> p h d", d=64)

        def h2(ap):
            return ap.rearrange("p h (t d) -> p (h t) d", d=32)
        fw.tt("dve", h2(t3), h2(src), cosb, ALU.mult, [tms, rps], [tcs])
        fw.tt("dve", h2(s3v), h2(src), sinb, ALU.mult, [tms, rps], [tsn])
        qv = qsf.t[:].rearrange("p (h d) -> p h d", d=64)
        kv_ = kvs.t[:, 0, :].rearrange("p (h d) -> p h d", d=64)
        fw.tt("dve", qv[:, :, 0:32], t3[:, 0:8, 0:32], s3v[:, 0:8, 32:64], ALU.subtract, [tcs, tsn], [qsf])
        fw.tt("dve", qv[:, :, 32:64], t3[:, 0:8, 32:64], s3v[:, 0:8, 0:32], ALU.add, [tcs, tsn], [qsf])
        fw.tt("dve", kv_[:, :, 0:32], t3[:, 8:14, 0:32], s3v[:, 8:14, 32:64], ALU.subtract, [tcs, tsn], [kvs])
        fw.tt("dve", kv_[:, :, 32:64], t3[:, 8:14, 32:64], s3v[:, 8:14, 0:32], ALU.add, [tcs, tsn], [kvs])
        fw.cp("dve", kvs.t[:, 1, :], tms.t[:, 896:1280], [tms], [kvs])
        fw.cp("dve", qsb.t[:], qsf.t[:], [qsf], [qsb])
        fw.cp("dve", kvsb.t[:], kvs.t[:], [kvs], [kvsb])
        kvo = kvs.t[:].rearrange("p j (b c) -> p j b c", c=128)
        fw.dma("sp", scmp_o.rearrange("s (j c) -> s j c", j=2), kvo[:, :, 0, :], reads=[kvs], is_output=True)
        fw.dma("sp", sslc_o.rearrange("s (j c) -> s j c", j=2), kvo[:, :, 1, :], reads=[kvs], is_output=True)
        fw.dma("sp", swin_o[:, 511, :].rearrange("s (j c) -> s j c", j=2), kvo[:, :, 2, :], reads=[kvs], is_output=True)
        fw.act(gs.t[:], tms.t[:, 1280:1304], AF.Exp, [tms], [gs], scale=-1.0)
        fw.ts("dve", gs.t[:], gs.t[:], 1.0, None, ALU.add, None, [gs], [gs])
        fw.op("dve", lambda e: e.reciprocal(out=gs.t[:], in_=gs.t[:]), [gs], [gs])
        f0 = FM0
        fw.tt("dve", us.t[:], tms.t[:, f0 + 512:f0 + 1024], tms.t[:, f0 + 1024:f0 + 1536], ALU.mult, [tms], [us])
        fw.dma("sp", sconv_o[:, 0, :], scv.t[:, 1, :], reads=[scv], is_output=True)
        fw.dma("sp", sconv_o[:, 1, :], us.t[:], reads=[us], is_output=True)
        fw.tt("dve", cy.t[:], scv.t[:, 0, :], cwr.t[:, 0, :], ALU.mult, [scv, cwr], [cy])
        fw.tt("dve", ezs.t[:], scv.t[:, 1, :], cwr.t[:, 1, :], ALU.mult, [scv, cwr], [ezs])
        fw.tt("dve", cy.t[:], cy.t[:], ezs.t[:], ALU.add, [cy, ezs], [cy])
        fw.tt("dve", ezs.t[:], us.t[:], cwr.t[:, 2, :], ALU.mult, [us, cwr], [ezs])
        fw.tt("dve", cy.t[:], cy.t[:], ezs.t[:], ALU.add, [cy, ezs], [cy])
        fw.tt("dve", cy.t[:], cy.t[:], cbr.t[:], ALU.add, [cy, cbr], [cy])
        fw.act(ezs.t[:], tms.t[:, f0 + 1536:f0 + 2048], AF.Exp, [tms], [ezs], scale=-1.0)
        fw.ts("dve", ezs.t[:], ezs.t[:], 1.0, None, ALU.add, None, [ezs], [ezs])
        fw.op("dve", lambda e: e.reciprocal(out=ezs.t[:], in_=ezs.t[:]), [ezs], [ezs])
        fw.tt("dve", ezs.t[:], ezs.t[:], tms.t[:, f0 + 1536:f0 + 2048], ALU.mult, [ezs, tms], [ezs])
        fw.tt("dve", ezs.t[:], ezs.t[:], tms.t[:, f0:f0 + 512], ALU.mult, [ezs, tms], [ezs])
        fw.tt("dve", mixs.t[:, 0:512], cy.t[:], ezs.t[:], ALU.mult, [cy, ezs], [mixs])
        fw.memset("pool", QTzs.t[:], 0.0, [QTzs])
        tb = nxt("T")
        for n in range(4):
            fw.tr(tb.t[:, n * 4:n * 4 + 4], qsb.t[:, n * 128:(n + 1) * 128], identb.t[0:4, 0:4], [qsb, identb], [tb])
        fw.cp("act", QTzs.t[0:64, :, 0, :], tb.t[0:64, 0:16].rearrange("p (n s) -> p s n", s=4), [tb], [QTzs])
        fw.cp("act", QTzs.t[64:128, :, 1, :], tb.t[64:128, 0:16].rearrange("p (n s) -> p s n", s=4), [tb], [QTzs])
        for bi, br in enumerate((1, 2)):
            fw.tt("dve", prod.t[:], qsf.t[:].rearrange("p (n c) -> p n c", c=128),
                  kvs.t[:, 0, br * 128:(br + 1) * 128].unsqueeze(1).to_broadcast([4, 4, 128]), ALU.mult, [qsf, kvs], [prod])
            fw.op("dve", lambda e: e.tensor_reduce(out=snew.t[:, bi, :], in_=prod.t[:].rearrange("p n (g d) -> p (n g) d", d=64),
                                                   axis=AX.X, op=ALU.add), [prod], [snew])
        fw.act(snew.t[:], snew.t[:], AF.Exp, [snew], [snew])

        for s in range(4):
            qrhs = QTzs.t[:, s, :, :].rearrange("p g n -> p (g n)")
            fw.ts("dve", pnm.t[:].rearrange("p b g n -> p b n g"), snew.t[:].rearrange("p b (n g) -> p b n g", g=2),
                  cs.t[0:4, 256 + s:257 + s], None, ALU.mult, None, [snew, cs], [pnm])
            fw.cp("dve", pnmb.t[:], pnm.t[:], [pnm], [pnmb])
            for ck in range(16):
                sg = stg[ck % 2]
                fw.dma("pool", sg.t[:].rearrange("p a b -> p (a b)"), ccmp_d[:, :], reads=[idxa], writes=[sg],
                       indirect=bass.IndirectOffsetOnAxis(ap=idxa.t[:, s, ck:ck + 1], axis=0))
                tk, tv = nxt("T"), nxt("T")
                for r8 in range(8):
                    fw.tr(tk.t[:, r8 * 128:(r8 + 1) * 128], sg.t[:, r8, 0:128], ident, [sg, identb], [tk])
                    fw.tr(tv.t[:, r8 * 128:(r8 + 1) * 128], sg.t[:, r8, 128:256], ident, [sg, identb], [tv])
                fw.cp("act", kcTs.t[:].rearrange("p a b -> p (a b)"), tk.t[:, :], [tk], [kcTs])
                fw.cp("act", vcTs.t[:].rearrange("p a b -> p (a b)"), tv.t[:, :], [tv], [vcTs])
                for r8 in range(8):
                    r = ck * 8 + r8
                    q4, r32 = r // 32, r % 32
                    fw.mm(OA.t[:, q4 * 128:(q4 + 1) * 128], wbd.t[:, r32, 0, :], kcTs.t[:, r8, :], r32 == 0, r32 == 31,
                          [wbd, kcTs], [OA])
                    fw.mm(OB.t[:, q4 * 128:(q4 + 1) * 128], vcTs.t[:, r8, :], wbd.t[:, r32, 1, :], r32 == 0, r32 == 31,
                          [wbd, vcTs], [OB])
            fw.act(KcmpTs.t[:], OA.t[:, :], AF.Identity, [OA, biasK], [KcmpTs], bias=biasK.t[:, 0:1])
            fw.tt("dve", Vcmps.t[:], OB.t[:, :].rearrange("p (q c) -> p q c", c=128),
                  biasVb.t[:].unsqueeze(1).to_broadcast([128, 4, 128]), ALU.add, [OB, biasVb], [Vcmps])
            for g in range(2):
                pb = nxt("P")
                fw.mm(pb.t[0:4, :], QTzs.t[:, s, g, :], KcmpTs.t[:, :], True, True, [QTzs, KcmpTs], [pb])
                fw.op("dve", lambda e: e.tensor_reduce(out=mxs.t[:, g:g + 1], in_=pb.t[0:4, :], axis=AX.X, op=ALU.max), [pb], [mxs])
                fw.ts("dve", mxs.t[:, g:g + 1], mxs.t[:, g:g + 1], -1.0, None, ALU.mult, None, [mxs], [mxs])
                fw.act(pns.t[:, g, :], pb.t[0:4, :], AF.Exp, [pb, mxs], [pns, sms], bias=mxs.t[:, g:g + 1],
                       accum_out=sms.t[:, g:g + 1])
                fw.op("dve", lambda e: e.reciprocal(out=sms.t[:, g:g + 1], in_=sms.t[:, g:g + 1]), [sms], [sms])
                fw.ts("dve", pns.t[:, g, :], pns.t[:, g, :], sms.t[:, g:g + 1], None, ALU.mult, None, [pns, sms], [pns])
                fw.cp("dve", pnsb.t[:, g, :], pns.t[:, g, :], [pns], [pnsb])
                pi = nxt("P")
                fw.mm(pi.t[0:1, :], cs.t[0:4, 268:269], pns.t[:, g, :], True, True, [cs, pns], [pi])
                fw.cp("act", impf.t[:], pi.t[0:1, :], [pi], [impf])
                iv = impf.t[:].rearrange("p (h two j) -> p h two j", two=2, j=128)
                fw.tt("dve", scs.t[:].rearrange("p (h j) -> p h j", j=128), iv[:, :, 0, :], iv[:, :, 1, :], ALU.add, [impf], [scs])
                fw.tt("dve", scs.t[:], scs.t[:], cs.t[0:1, 0:256], ALU.add, [scs, cs], [scs])
                fw.op("dve", lambda e: e.max(out=m8s.t[:, 0:8], in_=scs.t[:]), [scs], [m8s])
                fw.op("dve", lambda e: e.match_replace(out=wks.t[:], in_to_replace=m8s.t[:, 0:8], in_values=scs.t[:],
                                                       imm_value=-3.0e38), [scs, m8s], [wks])
                fw.op("dve", lambda e: e.max(out=m8s.t[:, 8:16], in_=wks.t[:]), [wks], [m8s])
                fw.ts("dve", nms.t[:], scs.t[:], m8s.t[:, 14:15], NEG, ALU.is_lt, ALU.mult, [scs, m8s], [nms])
                pc = nxt("P")
                for h in range(2):
                    fw.mm(pc.t[:, h:h + 1], nms.t[0:1, h * 128:(h + 1) * 128], identb.t[0:1, 0:1], True, True,
                          [nms, identb], [pc])
                fw.cp("act", nmcol.t[:, g, :], pc.t[:, 0:2], [pc], [nmcol])
                tb = nxt("T")
                for q4 in range(4):
                    fw.tr(tb.t[:, q4 * 4:q4 * 4 + 4], pnsb.t[:, g, q4 * 128:(q4 + 1) * 128], identb.t[0:4, 0:4],
                          [pnsb, identb], [tb])
                fw.cp("act", pnT.t[:, g, :, :].rearrange("p a b -> p (a b)"), tb.t[:, 0:16], [tb], [pnT])
                po = nxt("P")
                for q4 in range(4):
                    fw.mm(po.t[0:4, 0:64], pnT.t[:, g, q4, :], Vcmps.t[:, q4, g * 64:(g + 1) * 64], q4 == 0, q4 == 3,
                          [pnT, Vcmps], [po])
                fw.cp("act", Oall.t[:, 0, g, 0:64], po.t[0:4, 0:64], [po], [Oall])
            fw.mm(OA.t[0:4, 0:128], zerob.t[:, 0:4], zerob.t[:, 0:128], True, True, [zerob], [OA])
            fw.memset("pool", accs.t[:], 0.0, [accs])
            for ck in range(16):
                sg = stg[ck % 2]
                fw.dma("pool", sg.t[:].rearrange("p a b -> p (a b)"), cslc_d[:, :], reads=[idxa], writes=[sg],
                       indirect=bass.IndirectOffsetOnAxis(ap=idxa.t[:, s, ck:ck + 1], axis=0))
                tk = nxt("T")
                for r8 in range(8):
                    fw.tr(tk.t[:, r8 * 128:(r8 + 1) * 128], sg.t[:, r8, 0:128], ident, [sg, identb], [tk])
                fw.cp("act", kcTs.t[:].rearrange("p a b -> p (a b)"), tk.t[:, :], [tk], [kcTs])
                pb = nxt("P")
                for r8 in range(8):
                    fw.mm(pb.t[:, r8 * 8:r8 * 8 + 8], kcTs.t[:, r8, :], qrhs, True, True, [kcTs, QTzs], [pb])
                h = (ck * 8) // 64
                p4 = pb.t[:, 0:64].rearrange("p (r g n) -> p r g n", g=2, n=4)
                for g in range(2):
                    fw.act(PTs.t[:, :, g, :], p4[:, :, g, :], AF.Exp, [pb, nmcol], [PTs], bias=nmcol.t[:, g, h:h + 1])
                fw.op("dve", lambda e: e.tensor_reduce(out=acct.t[:], in_=PTs.t[:].rearrange("p r g n -> p (g n) r"),
                                                       axis=AX.X, op=ALU.add), [PTs], [acct])
                fw.tt("dve", accs.t[:], accs.t[:], acct.t[:], ALU.add, [accs, acct], [accs])
                for r8 in range(8):
                    for g in range(2):
                        fw.mm(OA.t[0:4, g * 64:(g + 1) * 64], PTs.t[:, r8, g, :], sg.t[:, r8, 128 + g * 64:192 + g * 64],
                              False, False, [PTs, sg], [OA], skip_group_check=True)
            ps_ = nxt("P")
            for g in range(2):
                fw.mm(OA.t[0:4, g * 64:(g + 1) * 64], pnmb.t[:, 0, g, :], kvsb.t[:, 1, 128 + g * 64:192 + g * 64],
                      False, False, [pnmb, kvsb], [OA], skip_group_check=True)
                fw.mm(ps_.t[0:4, g:g + 1], accs.t[:, g * 4:(g + 1) * 4], onesf, True, False, [accs, cs], [ps_])
                fw.mm(ps_.t[0:4, g:g + 1], pnm.t[:, 0, g, :], cs.t[0:4, 268:269], False, True, [pnm, cs], [ps_])
            fw.cp("act", Oall.t[:, 1, :, 0:64], OA.t[0:4, 0:128].rearrange("p (g d) -> p g d", d=64), [OA], [Oall])
            fw.cp("act", Oall.t[:, 1, :, 64], ps_.t[0:4, 0:2], [ps_], [Oall])
            fw.dma("sp", swt.t[:], swin_d[s].rearrange("(c p) f -> p c f", p=128), writes=[swt])
            fw.cp("dve", swb.t[:].rearrange("p a b -> p (a b)"), swt.t[:].rearrange("p a b -> p (a b)"), [swt], [swb])
            tk = nxt("T")
            for c in range(4):
                fw.tr(tk.t[:, c * 128:(c + 1) * 128], swb.t[:, c, 0:128], ident, [swb, identb], [tk])
            fw.cp("act", kcTs.t[:, 0:4, :].rearrange("p a b -> p (a b)"), tk.t[:, 0:512], [tk], [kcTs])
            pb = nxt("P")
            for c in range(4):
                fw.mm(pb.t[:, c * 8:c * 8 + 8], kcTs.t[:, c, :], qrhs, True, True, [kcTs, QTzs], [pb])
            fw.tt("dve", stw.t[:], pb.t[:, 0:32].rearrange("p (c e) -> p c e", e=8),
                  cs.t[:, 260:264].unsqueeze(2).to_broadcast([128, 4, 8]), ALU.add, [pb, cs], [stw])
            fw.act(PTs.t[:, 0:4, :, :].rearrange("p c g n -> p c (g n)"), stw.t[:], AF.Exp, [stw], [PTs])
            fw.op("dve", lambda e: e.tensor_reduce(out=acct.t[:], in_=PTs.t[:, 0:4, :, :].rearrange("p r g n -> p (g n) r"),
                                                   axis=AX.X, op=ALU.add), [PTs], [acct])
            fw.mm(OB.t[0:4, 0:128], zerob.t[:, 0:4], zerob.t[:, 0:128], True, True, [zerob], [OB])
            ps_ = nxt("P")
            for c in range(4):
                for g in range(2):
                    fw.mm(OB.t[0:4, g * 64:(g + 1) * 64], PTs.t[:, c, g, :], swb.t[:, c, 128 + g * 64:192 + g * 64],
                          False, False, [PTs, swb], [OB], skip_group_check=True)
            for g in range(2):
                fw.mm(OB.t[0:4, g * 64:(g + 1) * 64], pnmb.t[:, 1, g, :], kvsb.t[:, 1, 256 + g * 64:320 + g * 64],
                      False, False, [pnmb, kvsb], [OB], skip_group_check=True)
                fw.mm(ps_.t[0:4, g:g + 1], acct.t[:, g * 4:(g + 1) * 4], onesf, True, False, [acct, cs], [ps_])
                fw.mm(ps_.t[0:4, g:g + 1], pnm.t[:, 1, g, :], cs.t[0:4, 268:269], False, True, [pnm, cs], [ps_])
            fw.cp("act", Oall.t[:, 2, :, 0:64], OB.t[0:4, 0:128].rearrange("p (g d) -> p g d", d=64), [OB], [Oall])
            fw.cp("act", Oall.t[:, 2, :, 64], ps_.t[0:4, 0:2], [ps_], [Oall])
            for n in range(4):
                fw.dma("sp", Ocat.t[s:s + 1, n, :, :, :].rearrange("p a b c -> p (a b c)"),
                       Oall.t[n:n + 1, :, :, :].rearrange("p a b c -> p (a b c)"), reads=[Oall], writes=[Ocat])
        for g in range(2):
            gv = gs.t[:, g * 12:(g + 1) * 12].rearrange("p (n b) -> p n b", b=3)
            fw.ts("dve", fcs.t[:, 1, :], Ocat.t[:, :, 1, g, 64], 1e-30, None, ALU.max, None, [Ocat], [fcs])
            fw.ts("dve", fcs.t[:, 2, :], Ocat.t[:, :, 2, g, 64], 1e-30, None, ALU.max, None, [Ocat], [fcs])
            fw.op("dve", lambda e: e.reciprocal(out=fcs.t[:, 1:3, :], in_=fcs.t[:, 1:3, :]), [fcs], [fcs])
            fw.tt("dve", fcs.t[:, 1, :], fcs.t[:, 1, :], gv[:, :, 1], ALU.mult, [fcs, gs], [fcs])
            fw.tt("dve", fcs.t[:, 2, :], fcs.t[:, 2, :], gv[:, :, 2], ALU.mult, [fcs, gs], [fcs])
            fw.tt("dve", o1.t[:], Ocat.t[:, :, 0, g, 0:64], gv[:, :, 0].unsqueeze(2).to_broadcast([4, 4, 64]), ALU.mult,
                  [Ocat, gs], [o1])
            fw.tt("dve", o2.t[:], Ocat.t[:, :, 1, g, 0:64], fcs.t[:, 1, :].unsqueeze(2).to_broadcast([4, 4, 64]), ALU.mult,
                  [Ocat, fcs], [o2])
            fw.tt("dve", o1.t[:], o1.t[:], o2.t[:], ALU.add, [o1, o2], [o1])
            fw.tt("dve", o2.t[:], Ocat.t[:, :, 2, g, 0:64], fcs.t[:, 2, :].unsqueeze(2).to_broadcast([4, 4, 64]), ALU.mult,
                  [Ocat, fcs], [o2])
            fw.tt("dve", osf.t[:, g * 256:(g + 1) * 256].rearrange("p (n d) -> p n d", d=64), o1.t[:], o2.t[:], ALU.add,
                  [o1, o2], [osf])
        f0 = FM0 + 2048
        fw.act(ezs.t[:], tms.t[:, f0:f0 + 512], AF.Exp, [tms], [ezs], scale=-1.0)
        fw.ts("dve", ezs.t[:], ezs.t[:], 1.0, None, ALU.add, None, [ezs], [ezs])
        fw.op("dve", lambda e: e.reciprocal(out=ezs.t[:], in_=ezs.t[:]), [ezs], [ezs])
        fw.tt("dve", ezs.t[:], ezs.t[:], tms.t[:, f0:f0 + 512], ALU.mult, [ezs, tms], [ezs])
        fw.tt("dve", mixs.t[:, 512:1024], osf.t[:], ezs.t[:], ALU.mult, [osf, ezs], [mixs])
        fw.cp("dve", mixb.t[:], mixs.t[:], [mixs], [mixb])
        tb = nxt("T")
        for kc in range(8):
            fw.tr(tb.t[:, kc * 4:kc * 4 + 4], mixb.t[:, kc * 128:(kc + 1) * 128], identb.t[0:4, 0:4], [mixb, identb], [tb])
        fw.cp("act", mixTs.t[:].rearrange("p a b -> p (a b)"), tb.t[:, 0:32], [tb], [mixTs])
        for hf in range(2):
            pb = nxt("P")
            for kc in range(8):
                fw.mm(pb.t[0:4, :], mixTs.t[:, kc, :], woutb.t[:, kc, hf * 512:(hf + 1) * 512], kc == 0, kc == 7,
                      [mixTs, woutb], [pb])
            fw.tt("dve", xs.t[:, hf * 512:(hf + 1) * 512], pb.t[0:4, :], xs.t[:, hf * 512:(hf + 1) * 512], ALU.add,
                  [pb, xs], [xs])
        fw.dma("sp", ys_o[:, :], ysb.t[:], reads=[ysb], is_output=True)
        barrier([ysb, kvs, scv, us, swt, Oall, Ocat] + stg)
        ess.close()

    import os
    STAGE = int(os.environ.get("KSTAGE", "99"))
    if STAGE == 0:
        fw.finish()
        return nc, es

    def chk(n):
        if STAGE == n:
            raise StopBuild()
    if do_sample:
        sample_path()

    KTs = fw.sb("KTs", [128, NTILE, 128], BF16)
    KTs_b = [Buf("KTs%d" % i) for i in range(NTILE)]
    Vs = fw.sb("Vs", [128, NTILE, 2, 65], BF16)
    Vs_b = [Buf("Vs%d" % i) for i in range(NTILE)]
    KTw = fw.sb("KTw", [128, 8, 128], BF16)
    KTw_b = [Buf("KTw%d" % i) for i in range(8)]
    Vw = fw.sb("Vw", [128, 8, 2, 65], BF16)
    Vw_b = [Buf("Vw%d" % i) for i in range(8)]
    KcmpT = fw.sb("KcmpT", [128, 256], BF16)
    VcmpT = fw.sb("VcmpT", [128, 256], BF16)
    Vcmp_tm = fw.sb("Vcmp_tm", [128, 2, 128], BF16)
    score = [fw.sb("score%d" % g, [128, 128], F32) for g in range(2)]
    pn = fw.sb("pn", [128, 4, 256], F32)
    pnb = fw.sb("pnb", [128, 4, 256], BF16)
    xt = [fw.sb("xt%d" % i, [128, 1024], F32) for i in range(3)]
    rp = [fw.sb("rp%d" % i, [128, 64], F32) for i in range(3)]
    ss = fw.sb("ss", [128, 1], F32)
    lnv = fw.sb("lnv", [128, 1], F32)
    rstd = fw.sb("rstd", [128, 1], F32)
    hb = fw.sb("hb", [128, 1024], BF16)
    hT = [fw.sb("hT%d" % i, [128, 8, 128], BF16) for i in range(2)]
    tm = fw.sb("tm", [128, 920], F32)
    ssh = fw.sb("ssh", [128, 14], F32)
    lnh = fw.sb("lnh", [128, 14], F32)
    rsh = fw.sb("rsh", [128, 14], F32)
    tcs = fw.sb("tcs", [128, 896], F32)
    tsn = fw.sb("tsn", [128, 896], F32)
    kvout = fw.sb("kvout", [128, 2, 384], F32)
    kb = fw.sb("kb", [128, 4, 128], BF16)
    qb = fw.sb("qb", [128, 512], BF16)
    kcT = fw.sb("kcT", [128, 2, 2, 128], BF16)
    kcT_b = [Buf("kcT0"), Buf("kcT1")]
    QTz = fw.sb("QTz", [128, 2, 4, 128], BF16)
    gate = fw.sb("gate", [128, 24], F32)
    csb = fw.sb("csb", [128, 128], F32)
    uext = fw.sb("uext", [128, 4, 130], F32)
    halo = [fw.sb("halo%d" % i, [128, 4, 2], F32) for i in range(2)]
    halot = fw.sb("halot", [128, 4, 2], F32)
    hcs = fw.sb("hcs", [128, 16], F32)
    ez = fw.sb("ez", [128, 512], F32)
    g1 = fw.sb("g1", [128, 128], F32)
    acc = fw.sb("acc", [128, 128], F32)
    mixT = fw.sb("mixT", [128, 8, 128], BF16)
    szT = fw.sb("szT", [128, 4, 128], F32)
    mx = fw.sb("mx", [128, 4], F32)
    sm = fw.sb("sm", [128, 4], F32)
    a1 = fw.sb("a1", [128, 256], F32)
    a2 = fw.sb("a2", [128, 256], F32)
    m8 = fw.sb("m8", [128, 16], F32)
    wk = fw.sb("wk", [128, 128], F32)
    nmask = [fw.sb("nmask%d" % g, [128, 128], BF16) for g in range(2)]
    expb = [fw.sb("expb%d" % i, [128, 4, 128], BF16) for i in range(2)]
    PT = [fw.sb("PT%d" % i, [128, 4, 128], BF16) for i in range(2)]
    PTc = fw.sb("PTc", [128, 2, 4, 128], BF16)
    ocmp = fw.sb("ocmp", [128, 4, 64], F32)
    fac = fw.sb("fac", [128, 3, 4], F32)
    rs2 = fw.sb("rs2", [128, 2, 4], F32)
    ot1 = fw.sb("ot1", [128, 4, 64], F32)
    ot2 = fw.sb("ot2", [128, 4, 64], F32)
    oall = fw.sb("oall", [128, 8, 64], BF16)
    fw.memset("pool", Vs.t[:], 1.0, [Vs] + Vs_b)
    fw.memset("pool", Vw.t[:], 1.0, [Vw] + Vw_b)
    fw.memset("pool", KcmpT.t[:], 0.0, [KcmpT])
    fw.memset("pool", VcmpT.t[:], 0.0, [VcmpT])
    for g in range(2):
        fw.memset("pool", score[g].t[:], BIGNEG, [score[g]])
    fw.memset("pool", pn.t[:], 0.0, [pn])
    fw.memset("pool", pnb.t[:], 0.0, [pnb])
    fw.memset("pool", QTz.t[:], 0.0, [QTz])
    fw.memset("pool", halo[1].t[:], 0.0, [halo[1]])
    epsap = cst.t[:, C_AB + 2:C_AB + 3]

    def xbuf(m):
        k, own = m // 2, (m % 2 == 0)
        sq_ = 2 * k + (1 if own else 0)
        return xt[sq_ % 3], rp[sq_ % 3]

    def load_x(m):
        xb, rb = xbuf(m)
        fw.dma("sp", xb.t[:], xp[m * 128:(m + 1) * 128, :], writes=[xb])
        fw.dma("sp", rb.t[:], rope[m * 128:(m + 1) * 128, :], writes=[rb])

    def tile_front(m, own, k):
        xb, rb = xbuf(m)
        hTb = hT[m % 2]
        fw.act(hb.t[:], xb.t[:], AF.Square, [xb], [hb, ss], accum_out=ss.t[:])
        fw.act(lnv.t[:], ss.t[:], AF.Ln, [ss], [lnv], scale=1.0 / 1024, bias=epsap)
        fw.act(rstd.t[:], lnv.t[:], AF.Exp, [lnv], [rstd], scale=-0.5)
        fw.ts("dve", hb.t[:], xb.t[:], rstd.t[:, 0:1], None, ALU.mult, None, [xb, rstd], [hb])
        tb = nxt("T")
        for kc in range(8):
            fw.tr(tb.t[:, kc * 128:(kc + 1) * 128], hb.t[:, kc * 128:(kc + 1) * 128], ident, [hb, identb], [tb])
        fw.cp("act", hTb.t[:].rearrange("p a b -> p (a b)"), tb.t[:, :], [tb], [hTb])
        chk(10)
        pieces = [(0, 512), (512, 1024), (1024, 1304)] if own else [(512, 1024), (1024, 1280)]
        for (c0, c1) in pieces:
            pb = nxt("P")
            for kc in range(8):
                fw.mm(pb.t[:, 0:c1 - c0], hTb.t[:, kc, :], winb.t[:, kc, c0:c1], kc == 0, kc == 7, [hTb, winb], [pb])
            if c0 == 0:
                fw.cp("act", tm.t[:, 0:512], pb.t[:, 0:512], [pb], [tm])
            elif c0 == 512:
                fw.cp("act", tm.t[:, 512:896], pb.t[:, 0:384], [pb], [tm])
                fw.cp("act", kvout.t[:, 1, 0:128], pb.t[:, 384:512], [pb], [kvout])
            else:
                fw.cp("act", kvout.t[:, 1, 128:384], pb.t[:, 0:256], [pb], [kvout])
                if own:
                    fw.cp("act", tm.t[:, 896:920], pb.t[:, 256:280], [pb], [tm])
        r0, r1 = (0, 896) if own else (512, 896)
        chk(11)
        qk_norm("pool", tm, r0, r1, ssh, lnh, rsh, tsn, 128, epsap)
        chk(12)
        if own:
            cosb = rb.t[:, 0:32].unsqueeze(1).to_broadcast([128, 16, 32])
            sinb = rb.t[:, 32:64].unsqueeze(1).to_broadcast([128, 16, 32])
            rope_apply("pool", tm.t[:, 0:512].rearrange("p (h d) -> p h d", d=64),
                       qb.t[:].rearrange("p (h d) -> p h d", d=64), cosb, sinb, tcs, tsn, 8, [tm, rb], [qb])
        cosb = rb.t[:, 0:32].unsqueeze(1).to_broadcast([128, 12, 32])
        sinb = rb.t[:, 32:64].unsqueeze(1).to_broadcast([128, 12, 32])
        rope_apply("pool", tm.t[:, 512:896].rearrange("p (h d) -> p h d", d=64),
                   kvout.t[:, 0, :].rearrange("p (h d) -> p h d", d=64), cosb, sinb, tcs, tsn, 6, [tm, rb], [kvout])
        kvo = kvout.t[:].rearrange("p j (b c) -> p j b c", c=128)
        chk(13)
        if own:
            fw.dma("sp", cmp_o[k * 128:(k + 1) * 128, :].rearrange("t (j c) -> t j c", j=2), kvo[:, :, 0, :],
                   reads=[kvout], is_output=True)
            fw.dma("sp", slc_o[k * 128:(k + 1) * 128, :].rearrange("t (j c) -> t j c", j=2), kvo[:, :, 1, :],
                   reads=[kvout], is_output=True)
            if k >= 30:
                fw.dma("sp", win_o[(k - 30) * 128:(k - 29) * 128, :].rearrange("t (j c) -> t j c", j=2), kvo[:, :, 2, :],
                       reads=[kvout], is_output=True)
        chk(14)
        fw.cp("dve", kb.t[:, 0:2, :], kvo[:, :, 0, :], [kvout], [kb])
        fw.cp("dve", kb.t[:, 2:4, :], kvo[:, 0, 1:3, :], [kvout], [kb])
        fw.cp("pool", Vs.t[:, m, :, 0:64], kvout.t[:, 1, 128:256].rearrange("p (g d) -> p g d", d=64), [kvout], [Vs_b[m]])
        fw.cp("pool", Vw.t[:, m % 8, :, 0:64], kvout.t[:, 1, 256:384].rearrange("p (g d) -> p g d", d=64), [kvout],
              [Vw_b[m % 8]])
        chk(15)
        tb = nxt("T")
        for j in range(4):
            fw.tr(tb.t[:, j * 128:(j + 1) * 128], kb.t[:, j, :], ident, [kb, identb], [tb])
        if own:
            for n in range(4):
                fw.tr(tb.t[:, 512 + n * 128:512 + (n + 1) * 128], qb.t[:, n * 128:(n + 1) * 128], ident, [qb, identb], [tb])
        chk(16)
        ti = m % 2
        fw.cp("act", kcT.t[:, ti, :, :].rearrange("p a b -> p (a b)"), tb.t[:, 0:256], [tb], [kcT_b[ti]])
        fw.cp("act", KTs.t[:, m, :], tb.t[:, 256:384], [tb], [KTs_b[m]])
        fw.cp("act", KTw.t[:, m % 8, :], tb.t[:, 384:512], [tb], [KTw_b[m % 8]])
        if own:
            fw.cp("act", QTz.t[0:64, 0, :, :], tb.t[0:64, 512:1024].rearrange("p (n t) -> p n t", t=128), [tb], [QTz])
            fw.cp("act", QTz.t[64:128, 1, :, :], tb.t[64:128, 512:1024].rearrange("p (n t) -> p n t", t=128), [tb], [QTz])
            fw.act(gate.t[:], tm.t[:, 896:920], AF.Exp, [tm], [gate], scale=-1.0)
            fw.ts("dve", gate.t[:], gate.t[:], 1.0, None, ALU.add, None, [gate], [gate])
            fw.op("dve", lambda e: e.reciprocal(out=gate.t[:], in_=gate.t[:]), [gate], [gate])
        else:
            chk(17)
            pb = nxt("P")
            for ct in range(4):
                for j in range(2):
                    col = FM0 + (1 + j) * 512 + ct * 128
                    o0 = (ct * 2 + j) * 2
                    for kc in range(8):
                        fw.mm(pb.t[:, o0:o0 + 2], winb.t[:, kc, col:col + 128], hTb.t[:, kc, 126:128], kc == 0, kc == 7,
                              [winb, hTb], [pb], skip_group_check=True)
            hl = halo[k % 2]
            fw.cp("act", hcs.t[:], pb.t[:, 0:16], [pb], [hcs])
            hc4 = hcs.t[:].rearrange("p (c j t) -> p c j t", j=2, t=2)
            fw.tt("dve", hl.t[:], hc4[:, :, 0, :], hc4[:, :, 1, :], ALU.mult, [hcs], [hl])

    def compress(k):
        rk = kcT.t[:, :, 0, :].rearrange("p a (c r) -> p a c r", r=32)
        rv = kcT.t[:, :, 1, :].rearrange("p a (c r) -> p a c r", r=32)
        pb = nxt("P")
        for r in range(32):
            fw.mm(pb.t[:, 0:8], wbd.t[:, r, 0, :], rk[:, :, :, r], r == 0, r == 31, [wbd] + kcT_b, [pb])
        for r in range(32):
            fw.mm(pb.t[:, 8:16], wbd.t[:, r, 1, :], rv[:, :, :, r], r == 0, r == 31, [wbd] + kcT_b, [pb],
                  skip_group_check=True)
        fw.act(KcmpT.t[:, 8 * k:8 * k + 8], pb.t[:, 0:8], AF.Identity, [pb, biasK], [KcmpT], bias=biasK.t[:, 0:1])
        fw.cp("dve", VcmpT.t[:, 8 * k:8 * k + 8], pb.t[:, 8:16], [pb], [VcmpT])
        ch = (8 * k) // 128
        tb = nxt("T")
        fw.tr(tb.t[:, 0:128], VcmpT.t[:, ch * 128:(ch + 1) * 128], ident, [VcmpT, identb], [tb])
        fw.cp("act", csb.t[:], tb.t[:, 0:128], [tb], [csb])
        fw.tt("dve", Vcmp_tm.t[:, ch, :], csb.t[:], biasVb.t[:], ALU.add, [csb, biasVb], [Vcmp_tm])

    pipe = {"i": 0}

    def score_phase(g, kt_ap, kt_buf, bias_list):
        i = pipe["i"]
        pipe["i"] += 1
        sb_ = S[i % 2]
        sap = Sap(sb_)
        nb = len(bias_list)
        fw.mm(sap, kt_ap, QTz.t[:, g, :, :].rearrange("p n t -> p (n t)"), True, nb == 0, [kt_buf, QTz], [sb_])
        for bi, (l_ap, r_ap, bufs) in enumerate(bias_list):
            fw.mm(sap, l_ap, r_ap, False, bi == nb - 1, bufs, [sb_])
        pt = PT[i % 2]
        fw.act(pt.t[:].rearrange("p n t -> p (n t)"), sap, AF.Exp, [sb_], [pt])
        return pt

    def pv_phase(pt, v_ap, v_buf, Oacc):
        for n in range(4):
            fw.mm(Oacc.t[:, n * 65:n * 65 + 65], pt.t[:, n, :], v_ap, False, False, [pt, v_buf], [Oacc],
                  skip_group_check=True)

    def attention(k):
        C = 8 * (k + 1)
        nch = (C + 127) // 128
        NS = C // 2
        wp0 = C_WP + 128 - 4 * k
        for g in range(2):
            sa, sb2 = S[0], S[1]
            for n in range(4):
                bank = sa if n < 2 else sb2
                fw.mm(Sps[:, n * 256:n * 256 + C], QTz.t[:, g, n, :], KcmpT.t[:, 0:C], True, True, [QTz, KcmpT], [bank])
            s4 = Sps[:, :].rearrange("p (n c) -> p n c", c=256)[:, :, 0:C]
            fw.op("dve", lambda e: e.tensor_reduce(out=mx.t[:], in_=s4, axis=AX.X, op=ALU.max), [sa, sb2], [mx])
            fw.ts("dve", mx.t[:], mx.t[:], -1.0, None, ALU.mult, None, [mx], [mx])
            for n in range(4):
                bank = sa if n < 2 else sb2
                fw.act(pn.t[:, n, 0:C], Sps[:, n * 256:n * 256 + C], AF.Exp, [bank, mx], [pn], bias=mx.t[:, n:n + 1])
            fw.tt("dve", pn.t[:, :, C - 8:C], pn.t[:, :, C - 8:C],
                  cst.t[:, C_VIS:C_VIS + 8].unsqueeze(1).to_broadcast([128, 4, 8]), ALU.mult, [pn, cst], [pn])
            fw.op("dve", lambda e: e.tensor_reduce(out=sm.t[:], in_=pn.t[:, :, 0:C], axis=AX.X, op=ALU.add), [pn], [sm])
            fw.ts("dve", sm.t[:], sm.t[:], 1e-30, None, ALU.max, None, [sm], [sm])
            fw.op("dve", lambda e: e.reciprocal(out=sm.t[:], in_=sm.t[:]), [sm], [sm])
            fw.tt("dve", pn.t[:, :, 0:C], pn.t[:, :, 0:C], sm.t[:].unsqueeze(2).to_broadcast([128, 4, C]), ALU.mult,
                  [pn, sm], [pn])
            fw.cp("pool", pnb.t[:, :, 0:C], pn.t[:, :, 0:C], [pn], [pnb])
            fw.tt("pool", a1.t[:, 0:C], pn.t[:, 0, 0:C], pn.t[:, 1, 0:C], ALU.add, [pn], [a1])
            fw.tt("dve", a2.t[:, 0:C], pn.t[:, 2, 0:C], pn.t[:, 3, 0:C], ALU.add, [pn], [a2])
            fw.tt("dve", a1.t[:, 0:C], a1.t[:, 0:C], a2.t[:, 0:C], ALU.add, [a1, a2], [a1])
            a1v = a1.t[:, 0:C].rearrange("p (s two) -> p s two", two=2)
            sc = score[g]
            fw.tt("dve", sc.t[:, 0:NS], a1v[:, :, 0], a1v[:, :, 1], ALU.add, [a1], [sc])
            fw.tt("dve", sc.t[:, 0:NS], sc.t[:, 0:NS], cst.t[:, wp0:wp0 + NS], ALU.add, [sc, cst], [sc])
            fw.tt("dve", sc.t[:, 0:NS], sc.t[:, 0:NS], cst.t[:, C_B0:C_B0 + NS], ALU.add, [sc, cst], [sc])
            fw.op("dve", lambda e: e.max(out=m8.t[:, 0:8], in_=sc.t[:]), [sc], [m8])
            fw.op("dve", lambda e: e.match_replace(out=wk.t[:], in_to_replace=m8.t[:, 0:8], in_values=sc.t[:],
                                                   imm_value=-3.0e38), [sc, m8], [wk])
            fw.op("dve", lambda e: e.max(out=m8.t[:, 8:16], in_=wk.t[:]), [wk], [m8])
            nm = nmask[g]
            fw.ts("dve", nm.t[:], sc.t[:], m8.t[:, 15:16], NEG, ALU.is_lt, ALU.mult, [sc, m8], [nm])
            tb = nxt("T")
            for ch in range(nch):
                for n in range(4):
                    fw.tr(tb.t[:, (ch * 4 + n) * 128:(ch * 4 + n + 1) * 128], pnb.t[:, n, ch * 128:(ch + 1) * 128], ident,
                          [pnb, identb], [tb])
            fw.cp("act", PTc.t[:, 0:nch, :, :].rearrange("p c n t -> p (c n t)"), tb.t[:, 0:nch * 512], [tb], [PTc])
            pb = nxt("P")
            for n in range(4):
                for ch in range(nch):
                    fw.mm(pb.t[:, n * 64:(n + 1) * 64], PTc.t[:, ch, n, :], Vcmp_tm.t[:, ch, g * 64:(g + 1) * 64],
                          ch == 0, ch == nch - 1, [PTc, Vcmp_tm], [pb], skip_group_check=True)
            fw.cp("act", ocmp.t[:].rearrange("p n d -> p (n d)"), pb.t[:, 0:256], [pb], [ocmp])
            fw.mm(OA.t[:, :], zerob.t[:, 0:128], zerob.t[:, :], True, True, [zerob], [OA])
            fw.mm(OB.t[:, :], zerob.t[:, 0:128], zerob.t[:, :], True, True, [zerob], [OB])
            nchunk = 2 * k + 2
            work = []
            for mc in range(nchunk):
                eb = expb[(mc // 4) % 2]
                pre = None
                if mc % 4 == 0 or mc == 2 * k + 1:
                    if mc == 2 * k + 1:
                        lo, nb_ = mc, 1
                    else:
                        lo, nb_ = mc, min(4, 2 * k - mc)
                    if nb_ > 0:
                        pre = (eb, lo, nb_)
                if mc < 2 * k:
                    bl = [(eb.t[:, mc % 4, :], ident4, [eb, identb])]
                elif mc == 2 * k:
                    bl = [(ident, cbf.t[:, 0, :].unsqueeze(1).to_broadcast([128, 4, 128]), [identb, cbf])]
                else:
                    bl = [(eb.t[:, mc % 4, :], ident4, [eb, identb]),
                          (ident, cbf.t[:, 2, :].unsqueeze(1).to_broadcast([128, 4, 128]), [identb, cbf])]
                work.append((pre, KTs.t[:, mc, :], KTs_b[mc], bl, Vs.t[:, mc, g, :], Vs_b[mc], OA))
            for mc, bidx in ((2 * k - 4, 1), (2 * k - 3, 3), (2 * k - 2, None), (2 * k - 1, None), (2 * k, 0), (2 * k + 1, 2)):
                if mc < 0:
                    continue
                bl = []
                if bidx is not None:
                    bl = [(ident, cbf.t[:, bidx, :].unsqueeze(1).to_broadcast([128, 4, 128]), [identb, cbf])]
                work.append((None, KTw.t[:, mc % 8, :], KTw_b[mc % 8], bl, Vw.t[:, mc % 8, g, :], Vw_b[mc % 8], OB))

            def do_score(w):
                pre = w[0]
                if pre is not None:
                    eb_, lo, nb_ = pre
                    fw.cp("pool", eb_.t[:, lo % 4:lo % 4 + nb_, :].rearrange("p c (h r) -> p (c h) r", r=64),
                          nm.t[:, 2 * lo:2 * lo + 2 * nb_].unsqueeze(2).to_broadcast([128, 2 * nb_, 64]), [nm], [eb_])
                return score_phase(g, w[1], w[2], w[3])
            prev = None
            for w in work:
                pt_ = do_score(w)
                if prev is not None:
                    pv_phase(prev[0], prev[1][4], prev[1][5], prev[1][6])
                prev = (pt_, w)
            pv_phase(prev[0], prev[1][4], prev[1][5], prev[1][6])
            oa4 = OA.t[:, 0:260].rearrange("p (n d) -> p n d", d=65)
            ob4 = OB.t[:, 0:260].rearrange("p (n d) -> p n d", d=65)
            gv = gate.t[:, g * 12:(g + 1) * 12].rearrange("p (n b) -> p n b", b=3)
            fw.ts("dve", rs2.t[:, 0, :], oa4[:, :, 64], 1e-30, None, ALU.max, None, [OA], [rs2])
            fw.ts("dve", rs2.t[:, 1, :], ob4[:, :, 64], 1e-30, None, ALU.max, None, [OB], [rs2])
            fw.op("dve", lambda e: e.reciprocal(out=rs2.t[:], in_=rs2.t[:]), [rs2], [rs2])
            fw.tt("dve", fac.t[:, 1, :], rs2.t[:, 0, :], gv[:, :, 1], ALU.mult, [rs2, gate], [fac])
            fw.tt("dve", fac.t[:, 2, :], rs2.t[:, 1, :], gv[:, :, 2], ALU.mult, [rs2, gate], [fac])
            fw.tt("dve", ot1.t[:], ocmp.t[:], gv[:, :, 0].unsqueeze(2).to_broadcast([128, 4, 64]), ALU.mult, [ocmp, gate], [ot1])
            fw.tt("dve", ot2.t[:], oa4[:, :, 0:64], fac.t[:, 1, :].unsqueeze(2).to_broadcast([128, 4, 64]), ALU.mult,
                  [OA, fac], [ot2])
            fw.tt("pool", ot1.t[:], ot1.t[:], ot2.t[:], ALU.add, [ot1, ot2], [ot1])
            fw.tt("dve", ot2.t[:], ob4[:, :, 0:64], fac.t[:, 2, :].unsqueeze(2).to_broadcast([128, 4, 64]), ALU.mult,
                  [OB, fac], [ot2])
            fw.tt("dve", oall.t[:, g * 4:(g + 1) * 4, :], ot1.t[:], ot2.t[:], ALU.add, [ot1, ot2], [oall])

    def tile_back(m, k):
        hTb = hT[m % 2]
        ue, mt, sz = uext, mixT, szT
        fw.ts("dve", halot.t[:], halo[(k + 1) % 2].t[:], cst.t[:, C_AB:C_AB + 1], None, ALU.mult, None,
              [halo[(k + 1) % 2], cst], [halot])
        fw.stt(ue.t[:, :, 0:2], halo[k % 2].t[:], cst.t[:, C_AB + 1:C_AB + 2], halot.t[:], ALU.mult, ALU.add,
               [halo[k % 2], cst, halot], [ue])
        for ct in range(4):
            pb = nxt("P")
            for j in range(4):
                col = FM0 + j * 512 + ct * 128
                for kc in range(8):
                    fw.mm(pb.t[:, j * 128:(j + 1) * 128], winb.t[:, kc, col:col + 128], hTb.t[:, kc, :], kc == 0, kc == 7,
                          [winb, hTb], [pb], skip_group_check=True)
            fw.cp("act", csb.t[:], pb.t[:, 128:256], [pb], [csb])
            fw.tt("dve", ue.t[:, ct, 2:130], pb.t[:, 256:384], csb.t[:], ALU.mult, [pb, csb], [ue])
            fw.act(ez.t[:, 0:128], pb.t[:, 384:512], AF.Exp, [pb], [ez], scale=-1.0)
            fw.ts("dve", ez.t[:, 0:128], ez.t[:, 0:128], 1.0, None, ALU.add, None, [ez], [ez])
            fw.op("dve", lambda e: e.reciprocal(out=ez.t[:, 0:128], in_=ez.t[:, 0:128]), [ez], [ez])
            fw.tt("dve", g1.t[:], pb.t[:, 0:128], ez.t[:, 0:128], ALU.mult, [pb, ez], [g1])
            fw.tt("dve", g1.t[:], pb.t[:, 384:512], g1.t[:], ALU.mult, [pb, g1], [g1])
            fw.ts("dve", acc.t[:], ue.t[:, ct, 0:128], cvp.t[:, ct * 3:ct * 3 + 1], None, ALU.mult, None, [ue, cvp], [acc])
            fw.stt(acc.t[:], ue.t[:, ct, 1:129], cvp.t[:, ct * 3 + 1:ct * 3 + 2], acc.t[:], ALU.mult, ALU.add,
                   [ue, cvp, acc], [acc])
            fw.stt(acc.t[:], ue.t[:, ct, 2:130], cvp.t[:, ct * 3 + 2:ct * 3 + 3], acc.t[:], ALU.mult, ALU.add,
                   [ue, cvp, acc], [acc])
            fw.stt(mt.t[:, ct, :], acc.t[:], cvp.t[:, 12 + ct:13 + ct], g1.t[:], ALU.add, ALU.mult, [acc, cvp, g1], [mt])
        if k == NSLOT - 1:
            for ct in range(4):
                fw.dma("sp", conv_o[:, ct * 128:(ct + 1) * 128].rearrange("t p -> p t"), ue.t[:, ct, 128:130], reads=[ue],
                       is_output=True, allow_slow_non_contiguous=True)
        pb = nxt("P")
        for ct in range(4):
            col = FM0 + 2048 + ct * 128
            for kc in range(8):
                fw.mm(pb.t[:, ct * 128:(ct + 1) * 128], winb.t[:, kc, col:col + 128], hTb.t[:, kc, :], kc == 0, kc == 7,
                      [winb, hTb], [pb], skip_group_check=True)
        fw.act(ez.t[:], pb.t[:, :], AF.Exp, [pb], [ez], scale=-1.0)
        fw.ts("dve", ez.t[:], ez.t[:], 1.0, None, ALU.add, None, [ez], [ez])
        fw.op("dve", lambda e: e.reciprocal(out=ez.t[:], in_=ez.t[:]), [ez], [ez])
        fw.tt("dve", sz.t[:].rearrange("p a b -> p (a b)"), pb.t[:, :], ez.t[:], ALU.mult, [pb, ez], [sz])

    def tile_out(m, k):
        xb, _ = xbuf(m)
        mt, sz = mixT, szT
        tb = nxt("T")
        for ct in range(4):
            fw.tr(tb.t[:, ct * 128:(ct + 1) * 128], oall.t[:, 2 * ct:2 * ct + 2, :].rearrange("p a b -> p (a b)"), ident,
                  [oall, identb], [tb])
        fw.cp("act", ez.t[:], tb.t[:, 0:512], [tb], [ez])
        fw.tt("dve", mt.t[:, 4:8, :].rearrange("p a b -> p (a b)"), ez.t[:], sz.t[:].rearrange("p a b -> p (a b)"),
              ALU.mult, [ez, sz], [mt])
        for hf in range(2):
            pb = nxt("P")
            for kc in range(8):
                fw.mm(pb.t[:, :], mt.t[:, kc, :], woutb.t[:, kc, hf * 512:(hf + 1) * 512], kc == 0, kc == 7, [mt, woutb], [pb])
            fw.tt("dve", xb.t[:, hf * 512:(hf + 1) * 512], pb.t[:, :], xb.t[:, hf * 512:(hf + 1) * 512], ALU.add, [pb, xb], [xb])
        fw.dma("sp", y_o[k * 128:(k + 1) * 128, :], xb.t[:], reads=[xb], is_output=True)

    load_x(1)
    load_x(0)
    try:
      for k in range(nslot):
        mo, mw = 2 * k + 1, 2 * k
        if k + 1 < nslot:
            load_x(2 * k + 3)
        tile_front(mo, False, k)
        chk(1)
        if k + 1 < nslot:
            load_x(2 * k + 2)
        tile_front(mw, True, k)
        chk(2)
        compress(k)
        chk(3)
        tile_back(mw, k)
        chk(4)
        attention(k)
        chk(5)
        tile_out(mw, k)
    except StopBuild:
        pass

    fw.finish()
    print("program: %d tracked instructions, %d dma sems, counts %s" % (fw.ninst, fw.nd, fw.cnt))
    return nc, es


def _tile_of_slot(mm, p):
    k, o = mm // 2, mm % 2
    return 2 * k + (p if o == 0 else 1 - p)


def _consts(p):
    c = np.zeros((128, C_END), np.float32)
    c2 = np.zeros((128, 640), np.float32)
    kk = np.arange(128)[:, None]
    qq = np.arange(128)[None, :]
    c2[:, 0:128] = np.where(kk <= qq, 0.0, NEG)
    c2[:, 128:256] = np.where(kk > qq, 0.0, NEG)
    c2[:, 256:384] = NEG if p == 0 else 0.0
    c2[:, 384:512] = 0.0 if p == 0 else NEG
    c2[:, 512:640] = np.eye(128, dtype=np.float32)
    ql = np.arange(128)
    W = np.zeros((128, 384), np.float32)
    for r in range(-128, 256):
        col = 128 + r
        if r <= -2:
            v = np.zeros(128)
        elif r == -1:
            v = np.where(ql < 64, 1e4, 0.0) if p == 0 else np.zeros(128)
        elif r == 0:
            v = np.full(128, 1e4)
        elif r == 1:
            v = np.where(ql >= 64, 1e4, BIGNEG)
        elif r == 2:
            v = np.full(128, BIGNEG) if p == 0 else np.zeros(128)
        elif r == 3:
            v = np.full(128, BIGNEG) if p == 0 else np.where(ql < 64, 1e4, 0.0)
        else:
            v = np.full(128, BIGNEG)
        W[:, col] = v
    c[:, C_WP:C_WP + 384] = W
    c[:, C_B0 + (0 if p == 0 else 2)] = 1e4
    vis = np.zeros((128, 8), np.float32)
    for cl in range(4):
        vis[:, cl] = (32 * cl + 31 <= ql)
        vis[:, 4 + cl] = 0.0 if p == 0 else 1.0
    c[:, C_VIS:C_VIS + 8] = vis
    c[:, C_AB] = 1.0 if p == 0 else 0.0
    c[:, C_AB + 1] = 0.0 if p == 0 else 1.0
    c[:, C_AB + 2] = EPS
    return c, c2


def _consts_s():
    c = np.zeros((128, 272), np.float32)
    c[:, 0] = 1e4
    c[:, 255] = 1e4
    for s in range(4):
        c[s, 256 + s] = 1.0
    c[0, 260] = NEG
    c[:, 268] = 1.0
    return c


def _perm_w_in():
    q0, kv0, gl0, za0 = 2048, 2560, 3328, 3352
    cols = []
    for n in range(4):
        for g in range(2):
            cols += list(range(q0 + (g * 4 + n) * 64, q0 + (g * 4 + n + 1) * 64))
    for br in range(3):
        for g in range(2):
            cols += list(range(kv0 + br * 256 + g * 64, kv0 + br * 256 + g * 64 + 64))
    for br in range(3):
        for g in range(2):
            cols += list(range(kv0 + br * 256 + 128 + g * 64, kv0 + br * 256 + 128 + g * 64 + 64))
    cols += list(range(gl0, gl0 + 24))
    cols += list(range(0, 2048))
    cols += list(range(za0, za0 + 512))
    return np.array(cols)


_CACHE = {}


def kernel(x_prompt, x_sample, cache_cmp_kv, cache_slc_kv, state_win_kv, state_conv, page_table,
           norm_g, w_in, conv_w, conv_b, q_gain, k_gain, cmp_pe, cmp_w, w_out, _nslot=NSLOT, _do_sample=True):
    f = np.float32
    x_prompt = np.asarray(x_prompt, f)
    if "nc" not in _CACHE or _CACHE.get("key") != (_nslot, _do_sample):
        nc, es = build_program(_nslot, _do_sample)
        _CACHE["nc"], _CACHE["es"], _CACHE["key"] = nc, es, (_nslot, _do_sample)
    nc = _CACHE["nc"]
    wperm = np.ascontiguousarray(np.asarray(w_in, f)[0][:, _perm_w_in()])
    wo = np.ascontiguousarray(np.asarray(w_out, f)[0])
    ng = np.ascontiguousarray(np.asarray(norm_g, f)[0].reshape(8, 128).T)
    qg = np.asarray(q_gain, f)[0]
    kg = np.asarray(k_gain, f)[0]
    gains = np.concatenate([np.tile(qg, 8)] + [np.tile(kg[br], 2) for br in range(3)])[None, :].astype(f)
    cw = np.asarray(conv_w, f)[0]
    cb = np.asarray(conv_b, f)[0]
    convp = np.zeros((128, 16), f)
    for ct in range(4):
        for kk in range(3):
            convp[:, ct * 3 + kk] = cw[kk, ct * 128:(ct + 1) * 128]
        convp[:, 12 + ct] = cb[ct * 128:(ct + 1) * 128]
    cmpw = np.ascontiguousarray(np.asarray(cmp_w, f)[0].transpose(2, 0, 1, 3).reshape(64, 32 * 2 * 64))
    cmppe = np.ascontiguousarray(np.asarray(cmp_pe, f)[0].transpose(2, 0, 1).reshape(64, 64))
    inv = (10000.0 ** (-np.arange(32, dtype=np.float32) / 32)).astype(f)
    ccmp = np.asarray(cache_cmp_kv, f).reshape(NPOOL * 16, 2048)
    cslc = np.asarray(cache_slc_kv, f).reshape(NPOOL * 16, 2048)
    xs_all = np.asarray(x_sample, f).reshape(32, 1024)
    swin_all = np.asarray(state_win_kv, f).reshape(32, 512, 256)
    sconv_all = np.asarray(state_conv, f).reshape(32, 2, 512)
    pt_all = np.asarray(page_table).astype(np.int32)
    angs = (np.float32(PAST) * inv)[None, :]
    ropes = np.repeat(np.concatenate([np.cos(angs), np.sin(angs)], axis=1).astype(f), 4, axis=0)
    iota16 = np.tile(np.arange(16, dtype=np.float32)[None, :], (128, 1))
    cs_s = _consts_s()
    in_maps = []
    for c in range(8):
        b, p = c // 2, c % 2
        order = np.array([_tile_of_slot(mm, p) for mm in range(NTILE)])
        xpb = np.ascontiguousarray(x_prompt[b].reshape(NTILE, 128, 1024)[order].reshape(NTILE * 128, 1024))
        pos = (order[:, None] * 128 + np.arange(128)[None, :]).reshape(-1).astype(f)
        ang = pos[:, None] * inv[None, :]
        ropet = np.concatenate([np.cos(ang), np.sin(ang)], axis=1).astype(f)
        c1, c2 = _consts(p)
        sl = slice(4 * c, 4 * c + 4)
        in_maps.append(dict(xp=xpb, rope=ropet, w_in_d=wperm, w_out_d=wo, norm_g_d=ng, gains_in=gains, convp=convp,
                            cmpw=cmpw, cmppe=cmppe, consts=c1, consts2=c2,
                            xs=np.ascontiguousarray(xs_all[sl]), ropes=ropes,
                            ptab=np.ascontiguousarray(pt_all[sl].T), iota16=iota16, ccmp=ccmp, cslc=cslc,
                            swin=np.ascontiguousarray(swin_all[sl]), sconv=np.ascontiguousarray(sconv_all[sl]),
                            convw_row=np.ascontiguousarray(cw.reshape(1, 1536)), convb_row=np.ascontiguousarray(cb.reshape(1, 512)),
                            consts_s=cs_s))
    if not _do_sample:
        for mp in in_maps:
            del mp["ccmp"], mp["cslc"]
    res = run_bass_kernel_spmd(nc, in_maps, core_ids=list(range(8)))
    R = res.results
    B, T = 4, 8192
    y = np.zeros((B, T, 1024), f)
    pc = np.zeros((1, B, T, 2, 2, 64), f)
    psl = np.zeros((1, B, T, 2, 2, 64), f)
    pw = np.zeros((1, B, 512, 2, 2, 64), f)
    pcv = np.zeros((1, B, 2, 512), f)
    ys = np.zeros((32, 1, 1024), f)
    sc_ = np.zeros((1, 32, 1, 2, 2, 64), f)
    ss_ = np.zeros((1, 32, 1, 2, 2, 64), f)
    sw_ = np.zeros((1, 32, 512, 2, 2, 64), f)
    scv = np.zeros((1, 32, 2, 512), f)
    for c in range(8):
        b, p = c // 2, c % 2
        r = R[c]
        for k in range(NSLOT):
            i = 2 * k + p
            y[b, i * 128:(i + 1) * 128] = r["y_o"][k * 128:(k + 1) * 128]
            pc[0, b, i * 128:(i + 1) * 128] = r["cmp_o"][k * 128:(k + 1) * 128].reshape(128, 2, 2, 64)
            psl[0, b, i * 128:(i + 1) * 128] = r["slc_o"][k * 128:(k + 1) * 128].reshape(128, 2, 2, 64)
            if k >= 30:
                pw[0, b, (i - 60) * 128:(i - 59) * 128] = r["win_o"][(k - 30) * 128:(k - 29) * 128].reshape(128, 2, 2, 64)
        if p == 1:
            pcv[0, b] = r["conv_o"]
        sl = slice(4 * c, 4 * c + 4)
        ys[sl, 0] = r["ys_o"]
        sc_[0, sl, 0] = r["scmp_o"].reshape(4, 2, 2, 64)
        ss_[0, sl, 0] = r["sslc_o"].reshape(4, 2, 2, 64)
        sw_[0, sl] = r["swin_o"].reshape(4, 512, 2, 2, 64)
        scv[0, sl] = r["sconv_o"]
    return y, ys, pc, psl, pw, pcv, sc_, ss_, sw_, scv
```

```python
import numpy as np
from contextlib import ExitStack
import concourse.bass as bass
import concourse.mybir as mybir
from concourse.bass_utils import run_bass_kernel_spmd

F32 = mybir.dt.float32
BF16 = mybir.dt.bfloat16
I32 = mybir.dt.int32
U32 = mybir.dt.uint32
ALU = mybir.AluOpType
AF = mybir.ActivationFunctionType
AX = mybir.AxisListType

NEG = -30000.0
BIGNEG = -1.0e30
EPS = 1e-6
NTILE = 64
NSLOT = 32
IN_W = 3864
TMW = 1304
FM0 = 1304
PAST = 16384
NPOOL = 5120

C_WP, C_B0, C_VIS, C_AB, C_END = 0, 384, 512, 520, 528


class StopBuild(Exception):
    pass


class Buf:
    __slots__ = ("w", "r", "dsem", "dkey", "dcnt", "name")

    def __init__(self, name=""):
        self.w = None
        self.r = {}
        self.dsem = None
        self.dkey = None
        self.dcnt = 0
        self.name = name


class TT(Buf):
    __slots__ = ("t",)

    def __init__(self, t, name=""):
        Buf.__init__(self, name)
        self.t = t


class Fw:
    def __init__(self, nc, es):
        self.nc, self.es = nc, es
        self.eng = dict(pe=nc.tensor, act=nc.scalar, dve=nc.vector, pool=nc.gpsimd, sp=nc.sync)
        self.semh = {}
        for e in self.eng:
            self.semh[e] = es.enter_context(nc.semaphore("s_" + e))
        self.cnt = {e: 0 for e in self.eng}
        self.vc = {e: {} for e in self.eng}
        self.snaps = {}
        self.nd = 0
        self.out_events = []
        self.ninst = 0

    def sb(self, name, shape, dt):
        return TT(self.es.enter_context(self.nc.sbuf_tensor(name, shape, dt)), name)

    def ps(self, name, shape, dt):
        return TT(self.es.enter_context(self.nc.psum_tensor(name, shape, dt)), name)

    def _wait(self, e, ev):
        key, val = ev
        vc = self.vc[e]
        if vc.get(key, 0) >= val:
            return
        if key == e and e in ("pe", "sp"):
            return
        self.eng[e].wait_ge(self.semh[key], val)
        snap = self.snaps.get(ev)
        if snap:
            for k2, v2 in snap.items():
                if vc.get(k2, 0) < v2:
                    vc[k2] = v2
        if vc.get(key, 0) < val:
            vc[key] = val

    def _deps(self, e, reads, writes):
        deps = []
        for b in reads:
            if b.w is not None:
                deps.append(b.w)
        for b in writes:
            if b.w is not None:
                deps.append(b.w)
            deps.extend(b.r.items())
        deps.sort(key=lambda ev: -ev[1])
        for ev in deps:
            self._wait(e, ev)

    def _mark(self, ev, reads, writes):
        for b in reads:
            if b.r.get(ev[0], 0) < ev[1]:
                b.r[ev[0]] = ev[1]
        for b in writes:
            b.w = ev
            b.r = {}

    def op(self, e, fn, reads=(), writes=()):
        self._deps(e, reads, writes)
        inst = fn(self.eng[e])
        self.cnt[e] += 1
        n = self.cnt[e]
        inst.then_inc(self.semh[e], 1)
        ev = (e, n)
        s = dict(self.vc[e])
        s[e] = n
        self.snaps[ev] = s
        self._mark(ev, reads, writes)
        self.ninst += 1
        return ev

    def dma(self, q, out, in_, reads=(), writes=(), is_output=False, indirect=None, **kw):
        self._deps(q, reads, writes)
        owner = writes[0] if writes else reads[0]
        if owner.dsem is None:
            owner.dkey = "d%d" % self.nd
            self.nd += 1
            owner.dsem = self.es.enter_context(self.nc.semaphore(owner.dkey))
            self.semh[owner.dkey] = owner.dsem
        if indirect is not None:
            inst = self.eng[q].indirect_dma_start(out=out, out_offset=None, in_=in_, in_offset=indirect, **kw)
        else:
            inst = self.eng[q].dma_start(out=out, in_=in_, **kw)
        inst.then_inc(owner.dsem, 16)
        owner.dcnt += 16
        ev = (owner.dkey, owner.dcnt)
        self.snaps[ev] = dict(self.vc[q])
        self._mark(ev, reads, writes)
        if is_output:
            self.out_events.append(ev)
        self.ninst += 1
        return ev

    def finish(self):
        last = {}
        for k, v in self.out_events:
            last[k] = max(last.get(k, 0), v)
        for k, v in last.items():
            self._wait("sp", (k, v))
        for e in ("pe", "act", "dve", "pool"):
            if self.cnt[e]:
                self._wait("sp", (e, self.cnt[e]))

    def mm(self, out, lhsT, rhs, start, stop, reads, writes, **kw):
        return self.op("pe", lambda e: e.matmul(out, lhsT=lhsT, rhs=rhs, start=start, stop=stop, **kw),
                       reads, writes)

    def tr(self, out, in_, ident, reads, writes):
        return self.op("pe", lambda e: e.transpose(out=out, in_=in_, identity=ident), reads, writes)

    def act(self, out, in_, func, reads, writes, **kw):
        return self.op("act", lambda e: e.activation(out=out, in_=in_, func=func, **kw), reads, writes)

    def tt(self, eng, out, in0, in1, op, reads, writes):
        return self.op(eng, lambda e: e.tensor_tensor(out=out, in0=in0, in1=in1, op=op), reads, writes)

    def ts(self, eng, out, in0, s1, s2, op0, op1, reads, writes):
        if op1 is None:
            return self.op(eng, lambda e: e.tensor_scalar(out=out, in0=in0, scalar1=s1, scalar2=None, op0=op0),
                           reads, writes)
        return self.op(eng, lambda e: e.tensor_scalar(out=out, in0=in0, scalar1=s1, scalar2=s2, op0=op0, op1=op1),
                       reads, writes)

    def stt(self, out, in0, scalar, in1, op0, op1, reads, writes):
        return self.op("dve", lambda e: e.scalar_tensor_tensor(out=out, in0=in0, scalar=scalar, in1=in1,
                                                                op0=op0, op1=op1), reads, writes)

    def cp(self, eng, out, in_, reads, writes):
        if eng == "act":
            return self.act(out, in_, AF.Copy, reads, writes)
        return self.op(eng, lambda e: e.tensor_copy(out=out, in_=in_), reads, writes)

    def memset(self, eng, ap, val, writes):
        return self.op(eng, lambda e: e.memset(ap, val), (), writes)


def build_program(nslot=NSLOT, do_sample=True):
    nc = bass.Bass("TRN2", target_bir_lowering=False)
    es = ExitStack()
    fw = Fw(nc, es)

    def din(name, shape, dt=F32):
        return nc.dram_tensor(name, shape, dt, kind="ExternalInput").ap()

    def dout(name, shape, dt=F32):
        return nc.dram_tensor(name, shape, dt, kind="ExternalOutput").ap()

    xp = din("xp", [NTILE * 128, 1024])
    rope = din("rope", [NTILE * 128, 64])
    w_in = din("w_in_d", [1024, IN_W])
    w_out = din("w_out_d", [1024, 1024])
    norm_g = din("norm_g_d", [128, 8])
    gains_d = din("gains_in", [1, 896])
    convp = din("convp", [128, 16])
    cmpw = din("cmpw", [64, 32 * 2 * 64])
    cmppe = din("cmppe", [64, 64])
    consts_d = din("consts", [128, C_END])
    consts2_d = din("consts2", [128, 640])

    y_o = dout("y_o", [NSLOT * 128, 1024])
    cmp_o = dout("cmp_o", [NSLOT * 128, 256])
    slc_o = dout("slc_o", [NSLOT * 128, 256])
    win_o = dout("win_o", [256, 256])
    conv_o = dout("conv_o", [2, 512])

    xs_d = din("xs", [4, 1024])
    ropes_d = din("ropes", [4, 64])
    ptab_d = din("ptab", [128, 4], I32)
    iota_d = din("iota16", [128, 16], F32)
    ccmp_d = din("ccmp", [NPOOL * 16, 2048]) if do_sample else None
    cslc_d = din("cslc", [NPOOL * 16, 2048]) if do_sample else None
    swin_d = din("swin", [4, 512, 256])
    sconv_d = din("sconv", [4, 2, 512])
    convw_d = din("convw_row", [1, 3 * 512])
    convb_d = din("convb_row", [1, 512])
    cs_d = din("consts_s", [128, 272])
    ys_o = dout("ys_o", [4, 1024])
    scmp_o = dout("scmp_o", [4, 256])
    sslc_o = dout("sslc_o", [4, 256])
    swin_o = dout("swin_o", [4, 512, 256])
    sconv_o = dout("sconv_o", [4, 2, 512])

    winb = fw.sb("winb", [128, 8, IN_W], BF16)
    woutb = fw.sb("woutb", [128, 8, 1024], BF16)
    wbd = fw.sb("wbd", [128, 32, 2, 128], BF16)
    cst = fw.sb("cst", [128, C_END], F32)
    cbf = fw.sb("cbf", [128, 4, 128], BF16)
    identb = fw.sb("identb", [128, 128], BF16)
    zerob = fw.sb("zerob", [128, 512], BF16)
    gains = fw.sb("gains", [128, 896], F32)
    cvp = fw.sb("cvp", [128, 16], F32)
    ngt = fw.sb("ngt", [128, 8], F32)
    biasK = fw.sb("biasK", [128, 1], F32)
    biasVb = fw.sb("biasVb", [128, 128], F32)

    P = [fw.ps("psP%d" % i, [128, 512], F32) for i in range(2)]
    Sps = es.enter_context(nc.psum_tensor("psS", [128, 1024], F32))
    S = [TT(Sps, "S0"), TT(Sps, "S1")]
    OA = fw.ps("psOA", [128, 512], F32)
    OB = fw.ps("psOB", [128, 512], F32)
    Tb = [fw.ps("psT%d" % i, [128, 1024], BF16) for i in range(2)]
    rr = {"P": 0, "T": 0, "S": 0}

    def nxt(kind):
        lst = {"P": P, "T": Tb, "S": S}[kind]
        b = lst[rr[kind] % len(lst)]
        rr[kind] += 1
        return b

    def Sap(b):
        return Sps[:, 0:512] if b is S[0] else Sps[:, 512:1024]

    def barrier(extra=()):
        for e in ("dve", "pool", "act", "pe", "sp"):
            for e2 in ("dve", "pool", "act", "pe"):
                if fw.cnt[e2]:
                    fw._wait(e, (e2, fw.cnt[e2]))
            for b in extra:
                if b.w is not None:
                    fw._wait(e, b.w)
                for ev in b.r.items():
                    fw._wait(e, ev)

    fw.dma("sp", cst.t[:], consts_d[:, :], writes=[cst])
    fw.dma("sp", ngt.t[:], norm_g[:, :], writes=[ngt])
    fw.dma("sp", cvp.t[:], convp[:, :], writes=[cvp])
    fw.dma("sp", gains.t[:], gains_d[0:1, :].to_broadcast([128, 896]), writes=[gains])
    fw.ts("dve", gains.t[:, 0:512], gains.t[:, 0:512], 0.125, None, ALU.mult, None, [gains], [gains])
    fw.memset("pool", zerob.t[:], 0.0, [zerob])
    fw.memset("pool", wbd.t[:], 0.0, [wbd])

    with ExitStack() as es0:
        def sb0(name, shape, dt):
            return TT(es0.enter_context(nc.sbuf_tensor(name, shape, dt)), name)
        stg = [sb0("stg%d" % i, [128, 1932], F32) for i in range(2)]
        cwst = sb0("cwst", [128, 32 * 2 * 64], F32)
        pest = sb0("pest", [128, 64], F32)
        peb = sb0("peb", [128, 64], BF16)
        perep = [sb0("perep%d" % i, [128, 128], BF16) for i in range(2)]
        cst2 = sb0("cst2", [128, 640], F32)
        fw.dma("sp", cst2.t[:], consts2_d[:, :], writes=[cst2])
        fw.cp("dve", cbf.t[:].rearrange("p a b -> p (a b)"), cst2.t[:, 0:512], [cst2], [cbf])
        fw.cp("dve", identb.t[:], cst2.t[:, 512:640], [cst2], [identb])
        it = 0
        for kc in range(8):
            for hf in range(2):
                sg = stg[it % 2]
                it += 1
                fw.dma("sp", sg.t[:], w_in[kc * 128:(kc + 1) * 128, hf * 1932:(hf + 1) * 1932], writes=[sg])
                fw.ts("dve" if hf == 0 else "pool", winb.t[:, kc, hf * 1932:(hf + 1) * 1932], sg.t[:],
                      ngt.t[:, kc:kc + 1], None, ALU.mult, None, [sg, ngt], [winb])
        for kc in range(8):
            sg = stg[it % 2]
            it += 1
            fw.dma("sp", sg.t[:, 0:1024], w_out[kc * 128:(kc + 1) * 128, :], writes=[sg])
            fw.cp("dve" if kc % 2 == 0 else "pool", woutb.t[:, kc, :], sg.t[:, 0:1024], [sg], [woutb])
        fw.dma("sp", cwst.t[0:64, :], cmpw[:, :], writes=[cwst])
        fw.dma("sp", cwst.t[64:128, :], cmpw[:, :], writes=[cwst])
        fw.dma("sp", pest.t[0:64, :], cmppe[:, :], writes=[pest])
        fw.dma("sp", pest.t[64:128, :], cmppe[:, :], writes=[pest])
        cw4 = cwst.t[:].rearrange("p (r j e) -> p r j e", r=32, j=2)
        fw.cp("dve", wbd.t[0:64, :, :, 0:64], cw4[0:64], [cwst], [wbd])
        fw.cp("dve", wbd.t[64:128, :, :, 64:128], cw4[64:128], [cwst], [wbd])
        fw.cp("dve", peb.t[:], pest.t[:], [pest], [peb])
        pk = nxt("P")
        for r in range(32):
            fw.mm(pk.t[:, 0:2], wbd.t[:, r, 0, :], peb.t[:, 2 * r:2 * r + 2], r == 0, r == 31, [wbd, peb], [pk])
        fw.cp("act", biasK.t[:], pk.t[:, 0:1], [pk], [biasK])
        pv = nxt("P")
        for r in range(32):
            pr = perep[r % 2]
            fw.cp("dve", pr.t[:], peb.t[:, 2 * r + 1:2 * r + 2].to_broadcast([128, 128]), [peb], [pr])
            fw.mm(pv.t[:, 0:128], pr.t[:], wbd.t[:, r, 1, :], r == 0, r == 31, [pr, wbd], [pv])
        fw.cp("act", biasVb.t[:], pv.t[:, 0:128], [pv], [biasVb])
        barrier(stg + [cwst, pest, cst2])

    ident = identb.t[:]
    ident4 = identb.t[:].unsqueeze(1).to_broadcast([128, 4, 128])

    def rope_apply(eng2, src3, dst3, cosb, sinb, tcs, tsn, H, reads, wbufs, np_=128):
        def h2(ap):
            return ap.rearrange("p h (t d) -> p (h t) d", d=32)
        t3 = tcs.t[0:np_, 0:H * 64].rearrange("p (h d) -> p h d", d=64)
        s3 = tsn.t[0:np_, 0:H * 64].rearrange("p (h d) -> p h d", d=64)
        fw.tt("dve", h2(t3), h2(src3), cosb, ALU.mult, reads, [tcs])
        fw.tt(eng2, h2(s3), h2(src3), sinb, ALU.mult, reads, [tsn])
        fw.tt("dve", dst3[:, :, 0:32], t3[:, :, 0:32], s3[:, :, 32:64], ALU.subtract, [tcs, tsn], wbufs)
        fw.tt(eng2, dst3[:, :, 32:64], t3[:, :, 32:64], s3[:, :, 0:32], ALU.add, [tcs, tsn], wbufs)

    def qk_norm(eng2, tmb, r0, r1, ssh, lnh, rsh, tsn, np_, epsap):
        h0, h1 = r0 // 64, r1 // 64
        H = h1 - h0
        v = tmb.t[0:np_, r0:r1].rearrange("p (h d) -> p h d", d=64)
        sqv = tsn.t[0:np_, 0:r1 - r0]
        fw.tt(eng2, sqv, tmb.t[0:np_, r0:r1], tmb.t[0:np_, r0:r1], ALU.mult, [tmb], [tsn])
        fw.op("dve", lambda e: e.tensor_reduce(out=ssh.t[0:np_, h0:h1], in_=sqv.rearrange("p (h d) -> p h d", d=64),
                                               axis=AX.X, op=ALU.add), [tsn], [ssh])
        fw.act(lnh.t[0:np_, h0:h1], ssh.t[0:np_, h0:h1], AF.Ln, [ssh], [lnh], scale=1.0 / 64, bias=epsap)
        fw.act(rsh.t[0:np_, h0:h1], lnh.t[0:np_, h0:h1], AF.Exp, [lnh], [rsh], scale=-0.5)
        fw.tt("dve", v, v, rsh.t[0:np_, h0:h1].unsqueeze(2).to_broadcast([np_, H, 64]), ALU.mult, [tmb, rsh], [tmb])
        fw.tt(eng2, tmb.t[0:np_, r0:r1], tmb.t[0:np_, r0:r1], gains.t[0:np_, r0:r1], ALU.mult, [tmb, gains], [tmb])

    def sample_path():
        ess = ExitStack()

        def sbs(name, shape, dt):
            return TT(ess.enter_context(nc.sbuf_tensor(name, shape, dt)), name)
        cs = sbs("cs", [128, 272], F32)
        xs = sbs("xs_t", [4, 1024], F32)
        rps = sbs("rps", [4, 64], F32)
        ptab = sbs("ptab_t", [128, 4], I32)
        iot = sbs("iot", [128, 16], F32)
        idxf = sbs("idxf", [128, 4, 16], F32)
        idxa = sbs("idxa", [128, 4, 16], I32)
        hbs = sbs("hbs", [4, 1024], BF16)
        s1 = sbs("s1", [4, 1], F32)
        s2 = sbs("s2", [4, 1], F32)
        s3 = sbs("s3", [4, 1], F32)
        hTs = sbs("hTs", [128, 8, 4], BF16)
        tms = sbs("tms", [4, IN_W], F32)
        ssh = sbs("ssh_s", [4, 14], F32)
        lnh = sbs("lnh_s", [4, 14], F32)
        rsh = sbs("rsh_s", [4, 14], F32)
        tcs = sbs("tcs_s", [4, 896], F32)
        tsn = sbs("tsn_s", [4, 896], F32)
        qsf = sbs("qsf", [4, 512], F32)
        qsb = sbs("qsb", [4, 512], BF16)
        kvs = sbs("kvs", [4, 2, 384], F32)
        kvsb = sbs("kvsb", [4, 2, 384], BF16)
        gs = sbs("gs", [4, 24], F32)
        cwr = sbs("cwr", [4, 3, 512], F32)
        cbr = sbs("cbr", [4, 512], F32)
        scv = sbs("scv", [4, 2, 512], F32)
        us = sbs("us", [4, 512], F32)
        cy = sbs("cy", [4, 512], F32)
        ezs = sbs("ezs", [4, 512], F32)
        mixs = sbs("mixs", [4, 1024], F32)
        mixb = sbs("mixb", [4, 1024], BF16)
        mixTs = sbs("mixTs", [128, 8, 4], BF16)
        QTzs = sbs("QTzs", [128, 4, 2, 4], BF16)
        prod = sbs("prod", [4, 4, 128], F32)
        snew = sbs("snew", [4, 2, 8], F32)
        pnm = sbs("pnm", [4, 2, 2, 4], F32)
        pnmb = sbs("pnmb", [4, 2, 2, 4], BF16)
        stg = [sbs("sstg%d" % i, [128, 8, 256], BF16) for i in range(2)]
        kcTs = sbs("kcTs", [128, 8, 128], BF16)
        vcTs = sbs("vcTs", [128, 8, 128], BF16)
        KcmpTs = sbs("KcmpTs", [128, 512], BF16)
        Vcmps = sbs("Vcmps", [128, 4, 128], BF16)
        pns = sbs("pns", [4, 2, 512], F32)
        pnsb = sbs("pnsb", [4, 2, 512], BF16)
        mxs = sbs("mxs", [4, 2], F32)
        sms = sbs("sms", [4, 2], F32)
        impf = sbs("impf", [1, 512], F32)
        scs = sbs("scs", [1, 256], F32)
        wks = sbs("wks", [1, 256], F32)
        m8s = sbs("m8s", [1, 16], F32)
        nms = sbs("nms", [1, 256], BF16)
        nmcol = sbs("nmcol", [128, 2, 2], F32)
        pnT = sbs("pnT", [128, 2, 4, 4], BF16)
        PTs = sbs("PTs", [128, 8, 2, 4], BF16)
        accs = sbs("accs", [128, 8], F32)
        acct = sbs("acct", [128, 8], F32)
        swt = sbs("swt", [128, 4, 256], F32)
        swb = sbs("swb", [128, 4, 256], BF16)
        stw = sbs("stw", [128, 4, 8], F32)
        Oall = sbs("Oall", [4, 3, 2, 65], F32)
        Ocat = sbs("Ocat", [4, 4, 3, 2, 65], F32)
        fcs = sbs("fcs", [4, 3, 4], F32)
        o1 = sbs("o1", [4, 4, 64], F32)
        o2 = sbs("o2", [4, 4, 64], F32)
        osf = sbs("osf", [4, 512], F32)
        ysb = xs
        dd = Buf("dramcopy")
        epsap = cst.t[0:4, C_AB + 2:C_AB + 3]
        onesf = cs.t[:, 268:269]

        fw.dma("sp", cs.t[:], cs_d[:, :], writes=[cs])
        fw.dma("sp", xs.t[:], xs_d[:, :], writes=[xs])
        fw.dma("sp", rps.t[:], ropes_d[:, :], writes=[rps])
        fw.dma("sp", ptab.t[:], ptab_d[:, :], writes=[ptab])
        fw.dma("sp", iot.t[:], iota_d[:, :], writes=[iot])
        fw.dma("sp", cwr.t[:].rearrange("p a b -> p (a b)"), convw_d[0:1, :].to_broadcast([4, 1536]), writes=[cwr])
        fw.dma("sp", cbr.t[:], convb_d[0:1, :].to_broadcast([4, 512]), writes=[cbr])
        fw.dma("sp", scv.t[:], sconv_d[:, :, :], writes=[scv])
        fw.dma("sp", swin_o[:, 0:511, :], swin_d[:, 1:512, :], reads=[dd], is_output=True)
        for s in range(4):
            fw.ts("dve", idxf.t[:, s, :], ptab.t[:, s:s + 1].to_broadcast([128, 16]), 16.0, None, ALU.mult, None, [ptab], [idxf])
        fw.tt("dve", idxf.t[:], idxf.t[:], iot.t[:].unsqueeze(1).to_broadcast([128, 4, 16]), ALU.add, [idxf, iot], [idxf])
        fw.cp("dve", idxa.t[:], idxf.t[:], [idxf], [idxa])
        fw.act(hbs.t[:], xs.t[:], AF.Square, [xs], [hbs, s1], accum_out=s1.t[:])
        fw.act(s2.t[:], s1.t[:], AF.Ln, [s1], [s2], scale=1.0 / 1024, bias=epsap)
        fw.act(s3.t[:], s2.t[:], AF.Exp, [s2], [s3], scale=-0.5)
        fw.ts("dve", hbs.t[:], xs.t[:], s3.t[:, 0:1], None, ALU.mult, None, [xs, s3], [hbs])
        tb = nxt("T")
        for kc in range(8):
            fw.tr(tb.t[:, kc * 4:kc * 4 + 4], hbs.t[:, kc * 128:(kc + 1) * 128], identb.t[0:4, 0:4], [hbs, identb], [tb])
        fw.cp("act", hTs.t[:].rearrange("p a b -> p (a b)"), tb.t[:, 0:32], [tb], [hTs])
        c0 = 0
        while c0 < IN_W:
            c1 = min(c0 + 512, IN_W)
            pb = nxt("P")
            for kc in range(8):
                fw.mm(pb.t[0:4, 0:c1 - c0], hTs.t[:, kc, :], winb.t[:, kc, c0:c1], kc == 0, kc == 7, [hTs, winb], [pb])
            fw.cp("act", tms.t[:, c0:c1], pb.t[0:4, 0:c1 - c0], [pb], [tms])
            c0 = c1
        qk_norm("dve", tms, 0, 896, ssh, lnh, rsh, tsn, 4, epsap)
        cosb = rps.t[:, 0:32].unsqueeze(1).to_broadcast([4, 28, 32])
        sinb = rps.t[:, 32:64].unsqueeze(1).to_broadcast([4, 28, 32])
        src = tms.t[:, 0:896].rearrange("p (h d) -> p h d", d=64)
        t3 = tcs.t[:, :].rearrange("p (h d) -> p h d", d=64)
        s3v = tsn.t[:, :].rearrange("p (h d) -> p h d", d=64)

        def h2(ap):
            return ap.rearrange("p h (t d) -> p (h t) d", d=32)
        fw.tt("dve", h2(t3), h2(src), cosb, ALU.mult, [tms, rps], [tcs])
        fw.tt("dve", h2(s3v), h2(src), sinb, ALU.mult, [tms, rps], [tsn])
        qv = qsf.t[:].rearrange("p (h d) -> p h d", d=64)
        kv_ = kvs.t[:, 0, :].rearrange("p (h d) -> p h d", d=64)
        fw.tt("dve", qv[:, :, 0:32], t3[:, 0:8, 0:32], s3v[:, 0:8, 32:64], ALU.subtract, [tcs, tsn], [qsf])
        fw.tt("dve", qv[:, :, 32:64], t3[:, 0:8, 32:64], s3v[:, 0:8, 0:32], ALU.add, [tcs, tsn], [qsf])
        fw.tt("dve", kv_[:, :, 0:32], t3[:, 8:14, 0:32], s3v[:, 8:14, 32:64], ALU.subtract, [tcs, tsn], [kvs])
        fw.tt("dve", kv_[:, :, 32:64], t3[:, 8:14, 32:64], s3v[:, 8:14, 0:32], ALU.add, [tcs, tsn], [kvs])
        fw.cp("dve", kvs.t[:, 1, :], tms.t[:, 896:1280], [tms], [kvs])
        fw.cp("dve", qsb.t[:], qsf.t[:], [qsf], [qsb])
        fw.cp("dve", kvsb.t[:], kvs.t[:], [kvs], [kvsb])
        kvo = kvs.t[:].rearrange("p j (b c) -> p j b c", c=128)
        fw.dma("sp", scmp_o.rearrange("s (j c) -> s j c", j=2), kvo[:, :, 0, :], reads=[kvs], is_output=True)
        fw.dma("sp", sslc_o.rearrange("s (j c) -> s j c", j=2), kvo[:, :, 1, :], reads=[kvs], is_output=True)
        fw.dma("sp", swin_o[:, 511, :].rearrange("s (j c) -> s j c", j=2), kvo[:, :, 2, :], reads=[kvs], is_output=True)
        fw.act(gs.t[:], tms.t[:, 1280:1304], AF.Exp, [tms], [gs], scale=-1.0)
        fw.ts("dve", gs.t[:], gs.t[:], 1.0, None, ALU.add, None, [gs], [gs])
        fw.op("dve", lambda e: e.reciprocal(out=gs.t[:], in_=gs.t[:]), [gs], [gs])
        f0 = FM0
        fw.tt("dve", us.t[:], tms.t[:, f0 + 512:f0 + 1024], tms.t[:, f0 + 1024:f0 + 1536], ALU.mult, [tms], [us])
        fw.dma("sp", sconv_o[:, 0, :], scv.t[:, 1, :], reads=[scv], is_output=True)
        fw.dma("sp", sconv_o[:, 1, :], us.t[:], reads=[us], is_output=True)
        fw.tt("dve", cy.t[:], scv.t[:, 0, :], cwr.t[:, 0, :], ALU.mult, [scv, cwr], [cy])
        fw.tt("dve", ezs.t[:], scv.t[:, 1, :], cwr.t[:, 1, :], ALU.mult, [scv, cwr], [ezs])
        fw.tt("dve", cy.t[:], cy.t[:], ezs.t[:], ALU.add, [cy, ezs], [cy])
        fw.tt("dve", ezs.t[:], us.t[:], cwr.t[:, 2, :], ALU.mult, [us, cwr], [ezs])
        fw.tt("dve", cy.t[:], cy.t[:], ezs.t[:], ALU.add, [cy, ezs], [cy])
        fw.tt("dve", cy.t[:], cy.t[:], cbr.t[:], ALU.add, [cy, cbr], [cy])
        fw.act(ezs.t[:], tms.t[:, f0 + 1536:f0 + 2048], AF.Exp, [tms], [ezs], scale=-1.0)
        fw.ts("dve", ezs.t[:], ezs.t[:], 1.0, None, ALU.add, None, [ezs], [ezs])
        fw.op("dve", lambda e: e.reciprocal(out=ezs.t[:], in_=ezs.t[:]), [ezs], [ezs])
        fw.tt("dve", ezs.t[:], ezs.t[:], tms.t[:, f0 + 1536:f0 + 2048], ALU.mult, [ezs, tms], [ezs])
        fw.tt("dve", ezs.t[:], ezs.t[:], tms.t[:, f0:f0 + 512], ALU.mult, [ezs, tms], [ezs])
        fw.tt("dve", mixs.t[:, 0:512], cy.t[:], ezs.t[:], ALU.mult, [cy, ezs], [mixs])
        fw.memset("pool", QTzs.t[:], 0.0, [QTzs])
        tb = nxt("T")
        for n in range(4):
            fw.tr(tb.t[:, n * 4:n * 4 + 4], qsb.t[:, n * 128:(n + 1) * 128], identb.t[0:4, 0:4], [qsb, identb], [tb])
        fw.cp("act", QTzs.t[0:64, :, 0, :], tb.t[0:64, 0:16].rearrange("p (n s) -> p s n", s=4), [tb], [QTzs])
        fw.cp("act", QTzs.t[64:128, :, 1, :], tb.t[64:128, 0:16].rearrange("p (n s) -> p s n", s=4), [tb], [QTzs])
        for bi, br in enumerate((1, 2)):
            fw.tt("dve", prod.t[:], qsf.t[:].rearrange("p (n c) -> p n c", c=128),
                  kvs.t[:, 0, br * 128:(br + 1) * 128].unsqueeze(1).to_broadcast([4, 4, 128]), ALU.mult, [qsf, kvs], [prod])
            fw.op("dve", lambda e: e.tensor_reduce(out=snew.t[:, bi, :], in_=prod.t[:].rearrange("p n (g d) -> p (n g) d", d=64),
                                                   axis=AX.X, op=ALU.add), [prod], [snew])
        fw.act(snew.t[:], snew.t[:], AF.Exp, [snew], [snew])

        for s in range(4):
            qrhs = QTzs.t[:, s, :, :].rearrange("p g n -> p (g n)")
            fw.ts("dve", pnm.t[:].rearrange("p b g n -> p b n g"), snew.t[:].rearrange("p b (n g) -> p b n g", g=2),
                  cs.t[0:4, 256 + s:257 + s], None, ALU.mult, None, [snew, cs], [pnm])
            fw.cp("dve", pnmb.t[:], pnm.t[:], [pnm], [pnmb])
            for ck in range(16):
                sg = stg[ck % 2]
                fw.dma("pool", sg.t[:].rearrange("p a b -> p (a b)"), ccmp_d[:, :], reads=[idxa], writes=[sg],
                       indirect=bass.IndirectOffsetOnAxis(ap=idxa.t[:, s, ck:ck + 1], axis=0))
                tk, tv = nxt("T"), nxt("T")
                for r8 in range(8):
                    fw.tr(tk.t[:, r8 * 128:(r8 + 1) * 128], sg.t[:, r8, 0:128], ident, [sg, identb], [tk])
                    fw.tr(tv.t[:, r8 * 128:(r8 + 1) * 128], sg.t[:, r8, 128:256], ident, [sg, identb], [tv])
                fw.cp("act", kcTs.t[:].rearrange("p a b -> p (a b)"), tk.t[:, :], [tk], [kcTs])
                fw.cp("act", vcTs.t[:].rearrange("p a b -> p (a b)"), tv.t[:, :], [tv], [vcTs])
                for r8 in range(8):
                    r = ck * 8 + r8
                    q4, r32 = r // 32, r % 32
                    fw.mm(OA.t[:, q4 * 128:(q4 + 1) * 128], wbd.t[:, r32, 0, :], kcTs.t[:, r8, :], r32 == 0, r32 == 31,
                          [wbd, kcTs], [OA])
                    fw.mm(OB.t[:, q4 * 128:(q4 + 1) * 128], vcTs.t[:, r8, :], wbd.t[:, r32, 1, :], r32 == 0, r32 == 31,
                          [wbd, vcTs], [OB])
            fw.act(KcmpTs.t[:], OA.t[:, :], AF.Identity, [OA, biasK], [KcmpTs], bias=biasK.t[:, 0:1])
            fw.tt("dve", Vcmps.t[:], OB.t[:, :].rearrange("p (q c) -> p q c", c=128),
                  biasVb.t[:].unsqueeze(1).to_broadcast([128, 4, 128]), ALU.add, [OB, biasVb], [Vcmps])
            for g in range(2):
                pb = nxt("P")
                fw.mm(pb.t[0:4, :], QTzs.t[:, s, g, :], KcmpTs.t[:, :], True, True, [QTzs, KcmpTs], [pb])
                fw.op("dve", lambda e: e.tensor_reduce(out=mxs.t[:, g:g + 1], in_=pb.t[0:4, :], axis=AX.X, op=ALU.max), [pb], [mxs])
                fw.ts("dve", mxs.t[:, g:g + 1], mxs.t[:, g:g + 1], -1.0, None, ALU.mult, None, [mxs], [mxs])
                fw.act(pns.t[:, g, :], pb.t[0:4, :], AF.Exp, [pb, mxs], [pns, sms], bias=mxs.t[:, g:g + 1],
                       accum_out=sms.t[:, g:g + 1])
                fw.op("dve", lambda e: e.reciprocal(out=sms.t[:, g:g + 1], in_=sms.t[:, g:g + 1]), [sms], [sms])
                fw.ts("dve", pns.t[:, g, :], pns.t[:, g, :], sms.t[:, g:g + 1], None, ALU.mult, None, [pns, sms], [pns])
                fw.cp("dve", pnsb.t[:, g, :], pns.t[:, g, :], [pns], [pnsb])
                pi = nxt("P")
                fw.mm(pi.t[0:1, :], cs.t[0:4, 268:269], pns.t[:, g, :], True, True, [cs, pns], [pi])
                fw.cp("act", impf.t[:], pi.t[0:1, :], [pi], [impf])
                iv = impf.t[:].rearrange("p (h two j) -> p h two j", two=2, j=128)
                fw.tt("dve", scs.t[:].rearrange("p (h j) -> p h j", j=128), iv[:, :, 0, :], iv[:, :, 1, :], ALU.add, [impf], [scs])
                fw.tt("dve", scs.t[:], scs.t[:], cs.t[0:1, 0:256], ALU.add, [scs, cs], [scs])
                fw.op("dve", lambda e: e.max(out=m8s.t[:, 0:8], in_=scs.t[:]), [scs], [m8s])
                fw.op("dve", lambda e: e.match_replace(out=wks.t[:], in_to_replace=m8s.t[:, 0:8], in_values=scs.t[:],
                                                       imm_value=-3.0e38), [scs, m8s], [wks])
                fw.op("dve", lambda e: e.max(out=m8s.t[:, 8:16], in_=wks.t[:]), [wks], [m8s])
                fw.ts("dve", nms.t[:], scs.t[:], m8s.t[:, 14:15], NEG, ALU.is_lt, ALU.mult, [scs, m8s], [nms])
                pc = nxt("P")
                for h in range(2):
                    fw.mm(pc.t[:, h:h + 1], nms.t[0:1, h * 128:(h + 1) * 128], identb.t[0:1, 0:1], True, True,
                          [nms, identb], [pc])
                fw.cp("act", nmcol.t[:, g, :], pc.t[:, 0:2], [pc], [nmcol])
                tb = nxt("T")
                for q4 in range(4):
                    fw.tr(tb.t[:, q4 * 4:q4 * 4 + 4], pnsb.t[:, g, q4 * 128:(q4 + 1) * 128], identb.t[0:4, 0:4],
                          [pnsb, identb], [tb])
                fw.cp("act", pnT.t[:, g, :, :].rearrange("p a b -> p (a b)"), tb.t[:, 0:16], [tb], [pnT])
                po = nxt("P")
                for q4 in range(4):
                    fw.mm(po.t[0:4, 0:64], pnT.t[:, g, q4, :], Vcmps.t[:, q4, g * 64:(g + 1) * 64], q4 == 0, q4 == 3,
                          [pnT, Vcmps], [po])
                fw.cp("act", Oall.t[:, 0, g, 0:64], po.t[0:4, 0:64], [po], [Oall])
            fw.mm(OA.t[0:4, 0:128], zerob.t[:, 0:4], zerob.t[:, 0:128], True, True, [zerob], [OA])
            fw.memset("pool", accs.t[:], 0.0, [accs])
            for ck in range(16):
                sg = stg[ck % 2]
                fw.dma("pool", sg.t[:].rearrange("p a b -> p (a b)"), cslc_d[:, :], reads=[idxa], writes=[sg],
                       indirect=bass.IndirectOffsetOnAxis(ap=idxa.t[:, s, ck:ck + 1], axis=0))
                tk = nxt("T")
                for r8 in range(8):
                    fw.tr(tk.t[:, r8 * 128:(r8 + 1) * 128], sg.t[:, r8, 0:128], ident, [sg, identb], [tk])
                fw.cp("act", kcTs.t[:].rearrange("p a b -> p (a b)"), tk.t[:, :], [tk], [kcTs])
                pb = nxt("P")
                for r8 in range(8):
                    fw.mm(pb.t[:, r8 * 8:r8 * 8 + 8], kcTs.t[:, r8, :], qrhs, True, True, [kcTs, QTzs], [pb])
                h = (ck * 8) // 64
                p4 = pb.t[:, 0:64].rearrange("p (r g n) -> p r g n", g=2, n=4)
                for g in range(2):
                    fw.act(PTs.t[:, :, g, :], p4[:, :, g, :], AF.Exp, [pb, nmcol], [PTs], bias=nmcol.t[:, g, h:h + 1])
                fw.op("dve", lambda e: e.tensor_reduce(out=acct.t[:], in_=PTs.t[:].rearrange("p r g n -> p (g n) r"),
                                                       axis=AX.X, op=ALU.add), [PTs], [acct])
                fw.tt("dve", accs.t[:], accs.t[:], acct.t[:], ALU.add, [accs, acct], [accs])
                for r8 in range(8):
                    for g in range(2):
                        fw.mm(OA.t[0:4, g * 64:(g + 1) * 64], PTs.t[:, r8, g, :], sg.t[:, r8, 128 + g * 64:192 + g * 64],
                              False, False, [PTs, sg], [OA], skip_group_check=True)
            ps_ = nxt("P")
            for g in range(2):
                fw.mm(OA.t[0:4, g * 64:(g + 1) * 64], pnmb.t[:, 0, g, :], kvsb.t[:, 1, 128 + g * 64:192 + g * 64],
                      False, False, [pnmb, kvsb], [OA], skip_group_check=True)
                fw.mm(ps_.t[0:4, g:g + 1], accs.t[:, g * 4:(g + 1) * 4], onesf, True, False, [accs, cs], [ps_])
                fw.mm(ps_.t[0:4, g:g + 1], pnm.t[:, 0, g, :], cs.t[0:4, 268:269], False, True, [pnm, cs], [ps_])
            fw.cp("act", Oall.t[:, 1, :, 0:64], OA.t[0:4, 0:128].rearrange("p (g d) -> p g d", d=64), [OA], [Oall])
            fw.cp("act", Oall.t[:, 1, :, 64], ps_.t[0:4, 0:2], [ps_], [Oall])
            fw.dma("sp", swt.t[:], swin_d[s].rearrange("(c p) f -> p c f", p=128), writes=[swt])
            fw.cp("dve", swb.t[:].rearrange("p a b -> p (a b)"), swt.t[:].rearrange("p a b -> p (a b)"), [swt], [swb])
            tk = nxt("T")
            for c in range(4):
                fw.tr(tk.t[:, c * 128:(c + 1) * 128], swb.t[:, c, 0:128], ident, [swb, identb], [tk])
            fw.cp("act", kcTs.t[:, 0:4, :].rearrange("p a b -> p (a b)"), tk.t[:, 0:512], [tk], [kcTs])
            pb = nxt("P")
            for c in range(4):
                fw.mm(pb.t[:, c * 8:c * 8 + 8], kcTs.t[:, c, :], qrhs, True, True, [kcTs, QTzs], [pb])
            fw.tt("dve", stw.t[:], pb.t[:, 0:32].rearrange("p (c e) -> p c e", e=8),
                  cs.t[:, 260:264].unsqueeze(2).to_broadcast([128, 4, 8]), ALU.add, [pb, cs], [stw])
            fw.act(PTs.t[:, 0:4, :, :].rearrange("p c g n -> p c (g n)"), stw.t[:], AF.Exp, [stw], [PTs])
            fw.op("dve", lambda e: e.tensor_reduce(out=acct.t[:], in_=PTs.t[:, 0:4, :, :].rearrange("p r g n -> p (g n) r"),
                                                   axis=AX.X, op=ALU.add), [PTs], [acct])
            fw.mm(OB.t[0:4, 0:128], zerob.t[:, 0:4], zerob.t[:, 0:128], True, True, [zerob], [OB])
            ps_ = nxt("P")
            for c in range(4):
                for g in range(2):
                    fw.mm(OB.t[0:4, g * 64:(g + 1) * 64], PTs.t[:, c, g, :], swb.t[:, c, 128 + g * 64:192 + g * 64],
                          False, False, [PTs, swb], [OB], skip_group_check=True)
            for g in range(2):
                fw.mm(OB.t[0:4, g * 64:(g + 1) * 64], pnmb.t[:, 1, g, :], kvsb.t[:, 1, 256 + g * 64:320 + g * 64],
                      False, False, [pnmb, kvsb], [OB], skip_group_check=True)
                fw.mm(ps_.t[0:4, g:g + 1], acct.t[:, g * 4:(g + 1) * 4], onesf, True, False, [acct, cs], [ps_])
                fw.mm(ps_.t[0:4, g:g + 1], pnm.t[:, 1, g, :], cs.t[0:4, 268:269], False, True, [pnm, cs], [ps_])
            fw.cp("act", Oall.t[:, 2, :, 0:64], OB.t[0:4, 0:128].rearrange("p (g d) -> p g d", d=64), [OB], [Oall])
            fw.cp("act", Oall.t[:, 2, :, 64], ps_.t[0:4, 0:2], [ps_], [Oall])
            for n in range(4):
                fw.dma("sp", Ocat.t[s:s + 1, n, :, :, :].rearrange("p a b c -> p (a b c)"),
                       Oall.t[n:n + 1, :, :, :].rearrange("p a b c -> p (a b c)"), reads=[Oall], writes=[Ocat])
        for g in range(2):
            gv = gs.t[:, g * 12:(g + 1) * 12].rearrange("p (n b) -> p n b", b=3)
            fw.ts("dve", fcs.t[:, 1, :], Ocat.t[:, :, 1, g, 64], 1e-30, None, ALU.max, None, [Ocat], [fcs])
            fw.ts("dve", fcs.t[:, 2, :], Ocat.t[:, :, 2, g, 64], 1e-30, None, ALU.max, None, [Ocat], [fcs])
            fw.op("dve", lambda e: e.reciprocal(out=fcs.t[:, 1:3, :], in_=fcs.t[:, 1:3, :]), [fcs], [fcs])
            fw.tt("dve", fcs.t[:, 1, :], fcs.t[:, 1, :], gv[:, :, 1], ALU.mult, [fcs, gs], [fcs])
            fw.tt("dve", fcs.t[:, 2, :], fcs.t[:, 2, :], gv[:, :, 2], ALU.mult, [fcs, gs], [fcs])
            fw.tt("dve", o1.t[:], Ocat.t[:, :, 0, g, 0:64], gv[:, :, 0].unsqueeze(2).to_broadcast([4, 4, 64]), ALU.mult,
                  [Ocat, gs], [o1])
            fw.tt("dve", o2.t[:], Ocat.t[:, :, 1, g, 0:64], fcs.t[:, 1, :].unsqueeze(2).to_broadcast([4, 4, 64]), ALU.mult,
                  [Ocat, fcs], [o2])
            fw.tt("dve", o1.t[:], o1.t[:], o2.t[:], ALU.add, [o1, o2], [o1])
            fw.tt("dve", o2.t[:], Ocat.t[:, :, 2, g, 0:64], fcs.t[:, 2, :].unsqueeze(2).to_broadcast([4, 4, 64]), ALU.mult,
                  [Ocat, fcs], [o2])
            fw.tt("dve", osf.t[:, g * 256:(g + 1) * 256].rearrange("p (n d) -> p n d", d=64), o1.t[:], o2.t[:], ALU.add,
                  [o1, o2], [osf])
        f0 = FM0 + 2048
        fw.act(ezs.t[:], tms.t[:, f0:f0 + 512], AF.Exp, [tms], [ezs], scale=-1.0)
        fw.ts("dve", ezs.t[:], ezs.t[:], 1.0, None, ALU.add, None, [ezs], [ezs])
        fw.op("dve", lambda e: e.reciprocal(out=ezs.t[:], in_=ezs.t[:]), [ezs], [ezs])
        fw.tt("dve", ezs.t[:], ezs.t[:], tms.t[:, f0:f0 + 512], ALU.mult, [ezs, tms], [ezs])
        fw.tt("dve", mixs.t[:, 512:1024], osf.t[:], ezs.t[:], ALU.mult, [osf, ezs], [mixs])
        fw.cp("dve", mixb.t[:], mixs.t[:], [mixs], [mixb])
        tb = nxt("T")
        for kc in range(8):
            fw.tr(tb.t[:, kc * 4:kc * 4 + 4], mixb.t[:, kc * 128:(kc + 1) * 128], identb.t[0:4, 0:4], [mixb, identb], [tb])
        fw.cp("act", mixTs.t[:].rearrange("p a b -> p (a b)"), tb.t[:, 0:32], [tb], [mixTs])
        for hf in range(2):
            pb = nxt("P")
            for kc in range(8):
                fw.mm(pb.t[0:4, :], mixTs.t[:, kc, :], woutb.t[:, kc, hf * 512:(hf + 1) * 512], kc == 0, kc == 7,
                      [mixTs, woutb], [pb])
            fw.tt("dve", xs.t[:, hf * 512:(hf + 1) * 512], pb.t[0:4, :], xs.t[:, hf * 512:(hf + 1) * 512], ALU.add,
                  [pb, xs], [xs])
        fw.dma("sp", ys_o[:, :], ysb.t[:], reads=[ysb], is_output=True)
        barrier([ysb, kvs, scv, us, swt, Oall, Ocat] + stg)
        ess.close()

    import os
    STAGE = int(os.environ.get("KSTAGE", "99"))
    if STAGE == 0:
        fw.finish()
        return nc, es

    def chk(n):
        if STAGE == n:
            raise StopBuild()
    if do_sample:
        sample_path()

    KTs = fw.sb("KTs", [128, NTILE, 128], BF16)
    KTs_b = [Buf("KTs%d" % i) for i in range(NTILE)]
    Vs = fw.sb("Vs", [128, NTILE, 2, 65], BF16)
    Vs_b = [Buf("Vs%d" % i) for i in range(NTILE)]
    KTw = fw.sb("KTw", [128, 8, 128], BF16)
    KTw_b = [Buf("KTw%d" % i) for i in range(8)]
    Vw = fw.sb("Vw", [128, 8, 2, 65], BF16)
    Vw_b = [Buf("Vw%d" % i) for i in range(8)]
    KcmpT = fw.sb("KcmpT", [128, 256], BF16)
    VcmpT = fw.sb("VcmpT", [128, 256], BF16)
    Vcmp_tm = fw.sb("Vcmp_tm", [128, 2, 128], BF16)
    score = [fw.sb("score%d" % g, [128, 128], F32) for g in range(2)]
    pn = fw.sb("pn", [128, 4, 256], F32)
    pnb = fw.sb("pnb", [128, 4, 256], BF16)
    xt = [fw.sb("xt%d" % i, [128, 1024], F32) for i in range(3)]
    rp = [fw.sb("rp%d" % i, [128, 64], F32) for i in range(3)]
    ss = fw.sb("ss", [128, 1], F32)
    lnv = fw.sb("lnv", [128, 1], F32)
    rstd = fw.sb("rstd", [128, 1], F32)
    hb = fw.sb("hb", [128, 1024], BF16)
    hT = [fw.sb("hT%d" % i, [128, 8, 128], BF16) for i in range(2)]
    tm = fw.sb("tm", [128, 920], F32)
    ssh = fw.sb("ssh", [128, 14], F32)
    lnh = fw.sb("lnh", [128, 14], F32)
    rsh = fw.sb("rsh", [128, 14], F32)
    tcs = fw.sb("tcs", [128, 896], F32)
    tsn = fw.sb("tsn", [128, 896], F32)
    kvout = fw.sb("kvout", [128, 2, 384], F32)
    kb = fw.sb("kb", [128, 4, 128], BF16)
    qb = fw.sb("qb", [128, 512], BF16)
    kcT = fw.sb("kcT", [128, 2, 2, 128], BF16)
    kcT_b = [Buf("kcT0"), Buf("kcT1")]
    QTz = fw.sb("QTz", [128, 2, 4, 128], BF16)
    gate = fw.sb("gate", [128, 24], F32)
    csb = fw.sb("csb", [128, 128], F32)
    uext = fw.sb("uext", [128, 4, 130], F32)
    halo = [fw.sb("halo%d" % i, [128, 4, 2], F32) for i in range(2)]
    halot = fw.sb("halot", [128, 4, 2], F32)
    hcs = fw.sb("hcs", [128, 16], F32)
    ez = fw.sb("ez", [128, 512], F32)
    g1 = fw.sb("g1", [128, 128], F32)
    acc = fw.sb("acc", [128, 128], F32)
    mixT = fw.sb("mixT", [128, 8, 128], BF16)
    szT = fw.sb("szT", [128, 4, 128], F32)
    mx = fw.sb("mx", [128, 4], F32)
    sm = fw.sb("sm", [128, 4], F32)
    a1 = fw.sb("a1", [128, 256], F32)
    a2 = fw.sb("a2", [128, 256], F32)
    m8 = fw.sb("m8", [128, 16], F32)
    wk = fw.sb("wk", [128, 128], F32)
    nmask = [fw.sb("nmask%d" % g, [128, 128], BF16) for g in range(2)]
    expb = [fw.sb("expb%d" % i, [128, 4, 128], BF16) for i in range(2)]
    PT = [fw.sb("PT%d" % i, [128, 4, 128], BF16) for i in range(2)]
    PTc = fw.sb("PTc", [128, 2, 4, 128], BF16)
    ocmp = fw.sb("ocmp", [128, 4, 64], F32)
    fac = fw.sb("fac", [128, 3, 4], F32)
    rs2 = fw.sb("rs2", [128, 2, 4], F32)
    ot1 = fw.sb("ot1", [128, 4, 64], F32)
    ot2 = fw.sb("ot2", [128, 4, 64], F32)
    oall = fw.sb("oall", [128, 8, 64], BF16)
    fw.memset("pool", Vs.t[:], 1.0, [Vs] + Vs_b)
    fw.memset("pool", Vw.t[:], 1.0, [Vw] + Vw_b)
    fw.memset("pool", KcmpT.t[:], 0.0, [KcmpT])
    fw.memset("pool", VcmpT.t[:], 0.0, [VcmpT])
    for g in range(2):
        fw.memset("pool", score[g].t[:], BIGNEG, [score[g]])
    fw.memset("pool", pn.t[:], 0.0, [pn])
    fw.memset("pool", pnb.t[:], 0.0, [pnb])
    fw.memset("pool", QTz.t[:], 0.0, [QTz])
    fw.memset("pool", halo[1].t[:], 0.0, [halo[1]])
    epsap = cst.t[:, C_AB + 2:C_AB + 3]

    def xbuf(m):
        k, own = m // 2, (m % 2 == 0)
        sq_ = 2 * k + (1 if own else 0)
        return xt[sq_ % 3], rp[sq_ % 3]

    def load_x(m):
        xb, rb = xbuf(m)
        fw.dma("sp", xb.t[:], xp[m * 128:(m + 1) * 128, :], writes=[xb])
        fw.dma("sp", rb.t[:], rope[m * 128:(m + 1) * 128, :], writes=[rb])

    def tile_front(m, own, k):
        xb, rb = xbuf(m)
        hTb = hT[m % 2]
        fw.act(hb.t[:], xb.t[:], AF.Square, [xb], [hb, ss], accum_out=ss.t[:])
        fw.act(lnv.t[:], ss.t[:], AF.Ln, [ss], [lnv], scale=1.0 / 1024, bias=epsap)
        fw.act(rstd.t[:], lnv.t[:], AF.Exp, [lnv], [rstd], scale=-0.5)
        fw.ts("dve", hb.t[:], xb.t[:], rstd.t[:, 0:1], None, ALU.mult, None, [xb, rstd], [hb])
        tb = nxt("T")
        for kc in range(8):
            fw.tr(tb.t[:, kc * 128:(kc + 1) * 128], hb.t[:, kc * 128:(kc + 1) * 128], ident, [hb, identb], [tb])
        fw.cp("act", hTb.t[:].rearrange("p a b -> p (a b)"), tb.t[:, :], [tb], [hTb])
        chk(10)
        pieces = [(0, 512), (512, 1024), (1024, 1304)] if own else [(512, 1024), (1024, 1280)]
        for (c0, c1) in pieces:
            pb = nxt("P")
            for kc in range(8):
                fw.mm(pb.t[:, 0:c1 - c0], hTb.t[:, kc, :], winb.t[:, kc, c0:c1], kc == 0, kc == 7, [hTb, winb], [pb])
            if c0 == 0:
                fw.cp("act", tm.t[:, 0:512], pb.t[:, 0:512], [pb], [tm])
            elif c0 == 512:
                fw.cp("act", tm.t[:, 512:896], pb.t[:, 0:384], [pb], [tm])
                fw.cp("act", kvout.t[:, 1, 0:128], pb.t[:, 384:512], [pb], [kvout])
            else:
                fw.cp("act", kvout.t[:, 1, 128:384], pb.t[:, 0:256], [pb], [kvout])
                if own:
                    fw.cp("act", tm.t[:, 896:920], pb.t[:, 256:280], [pb], [tm])
        r0, r1 = (0, 896) if own else (512, 896)
        chk(11)
        qk_norm("dve", tm, r0, r1, ssh, lnh, rsh, tsn, 128, epsap)
        chk(12)
        if own:
            cosb = rb.t[:, 0:32].unsqueeze(1).to_broadcast([128, 16, 32])
            sinb = rb.t[:, 32:64].unsqueeze(1).to_broadcast([128, 16, 32])
            rope_apply("dve", tm.t[:, 0:512].rearrange("p (h d) -> p h d", d=64),
                       qb.t[:].rearrange("p (h d) -> p h d", d=64), cosb, sinb, tcs, tsn, 8, [tm, rb], [qb])
        cosb = rb.t[:, 0:32].unsqueeze(1).to_broadcast([128, 12, 32])
        sinb = rb.t[:, 32:64].unsqueeze(1).to_broadcast([128, 12, 32])
        rope_apply("dve", tm.t[:, 512:896].rearrange("p (h d) -> p h d", d=64),
                   kvout.t[:, 0, :].rearrange("p (h d) -> p h d", d=64), cosb, sinb, tcs, tsn, 6, [tm, rb], [kvout])
        kvo = kvout.t[:].rearrange("p j (b c) -> p j b c", c=128)
        chk(13)
        if own:
            fw.dma("sp", cmp_o[k * 128:(k + 1) * 128, :].rearrange("t (j c) -> t j c", j=2), kvo[:, :, 0, :],
                   reads=[kvout], is_output=True)
            fw.dma("sp", slc_o[k * 128:(k + 1) * 128, :].rearrange("t (j c) -> t j c", j=2), kvo[:, :, 1, :],
                   reads=[kvout], is_output=True)
            if k >= 30:
                fw.dma("sp", win_o[(k - 30) * 128:(k - 29) * 128, :].rearrange("t (j c) -> t j c", j=2), kvo[:, :, 2, :],
                       reads=[kvout], is_output=True)
        chk(14)
        fw.cp("dve", kb.t[:, 0:2, :], kvo[:, :, 0, :], [kvout], [kb])
        fw.cp("dve", kb.t[:, 2:4, :], kvo[:, 0, 1:3, :], [kvout], [kb])
        fw.cp("pool", Vs.t[:, m, :, 0:64], kvout.t[:, 1, 128:256].rearrange("p (g d) -> p g d", d=64), [kvout], [Vs_b[m]])
        fw.cp("pool", Vw.t[:, m % 8, :, 0:64], kvout.t[:, 1, 256:384].rearrange("p (g d) -> p g d", d=64), [kvout],
              [Vw_b[m % 8]])
        chk(15)
        tb = nxt("T")
        for j in range(4):
            fw.tr(tb.t[:, j * 128:(j + 1) * 128], kb.t[:, j, :], ident, [kb, identb], [tb])
        if own:
            for n in range(4):
                fw.tr(tb.t[:, 512 + n * 128:512 + (n + 1) * 128], qb.t[:, n * 128:(n + 1) * 128], ident, [qb, identb], [tb])
        chk(16)
        ti = m % 2
        fw.cp("act", kcT.t[:, ti, :, :].rearrange("p a b -> p (a b)"), tb.t[:, 0:256], [tb], [kcT_b[ti]])
        fw.cp("act", KTs.t[:, m, :], tb.t[:, 256:384], [tb], [KTs_b[m]])
        fw.cp("act", KTw.t[:, m % 8, :], tb.t[:, 384:512], [tb], [KTw_b[m % 8]])
        if own:
            fw.cp("act", QTz.t[0:64, 0, :, :], tb.t[0:64, 512:1024].rearrange("p (n t) -> p n t", t=128), [tb], [QTz])
            fw.cp("act", QTz.t[64:128, 1, :, :], tb.t[64:128, 512:1024].rearrange("p (n t) -> p n t", t=128), [tb], [QTz])
            fw.act(gate.t[:], tm.t[:, 896:920], AF.Exp, [tm], [gate], scale=-1.0)
            fw.ts("dve", gate.t[:], gate.t[:], 1.0, None, ALU.add, None, [gate], [gate])
            fw.op("dve", lambda e: e.reciprocal(out=gate.t[:], in_=gate.t[:]), [gate], [gate])
        else:
            chk(17)
            pb = nxt("P")
            for ct in range(4):
                for j in range(2):
                    col = FM0 + (1 + j) * 512 + ct * 128
                    o0 = (ct * 2 + j) * 2
                    for kc in range(8):
                        fw.mm(pb.t[:, o0:o0 + 2], winb.t[:, kc, col:col + 128], hTb.t[:, kc, 126:128], kc == 0, kc == 7,
                              [winb, hTb], [pb], skip_group_check=True)
            hl = halo[k % 2]
            fw.cp("act", hcs.t[:], pb.t[:, 0:16], [pb], [hcs])
            hc4 = hcs.t[:].rearrange("p (c j t) -> p c j t", j=2, t=2)
            fw.tt("dve", hl.t[:], hc4[:, :, 0, :], hc4[:, :, 1, :], ALU.mult, [hcs], [hl])

    def compress(k):
        rk = kcT.t[:, :, 0, :].rearrange("p a (c r) -> p a c r", r=32)
        rv = kcT.t[:, :, 1, :].rearrange("p a (c r) -> p a c r", r=32)
        pb = nxt("P")
        for r in range(32):
            fw.mm(pb.t[:, 0:8], wbd.t[:, r, 0, :], rk[:, :, :, r], r == 0, r == 31, [wbd] + kcT_b, [pb])
        for r in range(32):
            fw.mm(pb.t[:, 8:16], wbd.t[:, r, 1, :], rv[:, :, :, r], r == 0, r == 31, [wbd] + kcT_b, [pb],
                  skip_group_check=True)
        fw.act(KcmpT.t[:, 8 * k:8 * k + 8], pb.t[:, 0:8], AF.Identity, [pb, biasK], [KcmpT], bias=biasK.t[:, 0:1])
        fw.cp("dve", VcmpT.t[:, 8 * k:8 * k + 8], pb.t[:, 8:16], [pb], [VcmpT])
        ch = (8 * k) // 128
        tb = nxt("T")
        fw.tr(tb.t[:, 0:128], VcmpT.t[:, ch * 128:(ch + 1) * 128], ident, [VcmpT, identb], [tb])
        fw.cp("act", csb.t[:], tb.t[:, 0:128], [tb], [csb])
        fw.tt("dve", Vcmp_tm.t[:, ch, :], csb.t[:], biasVb.t[:], ALU.add, [csb, biasVb], [Vcmp_tm])

    pipe = {"i": 0}

    def score_phase(g, kt_ap, kt_buf, bias_list):
        i = pipe["i"]
        pipe["i"] += 1
        sb_ = S[i % 2]
        sap = Sap(sb_)
        nb = len(bias_list)
        fw.mm(sap, kt_ap, QTz.t[:, g, :, :].rearrange("p n t -> p (n t)"), True, nb == 0, [kt_buf, QTz], [sb_])
        for bi, (l_ap, r_ap, bufs) in enumerate(bias_list):
            fw.mm(sap, l_ap, r_ap, False, bi == nb - 1, bufs, [sb_])
        pt = PT[i % 2]
        fw.act(pt.t[:].rearrange("p n t -> p (n t)"), sap, AF.Exp, [sb_], [pt])
        return pt

    def pv_phase(pt, v_ap, v_buf, Oacc):
        for n in range(4):
            fw.mm(Oacc.t[:, n * 65:n * 65 + 65], pt.t[:, n, :], v_ap, False, False, [pt, v_buf], [Oacc],
                  skip_group_check=True)

    def attention(k):
        C = 8 * (k + 1)
        nch = (C + 127) // 128
        NS = C // 2
        wp0 = C_WP + 128 - 4 * k
        for g in range(2):
            sa, sb2 = S[0], S[1]
            for n in range(4):
                bank = sa if n < 2 else sb2
                fw.mm(Sps[:, n * 256:n * 256 + C], QTz.t[:, g, n, :], KcmpT.t[:, 0:C], True, True, [QTz, KcmpT], [bank])
            s4 = Sps[:, :].rearrange("p (n c) -> p n c", c=256)[:, :, 0:C]
            fw.op("dve", lambda e: e.tensor_reduce(out=mx.t[:], in_=s4, axis=AX.X, op=ALU.max), [sa, sb2], [mx])
            fw.ts("dve", mx.t[:], mx.t[:], -1.0, None, ALU.mult, None, [mx], [mx])
            for n in range(4):
                bank = sa if n < 2 else sb2
                fw.act(pn.t[:, n, 0:C], Sps[:, n * 256:n * 256 + C], AF.Exp, [bank, mx], [pn], bias=mx.t[:, n:n + 1])
            fw.tt("dve", pn.t[:, :, C - 8:C], pn.t[:, :, C - 8:C],
                  cst.t[:, C_VIS:C_VIS + 8].unsqueeze(1).to_broadcast([128, 4, 8]), ALU.mult, [pn, cst], [pn])
            fw.op("dve", lambda e: e.tensor_reduce(out=sm.t[:], in_=pn.t[:, :, 0:C], axis=AX.X, op=ALU.add), [pn], [sm])
            fw.ts("dve", sm.t[:], sm.t[:], 1e-30, None, ALU.max, None, [sm], [sm])
            fw.op("dve", lambda e: e.reciprocal(out=sm.t[:], in_=sm.t[:]), [sm], [sm])
            fw.tt("dve", pn.t[:, :, 0:C], pn.t[:, :, 0:C], sm.t[:].unsqueeze(2).to_broadcast([128, 4, C]), ALU.mult,
                  [pn, sm], [pn])
            fw.cp("pool", pnb.t[:, :, 0:C], pn.t[:, :, 0:C], [pn], [pnb])
            fw.tt("pool", a1.t[:, 0:C], pn.t[:, 0, 0:C], pn.t[:, 1, 0:C], ALU.add, [pn], [a1])
            fw.tt("dve", a2.t[:, 0:C], pn.t[:, 2, 0:C], pn.t[:, 3, 0:C], ALU.add, [pn], [a2])
            fw.tt("dve", a1.t[:, 0:C], a1.t[:, 0:C], a2.t[:, 0:C], ALU.add, [a1, a2], [a1])
            a1v = a1.t[:, 0:C].rearrange("p (s two) -> p s two", two=2)
            sc = score[g]
            fw.tt("dve", sc.t[:, 0:NS], a1v[:, :, 0], a1v[:, :, 1], ALU.add, [a1], [sc])
            fw.tt("dve", sc.t[:, 0:NS], sc.t[:, 0:NS], cst.t[:, wp0:wp0 + NS], ALU.add, [sc, cst], [sc])
            fw.tt("dve", sc.t[:, 0:NS], sc.t[:, 0:NS], cst.t[:, C_B0:C_B0 + NS], ALU.add, [sc, cst], [sc])
            fw.op("dve", lambda e: e.max(out=m8.t[:, 0:8], in_=sc.t[:]), [sc], [m8])
            fw.op("dve", lambda e: e.match_replace(out=wk.t[:], in_to_replace=m8.t[:, 0:8], in_values=sc.t[:],
                                                   imm_value=-3.0e38), [sc, m8], [wk])
            fw.op("dve", lambda e: e.max(out=m8.t[:, 8:16], in_=wk.t[:]), [wk], [m8])
            nm = nmask[g]
            fw.ts("dve", nm.t[:], sc.t[:], m8.t[:, 15:16], NEG, ALU.is_lt, ALU.mult, [sc, m8], [nm])
            tb = nxt("T")
            for ch in range(nch):
                for n in range(4):
                    fw.tr(tb.t[:, (ch * 4 + n) * 128:(ch * 4 + n + 1) * 128], pnb.t[:, n, ch * 128:(ch + 1) * 128], ident,
                          [pnb, identb], [tb])
            fw.cp("act", PTc.t[:, 0:nch, :, :].rearrange("p c n t -> p (c n t)"), tb.t[:, 0:nch * 512], [tb], [PTc])
            pb = nxt("P")
            for n in range(4):
                for ch in range(nch):
                    fw.mm(pb.t[:, n * 64:(n + 1) * 64], PTc.t[:, ch, n, :], Vcmp_tm.t[:, ch, g * 64:(g + 1) * 64],
                          ch == 0, ch == nch - 1, [PTc, Vcmp_tm], [pb], skip_group_check=True)
            fw.cp("act", ocmp.t[:].rearrange("p n d -> p (n d)"), pb.t[:, 0:256], [pb], [ocmp])
            fw.mm(OA.t[:, :], zerob.t[:, 0:128], zerob.t[:, :], True, True, [zerob], [OA])
            fw.mm(OB.t[:, :], zerob.t[:, 0:128], zerob.t[:, :], True, True, [zerob], [OB])
            nchunk = 2 * k + 2
            work = []
            for mc in range(nchunk):
                eb = expb[(mc // 4) % 2]
                pre = None
                if mc % 4 == 0 or mc == 2 * k + 1:
                    if mc == 2 * k + 1:
                        lo, nb_ = mc, 1
                    else:
                        lo, nb_ = mc, min(4, 2 * k - mc)
                    if nb_ > 0:
                        pre = (eb, lo, nb_)
                if mc < 2 * k:
                    bl = [(eb.t[:, mc % 4, :], ident4, [eb, identb])]
                elif mc == 2 * k:
                    bl = [(ident, cbf.t[:, 0, :].unsqueeze(1).to_broadcast([128, 4, 128]), [identb, cbf])]
                else:
                    bl = [(eb.t[:, mc % 4, :], ident4, [eb, identb]),
                          (ident, cbf.t[:, 2, :].unsqueeze(1).to_broadcast([128, 4, 128]), [identb, cbf])]
                work.append((pre, KTs.t[:, mc, :], KTs_b[mc], bl, Vs.t[:, mc, g, :], Vs_b[mc], OA))
            for mc, bidx in ((2 * k - 4, 1), (2 * k - 3, 3), (2 * k - 2, None), (2 * k - 1, None), (2 * k, 0), (2 * k + 1, 2)):
                if mc < 0:
                    continue
                bl = []
                if bidx is not None:
                    bl = [(ident, cbf.t[:, bidx, :].unsqueeze(1).to_broadcast([128, 4, 128]), [identb, cbf])]
                work.append((None, KTw.t[:, mc % 8, :], KTw_b[mc % 8], bl, Vw.t[:, mc % 8, g, :], Vw_b[mc % 8], OB))

            def do_score(w):
                pre = w[0]
                if pre is not None:
                    eb_, lo, nb_ = pre
                    fw.cp("pool", eb_.t[:, lo % 4:lo % 4 + nb_, :].rearrange("p c (h r) -> p (c h) r", r=64),
                          nm.t[:, 2 * lo:2 * lo + 2 * nb_].unsqueeze(2).to_broadcast([128, 2 * nb_, 64]), [nm], [eb_])
                return score_phase(g, w[1], w[2], w[3])
            prev = None
            for w in work:
                pt_ = do_score(w)
                if prev is not None:
                    pv_phase(prev[0], prev[1][4], prev[1][5], prev[1][6])
                prev = (pt_, w)
            pv_phase(prev[0], prev[1][4], prev[1][5], prev[1][6])
            oa4 = OA.t[:, 0:260].rearrange("p (n d) -> p n d", d=65)
            ob4 = OB.t[:, 0:260].rearrange("p (n d) -> p n d", d=65)
            gv = gate.t[:, g * 12:(g + 1) * 12].rearrange("p (n b) -> p n b", b=3)
            fw.ts("dve", rs2.t[:, 0, :], oa4[:, :, 64], 1e-30, None, ALU.max, None, [OA], [rs2])
            fw.ts("dve", rs2.t[:, 1, :], ob4[:, :, 64], 1e-30, None, ALU.max, None, [OB], [rs2])
            fw.op("dve", lambda e: e.reciprocal(out=rs2.t[:], in_=rs2.t[:]), [rs2], [rs2])
            fw.tt("dve", fac.t[:, 1, :], rs2.t[:, 0, :], gv[:, :, 1], ALU.mult, [rs2, gate], [fac])
            fw.tt("dve", fac.t[:, 2, :], rs2.t[:, 1, :], gv[:, :, 2], ALU.mult, [rs2, gate], [fac])
            fw.tt("dve", ot1.t[:], ocmp.t[:], gv[:, :, 0].unsqueeze(2).to_broadcast([128, 4, 64]), ALU.mult, [ocmp, gate], [ot1])
            fw.tt("dve", ot2.t[:], oa4[:, :, 0:64], fac.t[:, 1, :].unsqueeze(2).to_broadcast([128, 4, 64]), ALU.mult,
                  [OA, fac], [ot2])
            fw.tt("dve", ot1.t[:], ot1.t[:], ot2.t[:], ALU.add, [ot1, ot2], [ot1])
            fw.tt("dve", ot2.t[:], ob4[:, :, 0:64], fac.t[:, 2, :].unsqueeze(2).to_broadcast([128, 4, 64]), ALU.mult,
                  [OB, fac], [ot2])
            fw.tt("dve", oall.t[:, g * 4:(g + 1) * 4, :], ot1.t[:], ot2.t[:], ALU.add, [ot1, ot2], [oall])

    def tile_back(m, k):
        hTb = hT[m % 2]
        ue, mt, sz = uext, mixT, szT
        fw.ts("dve", halot.t[:], halo[(k + 1) % 2].t[:], cst.t[:, C_AB:C_AB + 1], None, ALU.mult, None,
              [halo[(k + 1) % 2], cst], [halot])
        fw.stt(ue.t[:, :, 0:2], halo[k % 2].t[:], cst.t[:, C_AB + 1:C_AB + 2], halot.t[:], ALU.mult, ALU.add,
               [halo[k % 2], cst, halot], [ue])
        for ct in range(4):
            pb = nxt("P")
            for j in range(4):
                col = FM0 + j * 512 + ct * 128
                for kc in range(8):
                    fw.mm(pb.t[:, j * 128:(j + 1) * 128], winb.t[:, kc, col:col + 128], hTb.t[:, kc, :], kc == 0, kc == 7,
                          [winb, hTb], [pb], skip_group_check=True)
            fw.cp("act", csb.t[:], pb.t[:, 128:256], [pb], [csb])
            fw.tt("dve", ue.t[:, ct, 2:130], pb.t[:, 256:384], csb.t[:], ALU.mult, [pb, csb], [ue])
            fw.act(ez.t[:, 0:128], pb.t[:, 384:512], AF.Exp, [pb], [ez], scale=-1.0)
            fw.act(ez.t[:, 0:128], ez.t[:, 0:128], AF.Ln, [ez, cst], [ez], bias=cst.t[:, C_AB + 3:C_AB + 4])
            fw.act(ez.t[:, 0:128], ez.t[:, 0:128], AF.Exp, [ez], [ez], scale=-1.0)
            fw.tt("dve", g1.t[:], pb.t[:, 0:128], ez.t[:, 0:128], ALU.mult, [pb, ez], [g1])
            fw.tt("dve", g1.t[:], pb.t[:, 384:512], g1.t[:], ALU.mult, [pb, g1], [g1])
            fw.ts("dve", acc.t[:], ue.t[:, ct, 0:128], cvp.t[:, ct * 3:ct * 3 + 1], None, ALU.mult, None, [ue, cvp], [acc])
            fw.stt(acc.t[:], ue.t[:, ct, 1:129], cvp.t[:, ct * 3 + 1:ct * 3 + 2], acc.t[:], ALU.mult, ALU.add,
                   [ue, cvp, acc], [acc])
            fw.stt(acc.t[:], ue.t[:, ct, 2:130], cvp.t[:, ct * 3 + 2:ct * 3 + 3], acc.t[:], ALU.mult, ALU.add,
                   [ue, cvp, acc], [acc])
            fw.stt(mt.t[:, ct, :], acc.t[:], cvp.t[:, 12 + ct:13 + ct], g1.t[:], ALU.add, ALU.mult, [acc, cvp, g1], [mt])
        if k == NSLOT - 1:
            for ct in range(4):
                fw.dma("sp", conv_o[:, ct * 128:(ct + 1) * 128].rearrange("t p -> p t"), ue.t[:, ct, 128:130], reads=[ue],
                       is_output=True, allow_slow_non_contiguous=True)
        pb = nxt("P")
        for ct in range(4):
            col = FM0 + 2048 + ct * 128
            for kc in range(8):
                fw.mm(pb.t[:, ct * 128:(ct + 1) * 128], winb.t[:, kc, col:col + 128], hTb.t[:, kc, :], kc == 0, kc == 7,
                      [winb, hTb], [pb], skip_group_check=True)
        fw.act(ez.t[:], pb.t[:, :], AF.Exp, [pb], [ez], scale=-1.0)
        fw.act(ez.t[:], ez.t[:], AF.Ln, [ez, cst], [ez], bias=cst.t[:, C_AB + 3:C_AB + 4])
        fw.act(ez.t[:], ez.t[:], AF.Exp, [ez], [ez], scale=-1.0)
        fw.tt("dve", sz.t[:].rearrange("p a b -> p (a b)"), pb.t[:, :], ez.t[:], ALU.mult, [pb, ez], [sz])

    def tile_out(m, k):
        xb, _ = xbuf(m)
        mt, sz = mixT, szT
        tb = nxt("T")
        for ct in range(4):
            fw.tr(tb.t[:, ct * 128:(ct + 1) * 128], oall.t[:, 2 * ct:2 * ct + 2, :].rearrange("p a b -> p (a b)"), ident,
                  [oall, identb], [tb])
        fw.cp("act", ez.t[:], tb.t[:, 0:512], [tb], [ez])
        fw.tt("dve", mt.t[:, 4:8, :].rearrange("p a b -> p (a b)"), ez.t[:], sz.t[:].rearrange("p a b -> p (a b)"),
              ALU.mult, [ez, sz], [mt])
        for hf in range(2):
            pb = nxt("P")
            for kc in range(8):
                fw.mm(pb.t[:, :], mt.t[:, kc, :], woutb.t[:, kc, hf * 512:(hf + 1) * 512], kc == 0, kc == 7, [mt, woutb], [pb])
            fw.tt("dve", xb.t[:, hf * 512:(hf + 1) * 512], pb.t[:, :], xb.t[:, hf * 512:(hf + 1) * 512], ALU.add, [pb, xb], [xb])
        fw.dma("sp", y_o[k * 128:(k + 1) * 128, :], xb.t[:], reads=[xb], is_output=True)

    load_x(1)
    load_x(0)
    try:
      for k in range(nslot):
        mo, mw = 2 * k + 1, 2 * k
        if k + 1 < nslot:
            load_x(2 * k + 3)
        tile_front(mo, False, k)
        chk(1)
        if k + 1 < nslot:
            load_x(2 * k + 2)
        tile_front(mw, True, k)
        chk(2)
        compress(k)
        chk(3)
        tile_back(mw, k)
        chk(4)
        attention(k)
        chk(5)
        tile_out(mw, k)
    except StopBuild:
        pass

    fw.finish()
    print("program: %d tracked instructions, %d dma sems, counts %s" % (fw.ninst, fw.nd, fw.cnt))
    return nc, es


def _tile_of_slot(mm, p):
    k, o = mm // 2, mm % 2
    return 2 * k + (p if o == 0 else 1 - p)


def _consts(p):
    c = np.zeros((128, C_END), np.float32)
    c2 = np.zeros((128, 640), np.float32)
    kk = np.arange(128)[:, None]
    qq = np.arange(128)[None, :]
    c2[:, 0:128] = np.where(kk <= qq, 0.0, NEG)
    c2[:, 128:256] = np.where(kk > qq, 0.0, NEG)
    c2[:, 256:384] = NEG if p == 0 else 0.0
    c2[:, 384:512] = 0.0 if p == 0 else NEG
    c2[:, 512:640] = np.eye(128, dtype=np.float32)
    ql = np.arange(128)
    W = np.zeros((128, 384), np.float32)
    for r in range(-128, 256):
        col = 128 + r
        if r <= -2:
            v = np.zeros(128)
        elif r == -1:
            v = np.where(ql < 64, 1e4, 0.0) if p == 0 else np.zeros(128)
        elif r == 0:
            v = np.full(128, 1e4)
        elif r == 1:
            v = np.where(ql >= 64, 1e4, BIGNEG)
        elif r == 2:
            v = np.full(128, BIGNEG) if p == 0 else np.zeros(128)
        elif r == 3:
            v = np.full(128, BIGNEG) if p == 0 else np.where(ql < 64, 1e4, 0.0)
        else:
            v = np.full(128, BIGNEG)
        W[:, col] = v
    c[:, C_WP:C_WP + 384] = W
    c[:, C_B0 + (0 if p == 0 else 2)] = 1e4
    vis = np.zeros((128, 8), np.float32)
    for cl in range(4):
        vis[:, cl] = (32 * cl + 31 <= ql)
        vis[:, 4 + cl] = 0.0 if p == 0 else 1.0
    c[:, C_VIS:C_VIS + 8] = vis
    c[:, C_AB] = 1.0 if p == 0 else 0.0
    c[:, C_AB + 1] = 0.0 if p == 0 else 1.0
    c[:, C_AB + 2] = EPS
    c[:, C_AB + 3] = 1.0
    return c, c2


def _consts_s():
    c = np.zeros((128, 272), np.float32)
    c[:, 0] = 1e4
    c[:, 255] = 1e4
    for s in range(4):
        c[s, 256 + s] = 1.0
    c[0, 260] = NEG
    c[:, 268] = 1.0
    return c


def _perm_w_in():
    q0, kv0, gl0, za0 = 2048, 2560, 3328, 3352
    cols = []
    for n in range(4):
        for g in range(2):
            cols += list(range(q0 + (g * 4 + n) * 64, q0 + (g * 4 + n + 1) * 64))
    for br in range(3):
        for g in range(2):
            cols += list(range(kv0 + br * 256 + g * 64, kv0 + br * 256 + g * 64 + 64))
    for br in range(3):
        for g in range(2):
            cols += list(range(kv0 + br * 256 + 128 + g * 64, kv0 + br * 256 + 128 + g * 64 + 64))
    cols += list(range(gl0, gl0 + 24))
    cols += list(range(0, 2048))
    cols += list(range(za0, za0 + 512))
    return np.array(cols)


_CACHE = {}


def kernel(x_prompt, x_sample, cache_cmp_kv, cache_slc_kv, state_win_kv, state_conv, page_table,
           norm_g, w_in, conv_w, conv_b, q_gain, k_gain, cmp_pe, cmp_w, w_out, _nslot=NSLOT, _do_sample=True):
    f = np.float32
    x_prompt = np.asarray(x_prompt, f)
    if "nc" not in _CACHE or _CACHE.get("key") != (_nslot, _do_sample):
        nc, es = build_program(_nslot, _do_sample)
        _CACHE["nc"], _CACHE["es"], _CACHE["key"] = nc, es, (_nslot, _do_sample)
    nc = _CACHE["nc"]
    wperm = np.ascontiguousarray(np.asarray(w_in, f)[0][:, _perm_w_in()])
    wo = np.ascontiguousarray(np.asarray(w_out, f)[0])
    ng = np.ascontiguousarray(np.asarray(norm_g, f)[0].reshape(8, 128).T)
    qg = np.asarray(q_gain, f)[0]
    kg = np.asarray(k_gain, f)[0]
    gains = np.concatenate([np.tile(qg, 8)] + [np.tile(kg[br], 2) for br in range(3)])[None, :].astype(f)
    cw = np.asarray(conv_w, f)[0]
    cb = np.asarray(conv_b, f)[0]
    convp = np.zeros((128, 16), f)
    for ct in range(4):
        for kk in range(3):
            convp[:, ct * 3 + kk] = cw[kk, ct * 128:(ct + 1) * 128]
        convp[:, 12 + ct] = cb[ct * 128:(ct + 1) * 128]
    cmpw = np.ascontiguousarray(np.asarray(cmp_w, f)[0].transpose(2, 0, 1, 3).reshape(64, 32 * 2 * 64))
    cmppe = np.ascontiguousarray(np.asarray(cmp_pe, f)[0].transpose(2, 0, 1).reshape(64, 64))
    inv = (10000.0 ** (-np.arange(32, dtype=np.float32) / 32)).astype(f)
    ccmp = np.asarray(cache_cmp_kv, f).reshape(NPOOL * 16, 2048)
    cslc = np.asarray(cache_slc_kv, f).reshape(NPOOL * 16, 2048)
    xs_all = np.asarray(x_sample, f).reshape(32, 1024)
    swin_all = np.asarray(state_win_kv, f).reshape(32, 512, 256)
    sconv_all = np.asarray(state_conv, f).reshape(32, 2, 512)
    pt_all = np.asarray(page_table).astype(np.int32)
    angs = (np.float32(PAST) * inv)[None, :]
    ropes = np.repeat(np.concatenate([np.cos(angs), np.sin(angs)], axis=1).astype(f), 4, axis=0)
    iota16 = np.tile(np.arange(16, dtype=np.float32)[None, :], (128, 1))
    cs_s = _consts_s()
    in_maps = []
    for c in range(8):
        b, p = c // 2, c % 2
        order = np.array([_tile_of_slot(mm, p) for mm in range(NTILE)])
        xpb = np.ascontiguousarray(x_prompt[b].reshape(NTILE, 128, 1024)[order].reshape(NTILE * 128, 1024))
        pos = (order[:, None] * 128 + np.arange(128)[None, :]).reshape(-1).astype(f)
        ang = pos[:, None] * inv[None, :]
        ropet = np.concatenate([np.cos(ang), np.sin(ang)], axis=1).astype(f)
        c1, c2 = _consts(p)
        sl = slice(4 * c, 4 * c + 4)
        in_maps.append(dict(xp=xpb, rope=ropet, w_in_d=wperm, w_out_d=wo, norm_g_d=ng, gains_in=gains, convp=convp,
                            cmpw=cmpw, cmppe=cmppe, consts=c1, consts2=c2,
                            xs=np.ascontiguousarray(xs_all[sl]), ropes=ropes,
                            ptab=np.ascontiguousarray(pt_all[sl].T), iota16=iota16, ccmp=ccmp, cslc=cslc,
                            swin=np.ascontiguousarray(swin_all[sl]), sconv=np.ascontiguousarray(sconv_all[sl]),
                            convw_row=np.ascontiguousarray(cw.reshape(1, 1536)), convb_row=np.ascontiguousarray(cb.reshape(1, 512)),
                            consts_s=cs_s))
    if not _do_sample:
        for mp in in_maps:
            del mp["ccmp"], mp["cslc"]
    res = run_bass_kernel_spmd(nc, in_maps, core_ids=list(range(8)))
    R = res.results
    B, T = 4, 8192
    y = np.zeros((B, T, 1024), f)
    pc = np.zeros((1, B, T, 2, 2, 64), f)
    psl = np.zeros((1, B, T, 2, 2, 64), f)
    pw = np.zeros((1, B, 512, 2, 2, 64), f)
    pcv = np.zeros((1, B, 2, 512), f)
    ys = np.zeros((32, 1, 1024), f)
    sc_ = np.zeros((1, 32, 1, 2, 2, 64), f)
    ss_ = np.zeros((1, 32, 1, 2, 2, 64), f)
    sw_ = np.zeros((1, 32, 512, 2, 2, 64), f)
    scv = np.zeros((1, 32, 2, 512), f)
    for c in range(8):
        b, p = c // 2, c % 2
        r = R[c]
        for k in range(NSLOT):
            i = 2 * k + p
            y[b, i * 128:(i + 1) * 128] = r["y_o"][k * 128:(k + 1) * 128]
            pc[0, b, i * 128:(i + 1) * 128] = r["cmp_o"][k * 128:(k + 1) * 128].reshape(128, 2, 2, 64)
            psl[0, b, i * 128:(i + 1) * 128] = r["slc_o"][k * 128:(k + 1) * 128].reshape(128, 2, 2, 64)
            if k >= 30:
                pw[0, b, (i - 60) * 128:(i - 59) * 128] = r["win_o"][(k - 30) * 128:(k - 29) * 128].reshape(128, 2, 2, 64)
        if p == 1:
            pcv[0, b] = r["conv_o"]
        sl = slice(4 * c, 4 * c + 4)
        ys[sl, 0] = r["ys_o"]
        sc_[0, sl, 0] = r["scmp_o"].reshape(4, 2, 2, 64)
        ss_[0, sl, 0] = r["sslc_o"].reshape(4, 2, 2, 64)
        sw_[0, sl] = r["swin_o"].reshape(4, 512, 2, 2, 64)
        scv[0, sl] = r["sconv_o"]
    return y, ys, pc, psl, pw, pcv, sc_, ss_, sw_, scv
```
